# Optimizing a Trainium2 kernel written in Bass

```python
import jax, jax.numpy as jnp
from jax import lax
import numpy as np

D_MODEL = 1024
BATCH = 4
SEQ = 8192
DEPTH = 2

N_A_LAYERS = max(1, DEPTH // 2)
N_B_LAYERS = DEPTH - N_A_LAYERS
HEAD_DIM = 64
N_HEADS_A = D_MODEL // HEAD_DIM
N_HEADS_B = D_MODEL // HEAD_DIM
DECAY_LORA = 64
AAA_LORA = 64
GATE_LORA = 160
N_SHIFT_MIX = 6
FFN_HIDDEN = -(-8 * D_MODEL // (3 * 256)) * 256
MOBA_BLOCK = 256
MOBA_TOPK = 3
Q_CHUNK = 16
NORM_EPS = 1e-6
GN_EPS = 64e-5
L2_EPS = 1e-12

kernel_name = "yoco_rwkv7_moba_hybrid"


def rms_norm(x, g):
    xf = x.astype(jnp.float32)
    y = xf * lax.rsqrt(jnp.mean(xf * xf, axis=-1, keepdims=True) + NORM_EPS)
    return (y * g.astype(jnp.float32)).astype(x.dtype)


def ada_params(c, w, b, n):
    m = jax.nn.silu(c) @ w + b
    return jnp.split(m[:, None, :], n, axis=-1)


def swiglu(h, w_gate, w_up, w_down):
    return (jax.nn.silu(h @ w_gate) * (h @ w_up)) @ w_down


def _rwkv7_step(S, inp):
    r, w, k, v, a, b = inp
    sa = jnp.einsum('bhij,bhj->bhi', S, a)
    S = S * w[:, :, None, :] + sa[..., None] * b[:, :, None, :] + v[..., None] * k[:, :, None, :]
    y = jnp.einsum('bhij,bhj->bhi', S, r)
    return S, y


def rwkv7_time_mix(h, mu, w_rkv, w0, w1, w2, a0, a1, a2, g1, g2, k_k, k_a, r_k, lnx_g, lnx_b, w_o):
    B, T, D = h.shape
    H, N = N_HEADS_A, HEAD_DIM
    heads = lambda z: z.reshape(B, T, H, N)
    h_prev = jnp.pad(h, ((0, 0), (1, 0), (0, 0)))[:, :-1]
    xs = h[None] + (h_prev - h)[None] * mu[:, None, None, :]
    r, k, v = jnp.einsum('sbtd,sde->sbte', xs[:3], w_rkv)
    xw, xa, xg = xs[3], xs[4], xs[5]
    w = -jax.nn.softplus(-(w0 + jnp.tanh(xw @ w1) @ w2)) - 0.5
    a = jax.nn.sigmoid(a0 + (xa @ a1) @ a2)
    g = jax.nn.sigmoid(xg @ g1) @ g2
    kk = heads(k * k_k).astype(jnp.float32)
    kk = kk / jnp.maximum(jnp.sqrt(jnp.sum(kk * kk, axis=-1, keepdims=True)), L2_EPS)
    k = k * (1 + (a - 1) * k_a)
    decay = jnp.exp(-jnp.exp(w.astype(jnp.float32)))
    tm = lambda z: heads(z).astype(jnp.float32).transpose(1, 0, 2, 3)
    kk_t = kk.transpose(1, 0, 2, 3)
    seqs = (tm(r), tm(decay), tm(k), tm(v), -kk_t, kk_t * tm(a))
    S0 = jnp.zeros((B, H, N, N), jnp.float32)
    _, y = lax.scan(_rwkv7_step, S0, seqs)
    y = y.transpose(1, 0, 2, 3)
    mean = jnp.mean(y, axis=-1, keepdims=True)
    var = jnp.mean(jnp.square(y - mean), axis=-1, keepdims=True)
    yn = ((y - mean) * lax.rsqrt(var + GN_EPS)).reshape(B, T, D)
    yn = (yn * lnx_g.astype(jnp.float32) + lnx_b.astype(jnp.float32)).astype(h.dtype)
    bonus = jnp.sum(heads(r) * heads(k) * r_k, axis=-1, keepdims=True) * heads(v)
    return ((yn + bonus.reshape(B, T, D)) * g) @ w_o


def shared_kv(x, c, kv_norm_g, kv_w_ada, kv_b_ada, kv_w_k, kv_w_v, k_norm_g):
    B, T, D = x.shape
    H, HD = N_HEADS_B, HEAD_DIM
    shift, scale = ada_params(c, kv_w_ada, kv_b_ada, 2)
    h = rms_norm(x, kv_norm_g) * (1 + scale) + shift
    k = rms_norm((h @ kv_w_k).reshape(B, T, H, HD), k_norm_g)
    v = (h @ kv_w_v).reshape(B, T, H, HD)
    nb = -(-T // MOBA_BLOCK)
    pad = nb * MOBA_BLOCK - T
    to_blocks = lambda z: jnp.pad(z, ((0, 0), (0, pad), (0, 0), (0, 0))).reshape(
        B, nb, MOBA_BLOCK, H, HD).transpose(0, 3, 1, 2, 4)
    kb, vb = to_blocks(k), to_blocks(v)
    kmean = jnp.mean(kb.astype(jnp.float32), axis=3).astype(kb.dtype)
    return kb, vb, kmean


def moba_attention(h, w_q, q_norm_g, kb, vb, kmean, w_o):
    B, T, D = h.shape
    H, HD = N_HEADS_B, HEAD_DIM
    nb = kb.shape[2]
    topk = min(MOBA_TOPK, nb)
    nc = T // Q_CHUNK
    sm_scale = HEAD_DIM ** -0.5
    q = rms_norm((h @ w_q).reshape(B, T, H, HD), q_norm_g)
    q_chunks = q.reshape(B, nc, Q_CHUNK, H, HD).transpose(1, 0, 3, 2, 4)
    bi = jnp.arange(B)[:, None, None, None]
    hi = jnp.arange(H)[None, :, None, None]
    slot = jnp.arange(topk)
    blocks = jnp.arange(nb)
    key_off = jnp.arange(MOBA_BLOCK)
    q_off = jnp.arange(Q_CHUNK)

    def chunk(args):
        qc, ci = args
        t0 = ci * Q_CHUNK
        blk = t0 // MOBA_BLOCK
        gate = jnp.einsum('bhqd,bhnd->bhqn', qc, kmean).astype(jnp.float32)
        gate = jnp.where(blocks < blk, gate, -jnp.inf)
        _, idx = lax.top_k(gate, topk)
        k_sel = kb[bi, hi, idx]
        v_sel = vb[bi, hi, idx]
        s_sel = jnp.einsum('bhqd,bhqskd->bhqsk', qc, k_sel).astype(jnp.float32) * sm_scale
        s_sel = jnp.where((slot < blk)[:, None], s_sel, -jnp.inf)
        k_own = lax.dynamic_index_in_dim(kb, blk, axis=2, keepdims=False)
        v_own = lax.dynamic_index_in_dim(vb, blk, axis=2, keepdims=False)
        s_own = jnp.einsum('bhqd,bhkd->bhqk', qc, k_own).astype(jnp.float32) * sm_scale
        causal = (blk * MOBA_BLOCK + key_off)[None, :] <= (t0 + q_off)[:, None]
        s_own = jnp.where(causal, s_own, -jnp.inf)
        s = jnp.concatenate([s_sel.reshape(B, H, Q_CHUNK, topk * MOBA_BLOCK), s_own], axis=-1)
        p = jax.nn.softmax(s, axis=-1).astype(vb.dtype)
        p_sel = p[..., :topk * MOBA_BLOCK].reshape(B, H, Q_CHUNK, topk, MOBA_BLOCK)
        p_own = p[..., topk * MOBA_BLOCK:]
        return (jnp.einsum('bhqsk,bhqskd->bhqd', p_sel, v_sel)
                + jnp.einsum('bhqk,bhkd->bhqd', p_own, v_own))

    o = lax.map(chunk, (q_chunks, jnp.arange(nc)))
    o = o.transpose(1, 0, 3, 2, 4).reshape(B, T, D)
    return o @ w_o


def setup_inputs(seed: int = 0) -> dict:
    key = jax.random.key(seed)
    ks = iter(jax.random.split(key, 40))
    f32 = jnp.float32
    D, F, NA, NBL, HD = D_MODEL, FFN_HIDDEN, N_A_LAYERS, N_B_LAYERS, HEAD_DIM
    nrm = lambda shape, s: jax.random.normal(next(ks), shape, f32) * s
    uni = lambda shape, lo, hi: jax.random.uniform(next(ks), shape, f32, lo, hi)
    return {
        "x": nrm((BATCH, SEQ, D), 1.0),
        "c": nrm((BATCH, D), 1.0),
        "norm_g": 1.0 + nrm((DEPTH, 2, D), 0.1),
        "w_ada": nrm((DEPTH, 2, D, 3 * D), 0.5 * D ** -0.5),
        "b_ada": nrm((DEPTH, 2, 3 * D), 0.02),
        "rw_mu": uni((NA, N_SHIFT_MIX, D), 0.0, 1.0),
        "rw_w_rkv": nrm((NA, 3, D, D), D ** -0.5),
        "rw_w0": uni((NA, D), -6.0, 0.0),
        "rw_w1": nrm((NA, D, DECAY_LORA), D ** -0.5),
        "rw_w2": nrm((NA, DECAY_LORA, D), 0.5 * DECAY_LORA ** -0.5),
        "rw_a0": nrm((NA, D), 0.5),
        "rw_a1": nrm((NA, D, AAA_LORA), D ** -0.5),
        "rw_a2": nrm((NA, AAA_LORA, D), AAA_LORA ** -0.5),
        "rw_g1": nrm((NA, D, GATE_LORA), D ** -0.5),
        "rw_g2": nrm((NA, GATE_LORA, D), GATE_LORA ** -0.5),
        "rw_k_k": 0.85 + nrm((NA, D), 0.1),
        "rw_k_a": 1.0 + nrm((NA, D), 0.1),
        "rw_r_k": nrm((NA, N_HEADS_A, HD), 0.1),
        "rw_lnx_g": 1.0 + nrm((NA, D), 0.1),
        "rw_lnx_b": nrm((NA, D), 0.02),
        "rw_w_o": nrm((NA, D, D), D ** -0.5),
        "ffn_w_gate": nrm((DEPTH, D, F), D ** -0.5),
        "ffn_w_up": nrm((DEPTH, D, F), D ** -0.5),
        "ffn_w_down": nrm((DEPTH, F, D), F ** -0.5),
        "kv_norm_g": 1.0 + nrm((D,), 0.1),
        "kv_w_ada": nrm((D, 2 * D), 0.5 * D ** -0.5),
        "kv_b_ada": nrm((2 * D,), 0.02),
        "kv_w_k": nrm((D, D), D ** -0.5),
        "kv_w_v": nrm((D, D), D ** -0.5),
        "k_norm_g": 1.0 + nrm((HD,), 0.1),
        "mb_w_q": nrm((NBL, D, D), D ** -0.5),
        "mb_q_norm_g": 1.0 + nrm((NBL, HD), 0.1),
        "mb_w_o": nrm((NBL, D, D), D ** -0.5),
    }


def reference(x, c, norm_g, w_ada, b_ada, rw_mu, rw_w_rkv, rw_w0, rw_w1, rw_w2, rw_a0, rw_a1, rw_a2,
              rw_g1, rw_g2, rw_k_k, rw_k_a, rw_r_k, rw_lnx_g, rw_lnx_b, rw_w_o,
              ffn_w_gate, ffn_w_up, ffn_w_down, kv_norm_g, kv_w_ada, kv_b_ada, kv_w_k, kv_w_v,
              k_norm_g, mb_w_q, mb_q_norm_g, mb_w_o):
    kb = vb = kmean = None
    for i in range(DEPTH):
        shift, scale, gate = ada_params(c, w_ada[i, 0], b_ada[i, 0], 3)
        h = rms_norm(x, norm_g[i, 0]) * (1 + scale) + shift
        if i < N_A_LAYERS:
            j = i
            mix = rwkv7_time_mix(h, rw_mu[j], rw_w_rkv[j], rw_w0[j], rw_w1[j], rw_w2[j],
                                 rw_a0[j], rw_a1[j], rw_a2[j], rw_g1[j], rw_g2[j],
                                 rw_k_k[j], rw_k_a[j], rw_r_k[j], rw_lnx_g[j], rw_lnx_b[j], rw_w_o[j])
        else:
            j = i - N_A_LAYERS
            mix = moba_attention(h, mb_w_q[j], mb_q_norm_g[j], kb, vb, kmean, mb_w_o[j])
        x = x + gate * mix
        shift, scale, gate = ada_params(c, w_ada[i, 1], b_ada[i, 1], 3)
        h = rms_norm(x, norm_g[i, 1]) * (1 + scale) + shift
        x = x + gate * swiglu(h, ffn_w_gate[i], ffn_w_up[i], ffn_w_down[i])
        if i == N_A_LAYERS - 1:
            kb, vb, kmean = shared_kv(x, c, kv_norm_g, kv_w_ada, kv_b_ada, kv_w_k, kv_w_v, k_norm_g)
    return x
```

```python
import contextlib
import numpy as np
import concourse.bass as bass
import concourse.mybir as mybir
from concourse.bass_utils import run_bass_kernel_spmd

F32 = mybir.dt.float32
BF16 = mybir.dt.bfloat16
AF = mybir.ActivationFunctionType
ALU = mybir.AluOpType
AX = mybir.AxisListType

D = 1024
NH = 16
HD = 64
FF = 2816
NFC = FF // 128
CDEC = float(np.exp(-0.5))
NEG = -1.0e30
NO_SELF_SYNC = ()


class Buf:
    __slots__ = ("t", "name")

    def __init__(self, t, name):
        self.t = t
        self.name = name

    def __getitem__(self, k):
        return self.t[k]


class View:
    def __init__(self, key, fn):
        self.key = key
        self.fn = fn

    def __getitem__(self, k):
        return self.fn()[k]


class TokSet:
    __slots__ = ("d",)

    def __init__(self):
        self.d = {}

    def add(self, tok):
        sem, val = tok
        k = id(sem)
        if k not in self.d or self.d[k][1] < val:
            self.d[k] = (sem, val)

    def items_(self):
        return list(self.d.values())


class Sched:
    def __init__(self, nc, es, ndma=10):
        self.nc = nc
        self.es = es
        self.eng = {"pe": nc.tensor, "act": nc.scalar, "dve": nc.vector, "pool": nc.gpsimd, "sp": nc.sync}
        self.esem = {k: es.enter_context(nc.semaphore("e_" + k)) for k in self.eng}
        self.ecnt = {k: 0 for k in self.eng}
        self.seen = {k: {} for k in self.eng}
        self.lastw = {}
        self.readers = {}
        self.dpool = {}
        self.dnext = {}
        self.dcnt = {}
        for q in ("sp", "pool", "act"):
            self.dpool[q] = [es.enter_context(nc.semaphore("d_%s%d" % (q, i))) for i in range(ndma)]
            self.dnext[q] = 0
        self.nbuf = 0
        self.parts = {}
        self.ninst = {k: 0 for k in self.eng}
        self.nwait = 0

    def sb(self, shape, dt=F32, name=None):
        self.nbuf += 1
        name = (name or "sb") + "_%d" % self.nbuf
        t = self.es.enter_context(self.nc.sbuf_tensor(name, list(shape), dt))
        return Buf(t, name)

    def ps(self, shape, dt=F32, name=None):
        self.nbuf += 1
        name = (name or "ps") + "_%d" % self.nbuf
        t = self.es.enter_context(self.nc.psum_tensor(name, list(shape), dt))
        return Buf(t, name)

    def _wait(self, engine, deps):
        e = self.eng[engine]
        seen = self.seen[engine]
        best = {}
        for (sem, val) in deps:
            k = id(sem)
            if k not in best or best[k][1] < val:
                best[k] = (sem, val)
        for k, (sem, val) in best.items():
            if engine == "pe" and sem is self.esem["pe"]:
                continue
            if engine in NO_SELF_SYNC and sem is self.esem.get(engine):
                continue
            if seen.get(k, 0) < val:
                e.wait_ge(sem, val)
                self.nwait += 1
                seen[k] = val

    def _rel(self, k):
        if isinstance(k, tuple):
            self.parts.setdefault(k[0], set()).add(k)
            return (k, k[0])
        ps = self.parts.get(k)
        return (k,) + tuple(ps) if ps else (k,)

    def _collect(self, reads, writes):
        deps = []
        reads = [getattr(b, "key", b) for b in reads]
        writes = [getattr(b, "key", b) for b in writes]
        for b0 in reads:
            for b in self._rel(b0):
                w = self.lastw.get(b)
                if w:
                    deps.extend(w.items_())
        for b0 in writes:
            for b in self._rel(b0):
                w = self.lastw.get(b)
                if w:
                    deps.extend(w.items_())
                r = self.readers.get(b)
                if r:
                    deps.extend(r.items_())
        return deps

    def _commit(self, tok, reads, writes):
        reads = [getattr(b, "key", b) for b in reads]
        writes = [getattr(b, "key", b) for b in writes]
        for b in reads:
            r = self.readers.get(b)
            if r is None:
                r = self.readers[b] = TokSet()
            r.add(tok)
        for b in writes:
            w = TokSet()
            w.add(tok)
            self.lastw[b] = w
            self.readers[b] = TokSet()

    def op(self, engine, fn, reads=(), writes=()):
        self._wait(engine, self._collect(reads, writes))
        inst = fn(self.eng[engine])
        self.ecnt[engine] += 1
        self.ninst[engine] += 1
        inst.then_inc(self.esem[engine], 1)
        tok = (self.esem[engine], self.ecnt[engine])
        self._commit(tok, reads, writes)
        return tok

    def dma(self, queue, out, in_, reads=(), writes=(), **kw):
        pool = self.dpool[queue]
        i = self.dnext[queue]
        self.dnext[queue] = (i + 1) % len(pool)
        sem = pool[i]
        prev = self.dcnt.get(id(sem), 0)
        deps = self._collect(reads, writes)
        if prev:
            deps.append((sem, prev))
        self._wait(queue, deps)
        inst = self.eng[queue].dma_start(out=out, in_=in_, **kw)
        inst.then_inc(sem, 16)
        self.ninst[queue] += 1
        self.dcnt[id(sem)] = prev + 16
        tok = (sem, prev + 16)
        self._commit(tok, reads, writes)
        return tok

    def all_tokens(self):
        deps = []
        for q, pool in self.dpool.items():
            for sem in pool:
                v = self.dcnt.get(id(sem), 0)
                if v:
                    deps.append((sem, v))
        for k in self.eng:
            if self.ecnt[k]:
                deps.append((self.esem[k], self.ecnt[k]))
        return deps

    def barrier(self):
        deps = self.all_tokens()
        for k in self.eng:
            self._wait(k, deps)

    def finish(self):
        self._wait("sp", self.all_tokens())


class StopBuild(Exception):
    pass


def build(T, stages="ABCDEF", taps=(), stop=None):
    NT = T // 128
    NB = T // 256
    TH = T // 2
    nc = bass.Bass("TRN2", target_bir_lowering=False)
    I = {}

    def inp(name, shape, dt=F32):
        I[name] = nc.dram_tensor(name, list(shape), dt, kind="ExternalInput").ap()
        return I[name]

    inp("xv", [T, D]); inp("valid", [128, NT]); inp("bbias", [128, NB]); inp("cvec", [128, 8])
    inp("w_ada", [4, D, 3 * D]); inp("b_ada", [4, 3 * D]); inp("norm_g", [4, D])
    inp("mu", [128, 6, 8]); inp("w_rkv", [3, D, D])
    for n in ("w0", "a0", "k_k", "k_a", "r_k"):
        inp(n, [128, 8])
    inp("w1", [D, 64]); inp("w2", [64, D]); inp("a1", [D, 64]); inp("a2", [64, D])
    inp("g1", [D, 160]); inp("g2", [160, D]); inp("lnx_g", [D]); inp("lnx_b", [D]); inp("rw_wo", [D, D])
    inp("ffn_g", [2, D, FF]); inp("ffn_u", [2, D, FF]); inp("ffn_d", [2, FF, D])
    inp("kv_norm_g", [D]); inp("kv_w_ada", [D, 2 * D]); inp("kv_b_ada", [2 * D])
    inp("kv_wk", [D, D]); inp("kv_wv", [D, D]); inp("k_norm_g", [D]); inp("q_norm_g", [D])
    inp("mb_wq", [D, D]); inp("mb_wo", [D, D])
    inp("c_ident", [128, 128]); inp("c_rmask", [128, 1024]); inp("c_msu", [128, 64]); inp("c_msl", [128, 64])
    inp("c_miu", [128, 64]); inp("c_id64", [128, 64]); inp("c_bones", [128, 128]); inp("c_hsel", [128, 2])
    inp("c_causal", [128, 2, 256]); inp("c_fut", [128, NB, NB])

    out = nc.dram_tensor("out", [TH, D], F32, kind="ExternalOutput").ap()
    XS = nc.dram_tensor("xs_scr", [T, D], F32, kind="Internal").ap()
    KT = nc.dram_tensor("kt_scr", [8, 128, T], BF16, kind="Internal").ap()
    VS = nc.dram_tensor("v_scr", [T, D], BF16, kind="Internal").ap()
    QT = nc.dram_tensor("qt_scr", [8, 128, TH], BF16, kind="Internal").ap()
    SEL = nc.dram_tensor("sel_scr", [TH, NH * NB], F32, kind="Internal").ap()
    OS = nc.dram_tensor("o_scr", [TH, D], BF16, kind="Internal").ap()
    TAP = {}
    for (name, shape) in taps:
        TAP[name] = nc.dram_tensor("tap_" + name, list(shape), F32, kind="ExternalOutput").ap()

    xs_key = "XS"
    with contextlib.ExitStack() as es0:
        S = Sched(nc, es0)

        def TTo(eng, out, a, b, op, reads, writes):
            return S.op(eng, lambda e: e.tensor_tensor(out=out, in0=a, in1=b, op=op), reads, writes)

        def STT(out, a, sc, b, op0, op1, reads, writes):
            return S.op("dve", lambda e: e.scalar_tensor_tensor(out=out, in0=a, scalar=sc, in1=b, op0=op0, op1=op1), reads, writes)

        def TS(eng, out, a, s1, s2, op0, op1, reads, writes):
            if op1 is None:
                return S.op(eng, lambda e: e.tensor_scalar(out=out, in0=a, scalar1=s1, scalar2=None, op0=op0), reads, writes)
            return S.op(eng, lambda e: e.tensor_scalar(out=out, in0=a, scalar1=s1, scalar2=s2, op0=op0, op1=op1), reads, writes)

        def ACT(out, in_, func, reads, writes, bias=None, scale=None, accum=None):
            kw = {}
            if bias is not None:
                kw["bias"] = bias
            if scale is not None:
                kw["scale"] = scale
            if accum is not None:
                kw["accum_out"] = accum
            return S.op("act", lambda e: e.activation(out=out, in_=in_, func=func, **kw), reads, writes)

        def CP(eng, out, in_, reads, writes):
            if eng == "act":
                return S.op("act", lambda e: e.copy(out=out, in_=in_), reads, writes)
            return S.op(eng, lambda e: e.tensor_copy(out=out, in_=in_), reads, writes)

        def MM(out, lhsT, rhs, start, stop, reads, writes):
            return S.op("pe", lambda e: e.matmul(out, lhsT, rhs, start=start, stop=stop), reads, writes)

        def TR(out, in_, ident, reads, writes):
            return S.op("pe", lambda e: e.transpose(out, in_, ident), reads, writes)

        def kc(v, ch):
            return (getattr(v, "key", v), ch)

        def ck(n):
            if stop is not None and n == stop:
                raise StopBuild()

        def tap(name, src_ap, reads):
            if name in TAP:
                S.dma("sp", TAP[name], src_ap, reads=reads)

        identf = S.sb([128, 128], F32, "identf"); identb = S.sb([128, 128], BF16, "identb")
        S.dma("sp", identf[:], I["c_ident"], writes=[identf])
        S.dma("pool", identb[:], I["c_ident"], writes=[identb])
        PSB = [S.ps([128, 512], F32, "bank%d" % i) for i in range(7)]
        psTf = S.ps([128, 512], F32, "psTf")
        psT = View(psTf, lambda: psTf[:].bitcast(BF16))
        cs = S.sb([128, 8], F32, "cs")
        S.dma("sp", cs[:], I["cvec"], writes=[cs])
        ACT(cs[:], cs[:], AF.Silu, [cs], [cs])

        def ada_rows(es, w_ap, b_ap, n, g_ap):
            rows = [S.sb([128, D], F32, "row%d" % j) for j in range(n)]
            with contextlib.ExitStack() as es1:
                S.es = es1
                brow = S.sb([128, n * D], F32, "brow")
                csrep = S.sb([128, 8, 128], F32, "csrep")
                S.op("dve", lambda e: e.memset(csrep[:], 1.0), (), [csrep])
                TTo("dve", csrep[:], csrep[:], cs[:].unsqueeze(2).to_broadcast([128, 8, 128]), ALU.mult, [cs, csrep], [csrep])
                grow = S.sb([128, D], F32, "grow")
                wsl = [S.sb([128, 8, 512], F32, "wsl%d" % i) for i in range(2)]
                S.dma("sp", brow[:], b_ap.partition_broadcast(128), writes=[brow])
                S.dma("sp", grow[:], g_ap.partition_broadcast(128), writes=[grow])
                wv = w_ap.rearrange("(dc p) n -> p dc n", p=128)
                for ns in range(2 * n):
                    w = wsl[ns % 2]
                    S.dma("sp" if ns % 2 == 0 else "act", w[:], wv[:, :, ns * 512:(ns + 1) * 512], writes=[w])
                    pb = PSB[ns % 2]
                    for dc in range(8):
                        MM(pb[:], csrep[:, dc, :], w[:, dc, :], dc == 0, dc == 7, [csrep, w], [pb])
                    j, hf = ns // 2, ns % 2
                    TTo("dve", rows[j][:, hf * 512:(hf + 1) * 512], pb[:], brow[:, ns * 512:(ns + 1) * 512], ALU.add,
                        [pb, brow], [rows[j]])
                STT(rows[1][:], rows[1][:], 1.0, grow[:], ALU.add, ALU.mult, [rows[1], grow], [rows[1]])
                S.barrier()
            S.es = es
            return rows

        def norm_tile_g(xt, gs, sh, hb, tmpk, ss, vcol=None):
            class _T:
                def __getitem__(s_, k):
                    a = tmpk[:]
                    if len(a.shape) == 3:
                        a = a.rearrange("p a b -> p (a b)")
                    return a[k]
            tmp = _T()
            ACT(tmp[:], xt[:], AF.Square, [xt], [tmpk, ss], accum=ss[:, 0:1])
            yield
            ACT(ss[:, 1:2], ss[:, 0:1], AF.Sqrt, [ss], [ss], bias=1e-6, scale=1.0 / D)
            yield
            S.op("dve", lambda e: e.reciprocal(out=ss[:, 2:3], in_=ss[:, 1:2]), [ss], [ss])
            yield
            if vcol is not None:
                TTo("dve", ss[:, 2:3], ss[:, 2:3], vcol, ALU.mult, [ss], [ss])
                yield
            STT(tmp[:], xt[:], ss[:, 2:3], gs[:], ALU.mult, ALU.mult, [xt, ss, gs], [tmpk])
            yield
            if vcol is not None:
                STT(hb[:], sh[:], vcol, tmp[:], ALU.mult, ALU.add, [sh, tmpk], [hb])
                yield
            else:
                TTo("dve", hb[:], tmp[:], sh[:], ALU.add, [tmpk, sh], [hb])
                yield

        def norm_tile(xt, gs, sh, hb, tmpk, ss, vcol=None):
            for _ in norm_tile_g(xt, gs, sh, hb, tmpk, ss, vcol):
                pass

        def transpose8(src, dst_view, reads_extra=(), dstbuf=None, eng="act"):
            for dc in range(8):
                TR(psT[:, dc * 128:(dc + 1) * 128], src[:, dc * 128:(dc + 1) * 128], identb[:], [src, identb], [psT])
            CP(eng, dst_view, psT[:].rearrange("p (a b) -> p a b", b=128), [psT], [dstbuf])

        def load_w_bf16(buf_view, w_ap, bufkey, q="pool"):
            S.dma(q, buf_view, w_ap, writes=[bufkey])

        valid_sb = S.sb([128, NT], F32, "valid")
        S.dma("sp", valid_sb[:], I["valid"], writes=[valid_sb])

        if "B" in stages:
            with contextlib.ExitStack() as es:
                S.es = es
                sh0, gs0, gt0 = ada_rows(es, I["w_ada"][0], I["b_ada"][0], 3, I["norm_g"][0])
                wrkv = [S.sb([128, 8, D], BF16, "wrkv%d" % i) for i in range(3)]
                for i in range(3):
                    S.dma("pool", wrkv[i][:], I["w_rkv"][i].rearrange("(dc p) n -> p dc n", p=128), writes=[wrkv[i]])
                w1 = S.sb([128, 8, 64], BF16, "w1"); a1 = S.sb([128, 8, 64], BF16, "a1"); g1 = S.sb([128, 8, 160], BF16, "g1")
                S.dma("pool", w1[:], I["w1"].rearrange("(dc p) n -> p dc n", p=128), writes=[w1])
                S.dma("pool", a1[:], I["a1"].rearrange("(dc p) n -> p dc n", p=128), writes=[a1])
                S.dma("pool", g1[:], I["g1"].rearrange("(dc p) n -> p dc n", p=128), writes=[g1])
                w2 = S.sb([64, D], BF16, "w2"); a2 = S.sb([64, D], BF16, "a2")
                g2a = S.sb([128, D], BF16, "g2a"); g2b = S.sb([128, D], BF16, "g2b")
                S.op("pool", lambda e: e.memset(g2b[:], 0.0), (), [g2b])
                S.dma("pool", w2[:], I["w2"], writes=[w2]); S.dma("pool", a2[:], I["a2"], writes=[a2])
                S.dma("pool", g2a[:], I["g2"][0:128, :], writes=[g2a]); S.dma("pool", g2b[0:32, :], I["g2"][128:160, :], writes=[g2b])
                wo = S.sb([128, 8, D], BF16, "wo")
                S.dma("pool", wo[:], I["rw_wo"].rearrange("(dc p) n -> p dc n", p=128), writes=[wo])
                mu = S.sb([128, 6, 8], F32, "mu"); S.dma("sp", mu[:], I["mu"], writes=[mu])
                pp = {}
                for n in ("w0", "a0", "k_k", "k_a", "r_k"):
                    pp[n] = S.sb([128, 8], F32, n); S.dma("sp", pp[n][:], I[n], writes=[pp[n]])
                lnxg = S.sb([128, D], F32, "lnxg"); lnxb = S.sb([128, D], F32, "lnxb")
                S.dma("sp", lnxg[:], I["lnx_g"].partition_broadcast(128), writes=[lnxg])
                S.dma("sp", lnxb[:], I["lnx_b"].partition_broadcast(128), writes=[lnxb])
                cst = {}
                for n, shp in (("c_msu", [128, 64]), ("c_msl", [128, 64]), ("c_miu", [128, 64]),
                               ("c_id64", [128, 64]), ("c_bones", [128, 128])):
                    cst[n] = S.sb(shp, F32, n); S.dma("sp", cst[n][:], I[n], writes=[cst[n]])
                cst["c_rmask"] = S.sb([128, 1024], BF16, "c_rmask"); S.dma("pool", cst["c_rmask"][:], I["c_rmask"], writes=[cst["c_rmask"]])
                hsel = S.sb([128, 2], BF16, "hsel"); S.dma("pool", hsel[:], I["c_hsel"], writes=[hsel])
                H32 = S.sb([128, 8, 64], F32, "H32"); Hb = S.sb([128, 8, 64], BF16, "Hb")
                S.op("dve", lambda e: e.memset(H32[:], 0.0), (), [H32])
                S.op("dve", lambda e: e.memset(Hb[:], 0.0), (), [Hb])
                xt = [S.sb([128, D], F32, "xt0")] * 2
                ss = S.sb([128, 4], F32, "ss")
                hb = S.sb([128, D], BF16, "hb")
                hT = [S.sb([128, 8, 129], BF16, "hT%d" % i) for i in range(2)]
                S.op("dve", lambda e: e.memset(hT[1][:], 0.0), (), [hT[1]])
                dx = S.sb([128, 8, 128], BF16, "dx")
                xs = [S.sb([128, 8, 128], BF16, "xs%d" % i) for i in range(6)]
                f = {n: S.sb([128, 8, 128], F32, "f_" + n) for n in
                     ("sigw", "lp", "P", "alr", "k", "r", "kk", "km", "bA", "tA", "tB")}
                for n in ("iP", "PC"):
                    f[n] = S.sb([128, 8, 128], BF16, "f_" + n)
                class _V:
                    def __init__(s_, b, pat, **kw):
                        s_.b, s_.pat, s_.kw = b, pat, kw
                    def __getitem__(s_, k):
                        return s_.b[:].rearrange(s_.pat, **s_.kw)[k]
                tmp_b = f["tA"]
                th = S.sb([64, 128], BF16, "th"); al = S.sb([64, 128], BF16, "al"); sg = S.sb([128, 2, 128], BF16, "sg")
                S.op("pool", lambda e: e.memset(sg[:], 0.0), (), [sg])
                v_sb = S.sb([128, D], BF16, "v_sb"); g_sb = S.sb([128, D], BF16, "g_sb")
                ARt = S.sb([128, 8, 2, 2, 64], BF16, "ARt"); BKt = S.sb([128, 8, 2, 2, 64], BF16, "BKt")
                BKh = S.sb([128, 8, 2, 128], BF16, "BKh")
                prod = S.sb([128, 8, 128], BF16, "prod"); rk_sb = S.sb([128, 16], F32, "rk_sb")
                def hview(b):
                    return View(b, lambda: b[:].rearrange("p a b -> p (a b)")[:, 0:512].rearrange("p (a b) -> p a b", b=64))
                A_sb = [hview(f["alr"]), hview(f["k"])]
                N_sb = [hview(f["kk"]), hview(f["km"])]
                T_sb = [hview(f["bA"]), hview(f["tB"])]
                Aak = S.sb([128, 8, 64], BF16, "Aak"); Qrb = S.sb([128, 8, 64], BF16, "Qrb"); Qrk = S.sb([128, 8, 64], BF16, "Qrk")
                BKhT = S.sb([128, 8, 2, 64], BF16, "BKhT")
                G_sb = hview(f["r"]); X0 = S.sb([128, 8, 64], F32, "X0"); U_sb = S.sb([128, 8, 64], BF16, "U_sb")
                st = S.sb([128, 4, 16], F32, "st")
                c0, c1, c2 = PSB[4], PSB[5], PSB[6]
                fm = [PSB[0], PSB[1]]
                tm = [PSB[2], PSB[3]]

                def bc8(t, n=128):
                    return t[:].unsqueeze(2).to_broadcast([128, 8, n])

                def m8(mk):
                    return cst[mk][:].unsqueeze(1).to_broadcast([128, 8, 64])

                def fl(b):
                    return b[:].rearrange("p a b -> p (a b)")

                def F1g(tt):
                    xcur = xt[tt % 2]
                    hTc, hTp = hT[tt % 2], hT[(tt + 1) % 2]
                    S.dma("sp", xcur[:], I["xv"][tt * 128:(tt + 1) * 128, :], writes=[xcur])
                    yield
                    for _ in norm_tile_g(xcur, gs0, sh0, hb, f["tA"], ss, vcol=valid_sb[:, tt:tt + 1]):
                        yield
                    CP("pool", hTc[:, :, 0:1], hTp[:, :, 128:129], [hTp], [hTc])
                    yield
                    transpose8(hb, hTc[:, :, 1:129], dstbuf=hTc)
                    yield
                    TTo("dve", dx[:], hTc[:, :, 0:128], hTc[:, :, 1:129], ALU.subtract, [hTc], [dx])
                    yield
                    for i in range(6):
                        eng = "dve" if i < 3 else "pool"
                        TTo(eng, xs[i][:], dx[:], mu[:, i, :].unsqueeze(2).to_broadcast([128, 8, 128]), ALU.mult, [dx, mu], [xs[i]])
                        yield
                        TTo(eng, xs[i][:], xs[i][:], hTc[:, :, 1:129], ALU.add, [xs[i], hTc], [xs[i]])
                        yield

                def F1(tt):
                    for _ in F1g(tt):
                        pass
                pend = [None]

                def drip(n=1):
                    g_ = pend[0]
                    if g_ is None:
                        return
                    for _ in range(n):
                        try:
                            next(g_)
                        except StopIteration:
                            pend[0] = None
                            return

                def F2a():
                    for qi, dst in ((0, f["r"]), (1, f["k"])):
                        for grp in range(2):
                            pb = fm[grp]
                            for pl in range(4):
                                pr = grp * 4 + pl
                                for dc in range(8):
                                    MM(pb[:, pl * 128:(pl + 1) * 128], wrkv[qi][:, dc, pr * 128:(pr + 1) * 128], xs[qi][:, dc, :],
                                       dc == 0, dc == 7, [wrkv[qi], xs[qi]], [pb])
                            CP("act", fl(dst)[:, grp * 512:(grp + 1) * 512], pb[:], [pb], [(dst, grp)])

                for tt in range(NT):
                  try:
                    if tt == 0:
                        F1(0)
                    ck(2)
                    if tt == 0:
                        F2a()
                    ck(3)
                    for hf in range(2):
                        for dc in range(8):
                            MM(tm[hf][:], xs[2][:, dc, :], wrkv[2][:, dc, hf * 512:(hf + 1) * 512], dc == 0, dc == 7, [xs[2], wrkv[2]], [tm[hf]])
                        CP("act", v_sb[:, hf * 512:(hf + 1) * 512], tm[hf][:], [tm[hf]], [v_sb])
                    ck(4)
                    for dc in range(8):
                        MM(c0[0:64, 0:128], w1[:, dc, :], xs[3][:, dc, :], dc == 0, dc == 7, [w1, xs[3]], [c0])
                    ACT(th[:], c0[0:64, 0:128], AF.Tanh, [c0], [th])
                    for dc in range(8):
                        MM(c1[0:64, 0:128], a1[:, dc, :], xs[4][:, dc, :], dc == 0, dc == 7, [a1, xs[4]], [c1])
                    CP("act", al[:], c1[0:64, 0:128], [c1], [al])
                    for (src, wgt, bias, dst) in ((th, w2, pp["w0"], f["sigw"]), (al, a2, pp["a0"], f["alr"])):
                        for grp in range(2):
                            pb = fm[grp]
                            for pl in range(4):
                                pr = grp * 4 + pl
                                MM(pb[:, pl * 128:(pl + 1) * 128], wgt[:, pr * 128:(pr + 1) * 128], src[:], True, True, [wgt, src], [pb])
                            for pl in range(4):
                                pr = grp * 4 + pl
                                ACT(dst[:, pr, :], pb[:, pl * 128:(pl + 1) * 128], AF.Sigmoid, [pb, bias], [(dst, pr)], bias=bias[:, pr:pr + 1])
                    ck(5)
                    for dc in range(8):
                        MM(c0[:, 0:128], g1[:, dc, 0:128], xs[5][:, dc, :], dc == 0, dc == 7, [g1, xs[5]], [c0])
                    for dc in range(8):
                        MM(c1[0:32, 0:128], g1[:, dc, 128:160], xs[5][:, dc, :], dc == 0, dc == 7, [g1, xs[5]], [c1])
                    ACT(sg[:, 0, :], c0[:, 0:128], AF.Sigmoid, [c0], [sg])
                    ACT(sg[0:32, 1, :], c1[0:32, 0:128], AF.Sigmoid, [c1], [sg])
                    for hf in range(2):
                        MM(tm[hf][:], sg[:, 0, :], g2a[:, hf * 512:(hf + 1) * 512], True, False, [sg, g2a], [tm[hf]])
                        MM(tm[hf][:], sg[:, 1, :], g2b[:, hf * 512:(hf + 1) * 512], False, True, [sg, g2b], [tm[hf]])
                        CP("act", g_sb[:, hf * 512:(hf + 1) * 512], tm[hf][:], [tm[hf]], [g_sb])
                    ck(6)
                    S.op("dve", lambda e: e.tensor_tensor_scan(out=fl(f["lp"]), data0=cst["c_rmask"][:], data1=fl(f["sigw"]),
                                                               initial=0.0, op0=ALU.mult, op1=ALU.add),
                         [cst["c_rmask"], f["sigw"]], [f["lp"]])
                    ACT(fl(f["P"]), fl(f["lp"]), AF.Exp, [f["lp"]], [f["P"]], scale=-CDEC)
                    ACT(fl(f["iP"]), fl(f["lp"]), AF.Exp, [f["lp"]], [f["iP"]], scale=CDEC)
                    lpv = fl(f["lp"]).rearrange("p (a t) -> p a t", t=64)
                    TTo("dve", fl(f["tB"]).rearrange("p (a t) -> p a t", t=64), lpv[:, :, 63:64].to_broadcast([128, 16, 64]), lpv,
                        ALU.subtract, [f["lp"]], [f["tB"]])
                    ACT(fl(f["PC"]), fl(f["tB"]), AF.Exp, [f["tB"]], [f["PC"]], scale=-CDEC)
                    ck(7)
                    TTo("dve", f["kk"][:], f["k"][:], bc8(pp["k_k"]), ALU.mult, [f["k"], pp["k_k"]], [f["kk"]])
                    ACT(fl(f["tA"]), fl(f["kk"]), AF.Square, [f["kk"]], [f["tA"]])
                    for grp in range(2):
                        MM(fm[grp][:], cst["c_bones"][:], fl(f["tA"])[:, grp * 512:(grp + 1) * 512], True, True, [cst["c_bones"], f["tA"]], [fm[grp]])
                        TS("dve", fl(f["tB"])[:, grp * 512:(grp + 1) * 512], fm[grp][:], 1e-24, None, ALU.max, None, [fm[grp]], [f["tB"]])
                    ACT(fl(f["tB"]), fl(f["tB"]), AF.Ln, [f["tB"]], [f["tB"]])
                    ACT(fl(f["tB"]), fl(f["tB"]), AF.Exp, [f["tB"]], [f["tB"]], scale=-0.5)
                    TTo("dve", f["kk"][:], f["kk"][:], f["tB"][:], ALU.mult, [f["kk"], f["tB"]], [f["kk"]])
                    ck(8)
                    STT(f["tA"][:], f["alr"][:], -1.0, bc8(pp["k_a"]), ALU.add, ALU.mult, [f["alr"], pp["k_a"]], [f["tA"]])
                    STT(f["km"][:], f["tA"][:], 1.0, f["k"][:], ALU.add, ALU.mult, [f["tA"], f["k"]], [f["km"]])
                    TTo("dve", f["bA"][:], f["kk"][:], f["alr"][:], ALU.mult, [f["kk"], f["alr"]], [f["bA"]])

                    def v4(b):
                        return b[:].rearrange("p a (c t) -> p a c t", t=64)
                    TTo("dve", ARt[:, :, :, 1, :], v4(f["r"]), v4(f["P"]), ALU.mult, [f["r"], f["P"]], [ARt])
                    A16 = ARt[:, :, :, 0, :].rearrange("p a c t -> p (a c) t")
                    kk16 = fl(f["kk"]).rearrange("p (a t) -> p a t", t=64); P16 = fl(f["P"]).rearrange("p (a t) -> p a t", t=64)
                    STT(A16[:, :, 1:64], kk16[:, :, 1:64], -1.0, P16[:, :, 0:63], ALU.mult, ALU.mult, [f["kk"], f["P"]], [ARt])
                    TS("dve", A16[:, :, 0:1], kk16[:, :, 0:1], -1.0, None, ALU.mult, None, [f["kk"]], [ARt])
                    TTo("pool", BKt[:, :, :, 1, :], v4(f["km"]), v4(f["iP"]), ALU.mult, [f["km"], f["iP"]], [BKt])
                    TTo("dve", BKt[:, :, :, 0, :], v4(f["bA"]), v4(f["iP"]), ALU.mult, [f["bA"], f["iP"]], [BKt])
                    TTo("dve", BKh[:, :, 1, :], f["km"][:], f["PC"][:], ALU.mult, [f["km"], f["PC"]], [BKh])
                    TTo("pool", BKh[:, :, 0, :], f["bA"][:], f["PC"][:], ALU.mult, [f["bA"], f["PC"]], [BKh])
                    ck(9)
                    TTo("pool", f["tA"][:], f["r"][:], f["km"][:], ALU.mult, [f["r"], f["km"]], [f["tA"]])
                    TTo("pool", prod[:], f["tA"][:], bc8(pp["r_k"]), ALU.mult, [f["tA"], pp["r_k"]], [prod])
                    for pr in range(8):
                        MM(c2[:, pr * 2:pr * 2 + 2], prod[:, pr, :], hsel[:], True, True, [prod, hsel], [c2])
                    CP("act", rk_sb[:], c2[:, 0:16], [c2], [rk_sb])
                    ck(10)
                    def bv(bk):
                        return bk[:].rearrange("p (a b) -> p a b", b=64)
                    B = PSB
                    Pv = f["P"][:].rearrange("p a (c t) -> p a c t", t=64)
                    for g in range(2):
                        hp = 64 * g
                        if g == 0 and tt + 1 < NT:
                            pend[0] = F1g(tt + 1)
                        for hl in range(8):
                            for ch in range(2):
                                tp = 64 * ch
                                bt = BKt[hp:hp + 64, hl, ch, 0, :]; kt = BKt[hp:hp + 64, hl, ch, 1, :]
                                at = ARt[hp:hp + 64, hl, ch, 0, :]
                                MM(bv(B[4])[tp:tp + 64, hl, :], bt, at, True, True, [BKt, ARt], [B[4]])
                                MM(bv(B[5])[tp:tp + 64, hl, :], at, bt, True, True, [BKt, ARt], [B[5]])
                                MM(bv(B[6])[tp:tp + 64, hl, :], kt, at, True, True, [BKt, ARt], [B[6]])
                        TTo("dve", N_sb[0][:], bv(B[4]), m8("c_msu"), ALU.mult, [B[4], cst["c_msu"]], [N_sb[0]])
                        TTo("dve", A_sb[0][:], bv(B[5]), m8("c_msl"), ALU.mult, [B[5], cst["c_msl"]], [A_sb[0]])
                        TTo("dve", Aak[:], bv(B[6]), m8("c_msu"), ALU.mult, [B[6], cst["c_msu"]], [Aak])
                        ck(11)
                        drip(1)
                        for hl in range(8):
                            for ch in range(2):
                                tp = 64 * ch
                                bt = BKt[hp:hp + 64, hl, ch, 0, :]; kt = BKt[hp:hp + 64, hl, ch, 1, :]
                                rt = ARt[hp:hp + 64, hl, ch, 1, :]
                                MM(bv(B[4])[tp:tp + 64, hl, :], bt, rt, True, True, [BKt, ARt], [B[4]])
                                MM(bv(B[5])[tp:tp + 64, hl, :], kt, rt, True, True, [BKt, ARt], [B[5]])
                        TTo("dve", Qrb[:], bv(B[4]), m8("c_miu"), ALU.mult, [B[4], cst["c_miu"]], [Qrb])
                        TTo("dve", Qrk[:], bv(B[5]), m8("c_miu"), ALU.mult, [B[5], cst["c_miu"]], [Qrk])
                        ck(12)
                        drip(1)
                        psTv = psT[:].rearrange("p (a w j) -> p a w j", w=2, j=64)
                        for hl in range(8):
                            for w in range(2):
                                TR(psTv[:, hl, w, :], BKh[hp:hp + 64, hl, w, :], identb[hp:hp + 64, hp:hp + 64], [BKh, identb], [psT])
                        CP("act", BKhT[:], psTv, [psT], [BKhT])
                        for ch in range(2):
                            tp = 64 * ch
                            for hl in range(8):
                                h = 2 * hl + g
                                MM(bv(B[ch])[tp:tp + 64, hl, :], Aak[tp:tp + 64, hl, :], v_sb[tp:tp + 64, h * 64:(h + 1) * 64], True, True, [Aak, v_sb], [B[ch]])
                            CP("act", X0[tp:tp + 64, :, :], bv(B[ch])[tp:tp + 64, :, :], [B[ch]], [kc(X0, ch)])
                        ck(13)
                        drip(1)
                        TTo("dve", T_sb[0][:], N_sb[0][:], m8("c_id64"), ALU.add, [N_sb[0], cst["c_id64"]], [T_sb[0]])
                        cur = 0
                        for lvl in range(5):
                            nxt = 1 - cur
                            for ch in range(2):
                                tp = 64 * ch
                                for hl in range(8):
                                    MM(bv(B[ch])[tp:tp + 64, hl, :], N_sb[cur][tp:tp + 64, hl, :], A_sb[cur][tp:tp + 64, hl, :], True, True,
                                       [kc(N_sb[cur], ch), kc(A_sb[cur], ch)], [B[ch]])
                                    if lvl < 4:
                                        MM(bv(B[2 + ch])[tp:tp + 64, hl, :], A_sb[cur][tp:tp + 64, hl, :], N_sb[cur][tp:tp + 64, hl, :], True, True,
                                           [kc(N_sb[cur], ch), kc(A_sb[cur], ch)], [B[2 + ch]])
                            for ch in range(2):
                                tp = 64 * ch
                                CP("act", A_sb[nxt][tp:tp + 64, :, :], bv(B[ch])[tp:tp + 64, :, :], [B[ch]], [kc(A_sb[nxt], ch)])
                                if lvl < 4:
                                    CP("act", N_sb[nxt][tp:tp + 64, :, :], bv(B[2 + ch])[tp:tp + 64, :, :], [B[2 + ch]], [kc(N_sb[nxt], ch)])
                            drip(1)
                            for ch in range(2):
                                tp = 64 * ch
                                for hl in range(8):
                                    MM(bv(B[4 + ch])[tp:tp + 64, hl, :], A_sb[nxt][tp:tp + 64, hl, :], T_sb[cur][tp:tp + 64, hl, :], True, True,
                                       [kc(A_sb[nxt], ch), kc(T_sb[cur], ch)], [B[4 + ch]])
                            for ch in range(2):
                                tp = 64 * ch
                                TTo("dve", T_sb[nxt][tp:tp + 64, :, :], bv(B[4 + ch])[tp:tp + 64, :, :], T_sb[cur][tp:tp + 64, :, :], ALU.add,
                                    [B[4 + ch], kc(T_sb[cur], ch)], [kc(T_sb[nxt], ch)])
                            cur = nxt
                            drip(1)
                        TTf = T_sb[cur]
                        ck(14)
                        GX, UX, HX, Y1 = B[0], B[1], B[2], B[3]
                        for ch in range(2):
                            tp = 64 * ch
                            Y2 = B[4 + ch]
                            for hl in range(8):
                                MM(bv(GX)[tp:tp + 64, hl, :], ARt[hp:hp + 64, hl, ch, 0, :], Hb[hp:hp + 64, hl, :], True, True, [ARt, Hb], [GX])
                            TTo("dve", G_sb[tp:tp + 64, :, :], bv(GX)[tp:tp + 64, :, :], X0[tp:tp + 64, :, :], ALU.add, [GX, kc(X0, ch)], [kc(G_sb, ch)])
                            for hl in range(8):
                                MM(bv(UX)[tp:tp + 64, hl, :], TTf[tp:tp + 64, hl, :], G_sb[tp:tp + 64, hl, :], True, True, [kc(TTf, ch), kc(G_sb, ch)], [UX])
                            CP("act", U_sb[tp:tp + 64, :, :], bv(UX)[tp:tp + 64, :, :], [UX], [kc(U_sb, ch)])
                            drip(1)
                            for hl in range(8):
                                MM(bv(Y1)[tp:tp + 64, hl, :], ARt[hp:hp + 64, hl, ch, 1, :], Hb[hp:hp + 64, hl, :], True, True, [ARt, Hb], [Y1])
                            for hl in range(8):
                                h = 2 * hl + g
                                MM(bv(Y2)[tp:tp + 64, hl, :], Qrb[tp:tp + 64, hl, :], U_sb[tp:tp + 64, hl, :], True, False, [Qrb, kc(U_sb, ch)], [Y2])
                                MM(bv(Y2)[tp:tp + 64, hl, :], Qrk[tp:tp + 64, hl, :], v_sb[tp:tp + 64, h * 64:(h + 1) * 64], False, True, [Qrk, v_sb], [Y2])
                            for hl in range(8):
                                h = 2 * hl + g
                                MM(bv(HX)[hp:hp + 64, hl, :], BKhT[tp:tp + 64, hl, 0, :], U_sb[tp:tp + 64, hl, :], True, False, [BKhT, kc(U_sb, ch)], [HX])
                                MM(bv(HX)[hp:hp + 64, hl, :], BKhT[tp:tp + 64, hl, 1, :], v_sb[tp:tp + 64, h * 64:(h + 1) * 64], False, True, [BKhT, v_sb], [HX])
                            Hg = H32[hp:hp + 64, :, :]
                            TTo("dve", Hg, Hg, Pv[hp:hp + 64, :, ch, 63:64].to_broadcast([64, 8, 64]), ALU.mult, [H32, f["P"]], [H32])
                            TTo("dve", Hg, Hg, bv(HX)[hp:hp + 64, :, :], ALU.add, [H32, HX], [H32])
                            CP("act", Hb[hp:hp + 64, :, :], Hg, [H32], [Hb])
                        Yg = fl(f["sigw"]).rearrange("p (a g b) -> p a g b", g=2, b=64)[:, :, g, :]
                        CP("act", Yg, bv(Y1), [Y1], [f["sigw"]])
                        for ch in range(2):
                            tp = 64 * ch
                            TTo("dve", Yg[tp:tp + 64], Yg[tp:tp + 64], bv(B[4 + ch])[tp:tp + 64, :, :], ALU.add, [f["sigw"], B[4 + ch]], [f["sigw"]])
                    ck(15)
                    drip(1000)
                    if tt + 1 < NT:
                        F2a()
                    YK, TK, XK = f["sigw"], f["tA"], f["lp"]
                    Yf = fl(YK); Y3 = Yf.rearrange("p (a b) -> p a b", b=64)
                    tmpf = fl(TK); tmp3 = tmpf.rearrange("p (a b) -> p a b", b=64)
                    x1f = fl(XK)
                    S.op("dve", lambda e: e.tensor_reduce(out=st[:, 0, :], in_=Y3, axis=AX.X, op=ALU.add), [YK], [st])
                    ACT(tmpf, Yf, AF.Square, [YK], [TK])
                    S.op("dve", lambda e: e.tensor_reduce(out=st[:, 1, :], in_=tmp3, axis=AX.X, op=ALU.add), [TK], [st])
                    TS("dve", st[:, 0, :], st[:, 0, :], 1.0 / 64, None, ALU.mult, None, [st], [st])
                    TTo("dve", st[:, 2, :], st[:, 0, :], st[:, 0, :], ALU.mult, [st], [st])
                    STT(st[:, 1, :], st[:, 1, :], 1.0 / 64, st[:, 2, :], ALU.mult, ALU.subtract, [st], [st])
                    ACT(st[:, 1, :], st[:, 1, :], AF.Sqrt, [st], [st], bias=64e-5, scale=1.0)
                    S.op("dve", lambda e: e.reciprocal(out=st[:, 1, :], in_=st[:, 1, :]), [st], [st])
                    TTo("dve", Y3, Y3, st[:, 0, :].unsqueeze(2).to_broadcast([128, 16, 64]), ALU.subtract, [YK, st], [YK])
                    TTo("dve", Y3, Y3, st[:, 1, :].unsqueeze(2).to_broadcast([128, 16, 64]), ALU.mult, [YK, st], [YK])
                    TTo("pool", Yf, Yf, lnxg[:], ALU.mult, [YK, lnxg], [YK])
                    TTo("pool", Yf, Yf, lnxb[:], ALU.add, [YK, lnxb], [YK])
                    TTo("dve", tmp3, v_sb[:].rearrange("p (a b) -> p a b", b=64),
                        rk_sb[:].unsqueeze(2).to_broadcast([128, 16, 64]), ALU.mult, [v_sb, rk_sb], [TK])
                    TTo("dve", tmpf, tmpf, Yf, ALU.add, [TK, YK], [TK])
                    zb = View(prod, lambda: prod[:].rearrange("p a b -> p (a b)")); zT = View(BKh, lambda: BKh[:, :, 0, :])
                    TTo("dve", zb[:], tmpf, g_sb[:], ALU.mult, [TK, g_sb], [zb])
                    transpose8(zb, zT[:], dstbuf=zT)
                    for hf in range(2):
                        for dc in range(8):
                            MM(tm[hf][:], zT[:, dc, :], wo[:, dc, hf * 512:(hf + 1) * 512], dc == 0, dc == 7, [zT, wo], [tm[hf]])
                        TTo("dve", x1f[:, hf * 512:(hf + 1) * 512], tm[hf][:], gt0[:, hf * 512:(hf + 1) * 512], ALU.mult, [tm[hf], gt0], [XK])
                    S.dma("sp", fl(f["P"]), I["xv"][tt * 128:(tt + 1) * 128, :], writes=[f["P"]])
                    TTo("pool", x1f, x1f, fl(f["P"]), ALU.add, [XK, f["P"]], [XK])
                    S.dma("sp", XS[tt * 128:(tt + 1) * 128, :], x1f, reads=[XK], writes=[("XS", tt)])
                    if tt == NT - 1:
                        tap("x1_last", x1f, [XK])
                  except StopBuild:
                    break
                S.barrier()
            S.es = es0

        def ffn_stage(L, tile0, tile1, to_out):
            with contextlib.ExitStack() as es:
                S.es = es
                sh, gs, gt = ada_rows(es, I["w_ada"][2 * L + 1], I["b_ada"][2 * L + 1], 3, I["norm_g"][2 * L + 1])
                wg = S.sb([128, 8, FF], BF16, "wg"); wu = S.sb([128, 8, FF], BF16, "wu"); wd = S.sb([128, NFC, D], BF16, "wd")
                gv = I["ffn_g"][L].rearrange("(dc p) n -> p dc n", p=128); uv = I["ffn_u"][L].rearrange("(dc p) n -> p dc n", p=128)
                dv = I["ffn_d"][L].rearrange("(fc p) n -> p fc n", p=128)
                for dc in range(8):
                    S.dma("pool", wg[:, dc, :], gv[:, dc, :], writes=[wg]); S.dma("pool", wu[:, dc, :], uv[:, dc, :], writes=[wu])
                for fc in range(0, NFC, 2):
                    S.dma("pool", wd[:, fc:fc + 2, :], dv[:, fc:fc + 2, :], writes=[wd])
                xts = [S.sb([128, 2, D], F32, "fx%d" % i) for i in range(2)]; tmpk = S.sb([128, D], F32, "ftmp"); ss = S.sb([128, 4], F32, "fss")
                hb = S.sb([128, D], BF16, "fhb"); hTs = [S.sb([128, 8, 256], BF16, "fhT%d" % i) for i in range(2)]
                act = S.sb([128, NFC, 256], BF16, "fact"); sls = [S.sb([128, 256], F32, "fsl%d" % i) for i in range(2)]; x2 = S.sb([128, D], F32, "fx2")

                def front(t2):
                    xt = xts[t2 % 2]; hT = hTs[t2 % 2]
                    for sub in range(2):
                        tt = t2 * 2 + sub
                        xs_v = View(xt, lambda sub=sub, xt=xt: xt[:, sub, :])
                        S.dma("sp", xt[:, sub, :], XS[tt * 128:(tt + 1) * 128, :], reads=[("XS", tt)], writes=[xt])
                        norm_tile(xs_v, gs, sh, hb, tmpk, ss)
                        for dc in range(8):
                            TR(psT[:, dc * 128:(dc + 1) * 128], hb[:, dc * 128:(dc + 1) * 128], identb[:], [hb, identb], [psT])
                        CP("act", hT[:, :, sub * 128:(sub + 1) * 128], psT[:].rearrange("p (a b) -> p a b", b=128), [psT], [hT])

                def back(t2):
                    xt = xts[t2 % 2]; hT = hTs[t2 % 2]
                    for fc in range(NFC):
                        pg = PSB[fc % 2]; pu = PSB[2 + fc % 2]
                        for dc in range(8):
                            MM(pg[:, 0:256], wg[:, dc, fc * 128:(fc + 1) * 128], hT[:, dc, :], dc == 0, dc == 7, [wg, hT], [pg])
                        for dc in range(8):
                            MM(pu[:, 0:256], wu[:, dc, fc * 128:(fc + 1) * 128], hT[:, dc, :], dc == 0, dc == 7, [wu, hT], [pu])
                        sl = sls[fc % 2]
                        ACT(sl[:], pg[:, 0:256], AF.Silu, [pg], [sl])
                        TTo("dve", act[:, fc, :], pu[:, 0:256], sl[:], ALU.mult, [pu, sl], [(act, fc)])
                        if fc == 10 and t2 + 1 < tile1 // 2:
                            front(t2 + 1)
                    for sub in range(2):
                        tt = t2 * 2 + sub
                        for hf in range(2):
                            po = PSB[4 + hf]
                            for fc in range(NFC):
                                MM(po[:], act[:, fc, sub * 128:(sub + 1) * 128], wd[:, fc, hf * 512:(hf + 1) * 512], fc == 0, fc == NFC - 1, [(act, fc), wd], [po])
                            TTo("dve", x2[:, hf * 512:(hf + 1) * 512], po[:], gt[:, hf * 512:(hf + 1) * 512], ALU.mult, [po, gt], [x2])
                        TTo("pool", x2[:], x2[:], xt[:, sub, :], ALU.add, [x2, xt], [x2])
                        if to_out:
                            S.dma("sp", out[(tt - tile0) * 128:(tt - tile0 + 1) * 128, :], x2[:], reads=[x2], writes=[("OUT", tt)])
                        else:
                            S.dma("sp", XS[tt * 128:(tt + 1) * 128, :], x2[:], reads=[x2], writes=[("XS", tt)])
                        if tt == tile1 - 1:
                            tap("ffn%d_last" % L, x2[:], [x2])
                front(tile0 // 2)
                for t2 in range(tile0 // 2, tile1 // 2):
                    back(t2)
                S.barrier()
            S.es = es0

        if "C" in stages:
            ffn_stage(0, 0, NT, False)

        VS2 = nc.dram_tensor("v2_scr", [8, T, 130], BF16, kind="Internal").ap()
        kmean = S.sb([128, 8, NB], BF16, "kmean")
        hnorm_tmp = {}

        def head_rmsnorm(src_ps_halves, grow, dst_f32_key, dst_ap, sq_key, sq_ap, st, extra_scale=1.0):
            for hf in range(2):
                ACT(sq_ap[:, hf * 512:(hf + 1) * 512], src_ps_halves[hf][:], AF.Square, [src_ps_halves[hf]], [sq_key])
            S.op("dve", lambda e: e.tensor_reduce(out=st[:, 0, :], in_=sq_ap.rearrange("p (a b) -> p a b", b=64), axis=AX.X, op=ALU.add), [sq_key], [st])
            ACT(st[:, 1, :], st[:, 0, :], AF.Sqrt, [st], [st], bias=1e-6, scale=1.0 / 64)
            S.op("dve", lambda e: e.reciprocal(out=st[:, 1, :], in_=st[:, 1, :]), [st], [st])
            if extra_scale != 1.0:
                TS("dve", st[:, 1, :], st[:, 1, :], extra_scale, None, ALU.mult, None, [st], [st])
            for hf in range(2):
                TTo("dve", dst_ap[:, hf * 512:(hf + 1) * 512].rearrange("p (a b) -> p a b", b=64),
                    src_ps_halves[hf][:].rearrange("p (a b) -> p a b", b=64),
                    st[:, 1, hf * 8:(hf + 1) * 8].unsqueeze(2).to_broadcast([128, 8, 64]), ALU.mult, [src_ps_halves[hf], st], [dst_f32_key])
            TTo("pool", dst_ap, dst_ap, grow[:], ALU.mult, [dst_f32_key, grow], [dst_f32_key])

        if "D" in stages:
            with contextlib.ExitStack() as es:
                S.es = es
                shk, gsk = ada_rows(es, I["kv_w_ada"], I["kv_b_ada"], 2, I["kv_norm_g"])
                wk = S.sb([128, 8, D], BF16, "wk"); wv = S.sb([128, 8, D], BF16, "wv")
                S.dma("pool", wk[:], I["kv_wk"].rearrange("(dc p) n -> p dc n", p=128), writes=[wk])
                S.dma("pool", wv[:], I["kv_wv"].rearrange("(dc p) n -> p dc n", p=128), writes=[wv])
                kng = S.sb([128, D], F32, "kng"); S.dma("sp", kng[:], I["k_norm_g"].partition_broadcast(128), writes=[kng])
                onesc = S.sb([128, 1], F32, "onesc"); S.op("dve", lambda e: e.memset(onesc[:], 1.0 / 256), (), [onesc])
                xts = [S.sb([128, D], F32, "dx_%d" % i) for i in range(2)]; tmpk = S.sb([128, D], F32, "dtmp"); ss = S.sb([128, 4], F32, "dss")
                hb = S.sb([128, D], BF16, "dhb"); hTs = [S.sb([128, 8, 128], BF16, "dhT%d" % i) for i in range(2)]
                kf = S.sb([128, D], F32, "kf"); kb = S.sb([128, D], BF16, "kb"); sq = S.sb([128, D], F32, "ksq")
                st = S.sb([128, 2, 16], F32, "kst"); kTs = [S.sb([128, 8, 128], BF16, "kT%d" % i) for i in range(2)]
                vexts = [S.sb([128, 16, 65], BF16, "vext%d" % i) for i in range(2)]
                for v_ in vexts:
                    S.op("dve", lambda e: e.memset(v_[:], 1.0), (), [v_])
                kmacc = S.sb([128, 8], F32, "kmacc")

                def dfront(tt):
                    xt = xts[tt % 2]; hT = hTs[tt % 2]
                    S.dma("sp", xt[:], XS[tt * 128:(tt + 1) * 128, :], reads=[("XS", tt)], writes=[xt])
                    norm_tile(xt, gsk, shk, hb, tmpk, ss)
                    transpose8(hb, hT[:], dstbuf=hT)

                kfs = [kf, S.sb([128, D], F32, "kf2")]; kbs = [kb, S.sb([128, D], BF16, "kb2")]

                def dmm(tt):
                    hT = hTs[tt % 2]; vext = vexts[tt % 2]
                    KB = [PSB[0], PSB[1]] if tt % 2 == 0 else [PSB[5], PSB[6]]
                    for hf in range(2):
                        for dc in range(8):
                            MM(KB[hf][:], hT[:, dc, :], wk[:, dc, hf * 512:(hf + 1) * 512], dc == 0, dc == 7, [hT, wk], [KB[hf]])
                    for hf in range(2):
                        for dc in range(8):
                            MM(PSB[2 + hf][:], hT[:, dc, :], wv[:, dc, hf * 512:(hf + 1) * 512], dc == 0, dc == 7, [hT, wv], [PSB[2 + hf]])
                        CP("act", vext[:, hf * 8:(hf + 1) * 8, 0:64], PSB[2 + hf][:].rearrange("p (a b) -> p a b", b=64), [PSB[2 + hf]], [vext])

                def dhead(tt):
                    KB = [PSB[0], PSB[1]] if tt % 2 == 0 else [PSB[5], PSB[6]]
                    kf_ = kfs[tt % 2]; kb_ = kbs[tt % 2]
                    head_rmsnorm(KB, kng, kf_, kf_[:], sq, sq[:], st)
                    CP("act", kb_[:], kf_[:], [kf_], [kb_])

                def dpost(tt):
                    kT = kTs[tt % 2]; vext = vexts[tt % 2]; kf_ = kfs[tt % 2]; kb_ = kbs[tt % 2]
                    for pr in range(8):
                        MM(PSB[4][:, pr:pr + 1], kf_[:, pr * 128:(pr + 1) * 128], onesc[:], True, True, [kf_, onesc], [PSB[4]])
                    if tt % 2 == 0:
                        CP("dve", kmacc[:], PSB[4][:, 0:8], [PSB[4]], [kmacc])
                    else:
                        TTo("dve", kmean[:, :, tt // 2], PSB[4][:, 0:8], kmacc[:], ALU.add, [PSB[4], kmacc], [kmean])
                    transpose8(kb_, kT[:], dstbuf=kT)
                    S.dma("sp", KT[:, :, tt * 128:(tt + 1) * 128].rearrange("a p t -> p a t"), kT[:], reads=[kT], writes=[("KT", tt)])
                    S.dma("sp", VS2[:, tt * 128:(tt + 1) * 128, :].rearrange("a p c -> p a c"),
                          vext[:].rearrange("p (a g) c -> p a (g c)", g=2), reads=[vext], writes=[("VS2", tt)])
                    if tt == NT - 1:
                        tap("kf_last", kf_[:], [kf_])
                dfront(0)
                for tt in range(NT):
                    dmm(tt)
                    if tt + 1 < NT:
                        dfront(tt + 1)
                    dhead(tt)
                    if tt >= 1:
                        dpost(tt - 1)
                dpost(NT - 1)
                S.barrier()
            S.es = es0

        if "E" in stages:
            NQT = NT // 2
            NQB = NB // 2
            gt1p = S.sb([128, D], F32, "gt1p")
            with contextlib.ExitStack() as es:
                S.es = es
                sh1, gs1, gt1 = ada_rows(es, I["w_ada"][2], I["b_ada"][2], 3, I["norm_g"][2])
                CP("pool", gt1p[:], gt1[:], [gt1], [gt1p])
                wq = S.sb([128, 8, D], BF16, "wq")
                S.dma("pool", wq[:], I["mb_wq"].rearrange("(dc p) n -> p dc n", p=128), writes=[wq])
                qng = S.sb([128, D], F32, "qng"); S.dma("sp", qng[:], I["q_norm_g"].partition_broadcast(128), writes=[qng])
                bb = S.sb([128, NB], F32, "bb"); S.dma("sp", bb[:], I["bbias"], writes=[bb])
                fut = S.sb([128, NB, NB], F32, "fut"); S.dma("sp", fut[:], I["c_fut"], writes=[fut])
                xts = [S.sb([128, D], F32, "ex%d" % i) for i in range(2)]; tmpk = S.sb([128, D], F32, "etmp"); ss = S.sb([128, 4], F32, "ess")
                hb = S.sb([128, D], BF16, "ehb"); hTs = [S.sb([128, 8, 128], BF16, "ehT%d" % i) for i in range(2)]
                qf = S.sb([128, D], F32, "qf"); qb_ = S.sb([128, D], BF16, "qb"); sq = S.sb([128, D], F32, "qsq")
                st = S.sb([128, 2, 16], F32, "qst"); qTs = [S.sb([128, 8, 128], BF16, "qT%d" % i) for i in range(2)]
                gsb = S.sb([128, 16, NB], F32, "gsb"); m8t = S.sb([128, 16, 8], F32, "m8t"); selts = [S.sb([128, 16, NB], F32, "selt%d" % i) for i in range(2)]
                sel2 = S.sb([128, 16, NB], F32, "sel2"); brow = S.sb([128, NB], F32, "browq")

                def efront(qt):
                    tt = NQT + qt
                    xt = xts[qt % 2]; hT = hTs[qt % 2]
                    S.dma("sp", xt[:], XS[tt * 128:(tt + 1) * 128, :], reads=[("XS", tt)], writes=[xt])
                    norm_tile(xt, gs1, sh1, hb, tmpk, ss)
                    transpose8(hb, hT[:], dstbuf=hT)

                def eback(qt):
                    tt = NQT + qt
                    vb = tt // 2
                    hT = hTs[qt % 2]; qT = qTs[qt % 2]; selt = selts[qt % 2]
                    QB = [PSB[0], PSB[1]] if qt % 2 == 0 else [PSB[5], PSB[6]]
                    for hf in range(2):
                        for dc in range(8):
                            MM(QB[hf][:], hT[:, dc, :], wq[:, dc, hf * 512:(hf + 1) * 512], dc == 0, dc == 7, [hT, wq], [QB[hf]])
                    if qt + 1 < NQT:
                        efront(qt + 1)
                    head_rmsnorm(QB, qng, qf, qf[:], sq, sq[:], st, extra_scale=0.125)
                    CP("act", qb_[:], qf[:], [qf], [qb_])
                    transpose8(qb_, qT[:], dstbuf=qT)
                    S.dma("sp", QT[:, :, qt * 128:(qt + 1) * 128].rearrange("a p t -> p a t"), qT[:], reads=[qT], writes=[("QT", qt)])
                    for g in range(2):
                        hp = 64 * g
                        for pr in range(8):
                            MM(PSB[2 + g][:, pr * NB:(pr + 1) * NB], qT[hp:hp + 64, pr, :], kmean[hp:hp + 64, pr, :], True, True, [qT, kmean], [PSB[2 + g]])
                    TTo("dve", brow[:], bb[:], fut[:, vb, :], ALU.add, [bb, fut], [brow])
                    gv4 = gsb[:].rearrange("p (a g) n -> p a g n", g=2)
                    for g in range(2):
                        TTo("dve", gv4[:, :, g, :], PSB[2 + g][:, 0:8 * NB].rearrange("p (a n) -> p a n", n=NB),
                            brow[:].unsqueeze(1).to_broadcast([128, 8, NB]), ALU.add, [PSB[2 + g], brow], [gsb])
                    for h in range(16):
                        S.op("dve", lambda e: e.max(out=m8t[:, h, :], in_=gsb[:, h, :]), [gsb], [m8t])
                    TTo("dve", selt[:], gsb[:], m8t[:, :, 2:3].to_broadcast([128, 16, NB]), ALU.is_ge, [gsb, m8t], [selt])
                    TS("dve", sel2[:], gsb[:], -1.0e29, None, ALU.is_gt, None, [gsb], [sel2])
                    TTo("dve", selt[:], selt[:], sel2[:], ALU.mult, [selt, sel2], [selt])
                    S.dma("sp", SEL[qt * 128:(qt + 1) * 128, :], selt[:].rearrange("p a n -> p (a n)"), reads=[selt], writes=[("SEL", qt)])
                    if qt == NQT - 1:
                        tap("sel_last", selt[:].rearrange("p a n -> p (a n)"), [selt])
                        tap("qf_last", qf[:], [qf])
                efront(0)
                for qt in range(NQT):
                    eback(qt)
                S.barrier()
            S.es = es0
            with contextlib.ExitStack() as es:
                S.es = es
                Kp = S.sb([128, T], BF16, "Kp"); Vp = S.sb([128, NT, 130], BF16, "Vp"); Qp = S.sb([128, TH], BF16, "Qp")
                caus = S.sb([128, 512], BF16, "caus"); S.dma("pool", caus[:], I["c_causal"].rearrange("p a b -> p (a b)"), writes=[caus])
                accs = [S.sb([128, 2, 2, 65], F32, "acc%d" % i) for i in range(2)]; pt = [S.sb([128, 512], BF16, "pt%d" % i) for i in range(6)]
                selps = [S.sb([128, 2, 2, NB], F32, "selp%d" % i) for i in range(2)]
                rc = S.sb([128, 4], F32, "rc"); ob = S.sb([128, 2, 128], BF16, "ob")
                SELv = SEL.rearrange("(q p) (h n) -> p q h n", p=128, n=NB)
                for pr in range(8):
                    S.dma("sp", Kp[:], KT[pr], reads=["KT"], writes=[Kp])
                    S.dma("sp", Vp[:], VS2[pr].rearrange("(n p) c -> p n c", p=128), reads=["VS2"], writes=[Vp])
                    S.dma("sp", Qp[:], QT[pr], reads=["QT"], writes=[Qp])
                    Vp4 = Vp[:].rearrange("p n (g c) -> p n g c", g=2)
                    for qb in range(NQB):
                        vb = NQB + qb
                        acc = accs[qb % 2]; selp = selps[qb % 2]
                        S.dma("act", selp[:], SELv[:, 2 * qb:2 * qb + 2, 2 * pr:2 * pr + 2, :], reads=["SEL"], writes=[selp])
                        S.op("pool", lambda e: e.memset(acc[:], 0.0), (), [acc])
                        SB6 = [PSB[0], PSB[1], PSB[2], PSB[3], PSB[4], PSB[5]]; OB2 = [PSB[6], psTf]

                        def emitS(n):
                            for kt in range(2):
                                for g in range(2):
                                    hp = 64 * g
                                    sb_ = SB6[(n % 3) * 2 + g]
                                    MM(sb_[:, kt * 256:(kt + 1) * 256], Kp[hp:hp + 64, n * 256 + kt * 128:n * 256 + (kt + 1) * 128],
                                       Qp[hp:hp + 64, qb * 256:(qb + 1) * 256], True, True, [Kp, Qp], [sb_])

                        def emitRest(n):
                            own = (n == vb)
                            for g in range(2):
                                sb_ = SB6[(n % 3) * 2 + g]; p_ = pt[(n % 3) * 2 + g]
                                ACT(p_[:], sb_[:], AF.Exp, [sb_], [p_])
                                if own:
                                    TTo("pool", p_[:], p_[:], caus[:], ALU.mult, [p_, caus], [p_])
                            for g in range(2):
                                p_ = pt[(n % 3) * 2 + g]; ob_ = OB2[g]
                                for qtl in range(2):
                                    for kt in range(2):
                                        MM(ob_[:, qtl * 65:(qtl + 1) * 65], p_[:, kt * 256 + qtl * 128:kt * 256 + (qtl + 1) * 128],
                                           Vp4[:, n * 2 + kt, g, :], kt == 0, kt == 1, [p_, Vp], [ob_])
                            for g in range(2):
                                ob_ = OB2[g]
                                for qtl in range(2):
                                    if own:
                                        TTo("dve", acc[:, qtl, g, :], ob_[:, qtl * 65:(qtl + 1) * 65], acc[:, qtl, g, :], ALU.add, [ob_, (acc, qtl * 2 + g)], [(acc, qtl * 2 + g)])
                                    else:
                                        STT(acc[:, qtl, g, :], ob_[:, qtl * 65:(qtl + 1) * 65], selp[:, qtl, g, n:n + 1], acc[:, qtl, g, :],
                                            ALU.mult, ALU.add, [ob_, selp, (acc, qtl * 2 + g)], [(acc, qtl * 2 + g)])
                        emitS(0)
                        if vb >= 1:
                            emitS(1)
                        for n in range(vb + 1):
                            if n + 2 <= vb:
                                emitS(n + 2)
                            emitRest(n)
                        S.op("dve", lambda e: e.reciprocal(out=rc[:].rearrange("p (a b) -> p a b", b=2), in_=acc[:, :, :, 64]), [acc], [rc])
                        TTo("dve", ob[:].rearrange("p q (g c) -> p q g c", g=2), acc[:, :, :, 0:64],
                            rc[:].rearrange("p (a b) -> p a b", b=2).unsqueeze(3).to_broadcast([128, 2, 2, 64]), ALU.mult, [acc, rc], [ob])
                        S.dma("pool", OS[qb * 256:(qb + 1) * 256, pr * 128:(pr + 1) * 128].rearrange("(q p) c -> p q c", p=128), ob[:],
                              reads=[ob], writes=[("OS", pr * 1000 + qb)])
                S.barrier()
            S.es = es0
            with contextlib.ExitStack() as es:
                S.es = es
                gt1 = gt1p
                wo1 = S.sb([128, 8, D], BF16, "wo1")
                S.dma("pool", wo1[:], I["mb_wo"].rearrange("(dc p) n -> p dc n", p=128), writes=[wo1])
                xts = [S.sb([128, D], F32, "e3x%d" % i) for i in range(2)]; ots = [S.sb([128, D], BF16, "e3o%d" % i) for i in range(2)]
                oTs = [S.sb([128, 8, 128], BF16, "e3oT%d" % i) for i in range(2)]
                x3s = [S.sb([128, D], F32, "e3x3%d" % i) for i in range(2)]

                def ffront(qt):
                    tt = NQT + qt
                    xt = xts[qt % 2]; ot = ots[qt % 2]; oT = oTs[qt % 2]
                    S.dma("sp", xt[:], XS[tt * 128:(tt + 1) * 128, :], reads=[("XS", tt)], writes=[xt])
                    S.dma("sp", ot[:], OS[qt * 128:(qt + 1) * 128, :], reads=["OS"], writes=[ot])
                    transpose8(ot, oT[:], dstbuf=oT)

                def fback(qt):
                    tt = NQT + qt
                    xt = xts[qt % 2]; oT = oTs[qt % 2]; x3 = x3s[qt % 2]
                    for hf in range(2):
                        for dc in range(8):
                            MM(PSB[hf][:], oT[:, dc, :], wo1[:, dc, hf * 512:(hf + 1) * 512], dc == 0, dc == 7, [oT, wo1], [PSB[hf]])
                        if hf == 0 and qt + 1 < NQT:
                            ffront(qt + 1)
                        TTo("dve", x3[:, hf * 512:(hf + 1) * 512], PSB[hf][:], gt1[:, hf * 512:(hf + 1) * 512], ALU.mult, [PSB[hf], gt1], [x3])
                    TTo("pool", x3[:], x3[:], xt[:], ALU.add, [x3, xt], [x3])
                    S.dma("sp", XS[tt * 128:(tt + 1) * 128, :], x3[:], reads=[x3], writes=[("XS", tt)])
                    if qt == NQT - 1:
                        tap("x3_last", x3[:], [x3])
                ffront(0)
                for qt in range(NQT):
                    fback(qt)
                S.barrier()
            S.es = es0

        if "F" in stages:
            ffn_stage(1, NT // 2, NT, True)

        S.finish()
        print("ninst", S.ninst, "nwait", S.nwait)
    return nc


def _consts():
    p = np.arange(128)
    c = {}
    c["c_ident"] = np.eye(128, dtype=np.float32)
    rm = np.ones((128, 1024), np.float32); rm[:, ::64] = 0.0
    c["c_rmask"] = rm
    t = np.arange(64)
    pm = (p % 64)[:, None]
    c["c_msu"] = (pm < t[None, :]).astype(np.float32)
    c["c_msl"] = (t[None, :] < pm).astype(np.float32)
    c["c_miu"] = (pm <= t[None, :]).astype(np.float32)
    c["c_id64"] = (pm == t[None, :]).astype(np.float32)
    c["c_bones"] = ((p[:, None] // 64) == (p[None, :] // 64)).astype(np.float32)
    c["c_hsel"] = ((p[:, None] // 64) == np.arange(2)[None, :]).astype(np.float32)
    q = np.arange(256)
    cz = np.zeros((128, 2, 256), np.float32)
    for kt in range(2):
        cz[:, kt, :] = ((kt * 128 + p)[:, None] <= q[None, :])
    c["c_causal"] = cz
    return c


def prep_shared(inp):
    f = lambda a: np.ascontiguousarray(np.asarray(a, dtype=np.float32))
    pp = lambda v: f(np.asarray(v).reshape(8, 128).T)
    sh = {}
    sh["w_ada"] = f(np.asarray(inp["w_ada"]).reshape(4, D, 3 * D))
    sh["b_ada"] = f(np.asarray(inp["b_ada"]).reshape(4, 3 * D))
    sh["norm_g"] = f(np.asarray(inp["norm_g"]).reshape(4, D))
    sh["mu"] = f(np.asarray(inp["rw_mu"])[0].reshape(6, 8, 128).transpose(2, 0, 1))
    sh["w_rkv"] = f(np.asarray(inp["rw_w_rkv"])[0])
    sh["w0"] = pp(inp["rw_w0"][0]); sh["a0"] = pp(inp["rw_a0"][0]); sh["k_k"] = pp(inp["rw_k_k"][0])
    sh["k_a"] = pp(inp["rw_k_a"][0]); sh["r_k"] = pp(np.asarray(inp["rw_r_k"])[0].reshape(-1))
    sh["w1"] = f(inp["rw_w1"][0]); sh["w2"] = f(inp["rw_w2"][0]); sh["a1"] = f(inp["rw_a1"][0]); sh["a2"] = f(inp["rw_a2"][0])
    sh["g1"] = f(inp["rw_g1"][0]); sh["g2"] = f(inp["rw_g2"][0])
    sh["lnx_g"] = f(inp["rw_lnx_g"][0]); sh["lnx_b"] = f(inp["rw_lnx_b"][0]); sh["rw_wo"] = f(inp["rw_w_o"][0])
    sh["ffn_g"] = f(inp["ffn_w_gate"]); sh["ffn_u"] = f(inp["ffn_w_up"]); sh["ffn_d"] = f(inp["ffn_w_down"])
    sh["kv_norm_g"] = f(inp["kv_norm_g"]); sh["kv_w_ada"] = f(inp["kv_w_ada"]); sh["kv_b_ada"] = f(inp["kv_b_ada"])
    sh["kv_wk"] = f(inp["kv_w_k"]); sh["kv_wv"] = f(inp["kv_w_v"])
    sh["k_norm_g"] = f(np.tile(np.asarray(inp["k_norm_g"]), NH)); sh["q_norm_g"] = f(np.tile(np.asarray(inp["mb_q_norm_g"])[0], NH))
    sh["mb_wq"] = f(inp["mb_w_q"][0]); sh["mb_wo"] = f(inp["mb_w_o"][0])
    sh.update(_consts())
    return sh


def prep_core(x_b, c_b, hf, T):
    TH = T // 2
    NT, NB = T // 128, T // 256
    m = {}
    if hf == 1:
        xv = np.asarray(x_b, dtype=np.float32)
        valid = np.ones(T, np.float32)
    else:
        xv = np.concatenate([np.zeros((TH, D), np.float32), np.asarray(x_b[:TH], dtype=np.float32)], 0)
        valid = np.concatenate([np.zeros(TH, np.float32), np.ones(TH, np.float32)])
    m["xv"] = np.ascontiguousarray(xv)
    m["valid"] = np.ascontiguousarray(valid.reshape(NT, 128).T)
    bb = np.where(valid.reshape(NB, 256)[:, 0] > 0, 0.0, NEG).astype(np.float32)
    m["bbias"] = np.ascontiguousarray(np.tile(bb[None, :], (128, 1)))
    m["cvec"] = np.ascontiguousarray(np.asarray(c_b, dtype=np.float32).reshape(8, 128).T)
    fu = np.where(np.arange(NB)[None, :] >= np.arange(NB)[:, None], NEG, 0.0).astype(np.float32)
    m["c_fut"] = np.ascontiguousarray(np.tile(fu[None], (128, 1, 1)))
    return m


_NC_CACHE = {}


def kernel(**inputs):
    x = np.asarray(inputs["x"], dtype=np.float32)
    c = np.asarray(inputs["c"], dtype=np.float32)
    Bn, T, _ = x.shape
    TH = T // 2
    key = (T,)
    if key not in _NC_CACHE:
        _NC_CACHE[key] = build(T)
    nc = _NC_CACHE[key]
    sh = prep_shared(inputs)
    in_maps = []
    for b in range(Bn):
        for hf in range(2):
            m = dict(sh)
            m.update(prep_core(x[b], c[b], hf, T))
            in_maps.append(m)
    res = run_bass_kernel_spmd(nc, in_maps, core_ids=list(range(2 * Bn)))
    outp = np.empty((Bn, T, D), np.float32)
    for b in range(Bn):
        for hf in range(2):
            outp[b, hf * TH:(hf + 1) * TH] = res.results[b * 2 + hf]["out"]
    return outp
```

```python
import contextlib
import numpy as np
import concourse.bass as bass
import concourse.mybir as mybir
from concourse.bass_utils import run_bass_kernel_spmd

F32 = mybir.dt.float32
BF16 = mybir.dt.bfloat16
AF = mybir.ActivationFunctionType
ALU = mybir.AluOpType
AX = mybir.AxisListType

D = 1024
NH = 16
HD = 64
FF = 2816
NFC = FF // 128
CDEC = float(np.exp(-0.5))
NEG = -1.0e30
NO_SELF_SYNC = ()


class Buf:
    __slots__ = ("t", "name")

    def __init__(self, t, name):
        self.t = t
        self.name = name

    def __getitem__(self, k):
        return self.t[k]


class View:
    def __init__(self, key, fn):
        self.key = key
        self.fn = fn

    def __getitem__(self, k):
        return self.fn()[k]


class TokSet:
    __slots__ = ("d",)

    def __init__(self):
        self.d = {}

    def add(self, tok):
        sem, val = tok
        k = id(sem)
        if k not in self.d or self.d[k][1] < val:
            self.d[k] = (sem, val)

    def items_(self):
        return list(self.d.values())


class Sched:
    def __init__(self, nc, es, ndma=10):
        self.nc = nc
        self.es = es
        self.eng = {"pe": nc.tensor, "act": nc.scalar, "dve": nc.vector, "pool": nc.gpsimd, "sp": nc.sync}
        self.esem = {k: es.enter_context(nc.semaphore("e_" + k)) for k in self.eng}
        self.ecnt = {k: 0 for k in self.eng}
        self.seen = {k: {} for k in self.eng}
        self.lastw = {}
        self.readers = {}
        self.dpool = {}
        self.dnext = {}
        self.dcnt = {}
        for q in ("sp", "pool", "act"):
            self.dpool[q] = [es.enter_context(nc.semaphore("d_%s%d" % (q, i))) for i in range(ndma)]
            self.dnext[q] = 0
        self.nbuf = 0
        self.parts = {}
        self.ninst = {k: 0 for k in self.eng}
        self.nwait = 0

    def sb(self, shape, dt=F32, name=None):
        self.nbuf += 1
        name = (name or "sb") + "_%d" % self.nbuf
        t = self.es.enter_context(self.nc.sbuf_tensor(name, list(shape), dt))
        return Buf(t, name)

    def ps(self, shape, dt=F32, name=None):
        self.nbuf += 1
        name = (name or "ps") + "_%d" % self.nbuf
        t = self.es.enter_context(self.nc.psum_tensor(name, list(shape), dt))
        return Buf(t, name)

    def _wait(self, engine, deps):
        e = self.eng[engine]
        seen = self.seen[engine]
        best = {}
        for (sem, val) in deps:
            k = id(sem)
            if k not in best or best[k][1] < val:
                best[k] = (sem, val)
        for k, (sem, val) in best.items():
            if engine == "pe" and sem is self.esem["pe"]:
                continue
            if engine in NO_SELF_SYNC and sem is self.esem.get(engine):
                continue
            if seen.get(k, 0) < val:
                e.wait_ge(sem, val)
                self.nwait += 1
                seen[k] = val

    def _rel(self, k):
        if isinstance(k, tuple):
            self.parts.setdefault(k[0], set()).add(k)
            return (k, k[0])
        ps = self.parts.get(k)
        return (k,) + tuple(ps) if ps else (k,)

    def _collect(self, reads, writes):
        deps = []
        reads = [getattr(b, "key", b) for b in reads]
        writes = [getattr(b, "key", b) for b in writes]
        for b0 in reads:
            for b in self._rel(b0):
                w = self.lastw.get(b)
                if w:
                    deps.extend(w.items_())
        for b0 in writes:
            for b in self._rel(b0):
                w = self.lastw.get(b)
                if w:
                    deps.extend(w.items_())
                r = self.readers.get(b)
                if r:
                    deps.extend(r.items_())
        return deps

    def _commit(self, tok, reads, writes):
        reads = [getattr(b, "key", b) for b in reads]
        writes = [getattr(b, "key", b) for b in writes]
        for b in reads:
            r = self.readers.get(b)
            if r is None:
                r = self.readers[b] = TokSet()
            r.add(tok)
        for b in writes:
            w = TokSet()
            w.add(tok)
            self.lastw[b] = w
            self.readers[b] = TokSet()

    def op(self, engine, fn, reads=(), writes=()):
        self._wait(engine, self._collect(reads, writes))
        inst = fn(self.eng[engine])
        self.ecnt[engine] += 1
        self.ninst[engine] += 1
        inst.then_inc(self.esem[engine], 1)
        tok = (self.esem[engine], self.ecnt[engine])
        self._commit(tok, reads, writes)
        return tok

    def dma(self, queue, out, in_, reads=(), writes=(), **kw):
        pool = self.dpool[queue]
        i = self.dnext[queue]
        self.dnext[queue] = (i + 1) % len(pool)
        sem = pool[i]
        prev = self.dcnt.get(id(sem), 0)
        deps = self._collect(reads, writes)
        if prev:
            deps.append((sem, prev))
        self._wait(queue, deps)
        inst = self.eng[queue].dma_start(out=out, in_=in_, **kw)
        inst.then_inc(sem, 16)
        self.ninst[queue] += 1
        self.dcnt[id(sem)] = prev + 16
        tok = (sem, prev + 16)
        self._commit(tok, reads, writes)
        return tok

    def all_tokens(self):
        deps = []
        for q, pool in self.dpool.items():
            for sem in pool:
                v = self.dcnt.get(id(sem), 0)
                if v:
                    deps.append((sem, v))
        for k in self.eng:
            if self.ecnt[k]:
                deps.append((self.esem[k], self.ecnt[k]))
        return deps

    def barrier(self):
        deps = self.all_tokens()
        for k in self.eng:
            self._wait(k, deps)

    def finish(self):
        self._wait("sp", self.all_tokens())


class StopBuild(Exception):
    pass


def build(T, stages="ABCDEF", taps=(), stop=None):
    NT = T // 128
    NB = T // 256
    TH = T // 2
    nc = bass.Bass("TRN2", target_bir_lowering=False)
    I = {}

    def inp(name, shape, dt=F32):
        I[name] = nc.dram_tensor(name, list(shape), dt, kind="ExternalInput").ap()
        return I[name]

    inp("xv", [T, D]); inp("valid", [128, NT]); inp("bbias", [128, NB]); inp("cvec", [128, 8])
    inp("w_ada", [4, D, 3 * D]); inp("b_ada", [4, 3 * D]); inp("norm_g", [4, D])
    inp("mu", [128, 6, 8]); inp("w_rkv", [3, D, D])
    for n in ("w0", "a0", "k_k", "k_a", "r_k"):
        inp(n, [128, 8])
    inp("w1", [D, 64]); inp("w2", [64, D]); inp("a1", [D, 64]); inp("a2", [64, D])
    inp("g1", [D, 160]); inp("g2", [160, D]); inp("lnx_g", [D]); inp("lnx_b", [D]); inp("rw_wo", [D, D])
    inp("ffn_g", [2, D, FF]); inp("ffn_u", [2, D, FF]); inp("ffn_d", [2, FF, D])
    inp("kv_norm_g", [D]); inp("kv_w_ada", [D, 2 * D]); inp("kv_b_ada", [2 * D])
    inp("kv_wk", [D, D]); inp("kv_wv", [D, D]); inp("k_norm_g", [D]); inp("q_norm_g", [D])
    inp("mb_wq", [D, D]); inp("mb_wo", [D, D])
    inp("c_ident", [128, 128]); inp("c_rmask", [128, 1024]); inp("c_msu", [128, 64]); inp("c_msl", [128, 64])
    inp("c_miu", [128, 64]); inp("c_id64", [128, 64]); inp("c_bones", [128, 128]); inp("c_hsel", [128, 2])
    inp("c_causal", [128, 2, 256]); inp("c_fut", [128, NB, NB])

    out = nc.dram_tensor("out", [TH, D], F32, kind="ExternalOutput").ap()
    XS = nc.dram_tensor("xs_scr", [T, D], F32, kind="Internal").ap()
    KT = nc.dram_tensor("kt_scr", [8, 128, T], BF16, kind="Internal").ap()
    VS = nc.dram_tensor("v_scr", [T, D], BF16, kind="Internal").ap()
    QT = nc.dram_tensor("qt_scr", [8, 128, TH], BF16, kind="Internal").ap()
    SEL = nc.dram_tensor("sel_scr", [TH, NH * NB], F32, kind="Internal").ap()
    OS = nc.dram_tensor("o_scr", [TH, D], BF16, kind="Internal").ap()
    TAP = {}
    for (name, shape) in taps:
        TAP[name] = nc.dram_tensor("tap_" + name, list(shape), F32, kind="ExternalOutput").ap()

    xs_key = "XS"
    with contextlib.ExitStack() as es0:
        S = Sched(nc, es0)

        def TTo(eng, out, a, b, op, reads, writes):
            return S.op(eng, lambda e: e.tensor_tensor(out=out, in0=a, in1=b, op=op), reads, writes)

        def STT(out, a, sc, b, op0, op1, reads, writes):
            return S.op("dve", lambda e: e.scalar_tensor_tensor(out=out, in0=a, scalar=sc, in1=b, op0=op0, op1=op1), reads, writes)

        def TS(eng, out, a, s1, s2, op0, op1, reads, writes):
            if op1 is None:
                return S.op(eng, lambda e: e.tensor_scalar(out=out, in0=a, scalar1=s1, scalar2=None, op0=op0), reads, writes)
            return S.op(eng, lambda e: e.tensor_scalar(out=out, in0=a, scalar1=s1, scalar2=s2, op0=op0, op1=op1), reads, writes)

        def ACT(out, in_, func, reads, writes, bias=None, scale=None, accum=None):
            kw = {}
            if bias is not None:
                kw["bias"] = bias
            if scale is not None:
                kw["scale"] = scale
            if accum is not None:
                kw["accum_out"] = accum
            return S.op("act", lambda e: e.activation(out=out, in_=in_, func=func, **kw), reads, writes)

        def CP(eng, out, in_, reads, writes):
            if eng == "act":
                return S.op("act", lambda e: e.copy(out=out, in_=in_), reads, writes)
            return S.op(eng, lambda e: e.tensor_copy(out=out, in_=in_), reads, writes)

        def MM(out, lhsT, rhs, start, stop, reads, writes):
            return S.op("pe", lambda e: e.matmul(out, lhsT, rhs, start=start, stop=stop), reads, writes)

        def TR(out, in_, ident, reads, writes):
            return S.op("pe", lambda e: e.transpose(out, in_, ident), reads, writes)

        def kc(v, ch):
            return (getattr(v, "key", v), ch)

        def ck(n):
            if stop is not None and n == stop:
                raise StopBuild()

        def tap(name, src_ap, reads):
            if name in TAP:
                S.dma("sp", TAP[name], src_ap, reads=reads)

        identf = S.sb([128, 128], F32, "identf"); identb = S.sb([128, 128], BF16, "identb")
        S.dma("sp", identf[:], I["c_ident"], writes=[identf])
        S.dma("pool", identb[:], I["c_ident"], writes=[identb])
        PSB = [S.ps([128, 512], F32, "bank%d" % i) for i in range(7)]
        psTf = S.ps([128, 512], F32, "psTf")
        psT = View(psTf, lambda: psTf[:].bitcast(BF16))
        cs = S.sb([128, 8], F32, "cs")
        S.dma("sp", cs[:], I["cvec"], writes=[cs])
        ACT(cs[:], cs[:], AF.Silu, [cs], [cs])

        def ada_rows(es, w_ap, b_ap, n, g_ap):
            rows = [S.sb([128, D], F32, "row%d" % j) for j in range(n)]
            with contextlib.ExitStack() as es1:
                S.es = es1
                brow = S.sb([128, n * D], F32, "brow")
                csrep = S.sb([128, 8, 128], F32, "csrep")
                S.op("dve", lambda e: e.memset(csrep[:], 1.0), (), [csrep])
                TTo("dve", csrep[:], csrep[:], cs[:].unsqueeze(2).to_broadcast([128, 8, 128]), ALU.mult, [cs, csrep], [csrep])
                grow = S.sb([128, D], F32, "grow")
                wsl = [S.sb([128, 8, 512], F32, "wsl%d" % i) for i in range(2)]
                S.dma("sp", brow[:], b_ap.partition_broadcast(128), writes=[brow])
                S.dma("sp", grow[:], g_ap.partition_broadcast(128), writes=[grow])
                wv = w_ap.rearrange("(dc p) n -> p dc n", p=128)
                for ns in range(2 * n):
                    w = wsl[ns % 2]
                    S.dma("sp" if ns % 2 == 0 else "act", w[:], wv[:, :, ns * 512:(ns + 1) * 512], writes=[w])
                    pb = PSB[ns % 2]
                    for dc in range(8):
                        MM(pb[:], csrep[:, dc, :], w[:, dc, :], dc == 0, dc == 7, [csrep, w], [pb])
                    j, hf = ns // 2, ns % 2
                    TTo("dve", rows[j][:, hf * 512:(hf + 1) * 512], pb[:], brow[:, ns * 512:(ns + 1) * 512], ALU.add,
                        [pb, brow], [rows[j]])
                STT(rows[1][:], rows[1][:], 1.0, grow[:], ALU.add, ALU.mult, [rows[1], grow], [rows[1]])
                S.barrier()
            S.es = es
            return rows

        def norm_tile_g(xt, gs, sh, hb, tmpk, ss, vcol=None):
            class _T:
                def __getitem__(s_, k):
                    a = tmpk[:]
                    if len(a.shape) == 3:
                        a = a.rearrange("p a b -> p (a b)")
                    return a[k]
            tmp = _T()
            ACT(tmp[:], xt[:], AF.Square, [xt], [tmpk, ss], accum=ss[:, 0:1])
            yield
            ACT(ss[:, 1:2], ss[:, 0:1], AF.Sqrt, [ss], [ss], bias=1e-6, scale=1.0 / D)
            yield
            S.op("dve", lambda e: e.reciprocal(out=ss[:, 2:3], in_=ss[:, 1:2]), [ss], [ss])
            yield
            if vcol is not None:
                TTo("dve", ss[:, 2:3], ss[:, 2:3], vcol, ALU.mult, [ss], [ss])
                yield
            STT(tmp[:], xt[:], ss[:, 2:3], gs[:], ALU.mult, ALU.mult, [xt, ss, gs], [tmpk])
            yield
            if vcol is not None:
                STT(hb[:], sh[:], vcol, tmp[:], ALU.mult, ALU.add, [sh, tmpk], [hb])
                yield
            else:
                TTo("dve", hb[:], tmp[:], sh[:], ALU.add, [tmpk, sh], [hb])
                yield

        def norm_tile(xt, gs, sh, hb, tmpk, ss, vcol=None):
            for _ in norm_tile_g(xt, gs, sh, hb, tmpk, ss, vcol):
                pass

        def transpose8(src, dst_view, reads_extra=(), dstbuf=None, eng="act"):
            for dc in range(8):
                TR(psT[:, dc * 128:(dc + 1) * 128], src[:, dc * 128:(dc + 1) * 128], identb[:], [src, identb], [psT])
            CP(eng, dst_view, psT[:].rearrange("p (a b) -> p a b", b=128), [psT], [dstbuf])

        def load_w_bf16(buf_view, w_ap, bufkey, q="pool"):
            S.dma(q, buf_view, w_ap, writes=[bufkey])

        valid_sb = S.sb([128, NT], F32, "valid")
        S.dma("sp", valid_sb[:], I["valid"], writes=[valid_sb])

        if "B" in stages:
            with contextlib.ExitStack() as es:
                S.es = es
                sh0, gs0, gt0 = ada_rows(es, I["w_ada"][0], I["b_ada"][0], 3, I["norm_g"][0])
                wrkv = [S.sb([128, 8, D], BF16, "wrkv%d" % i) for i in range(3)]
                for i in range(3):
                    S.dma("pool", wrkv[i][:], I["w_rkv"][i].rearrange("(dc p) n -> p dc n", p=128), writes=[wrkv[i]])
                w1 = S.sb([128, 8, 64], BF16, "w1"); a1 = S.sb([128, 8, 64], BF16, "a1"); g1 = S.sb([128, 8, 160], BF16, "g1")
                S.dma("pool", w1[:], I["w1"].rearrange("(dc p) n -> p dc n", p=128), writes=[w1])
                S.dma("pool", a1[:], I["a1"].rearrange("(dc p) n -> p dc n", p=128), writes=[a1])
                S.dma("pool", g1[:], I["g1"].rearrange("(dc p) n -> p dc n", p=128), writes=[g1])
                w2 = S.sb([64, D], BF16, "w2"); a2 = S.sb([64, D], BF16, "a2")
                g2a = S.sb([128, D], BF16, "g2a"); g2b = S.sb([128, D], BF16, "g2b")
                S.op("pool", lambda e: e.memset(g2b[:], 0.0), (), [g2b])
                S.dma("pool", w2[:], I["w2"], writes=[w2]); S.dma("pool", a2[:], I["a2"], writes=[a2])
                S.dma("pool", g2a[:], I["g2"][0:128, :], writes=[g2a]); S.dma("pool", g2b[0:32, :], I["g2"][128:160, :], writes=[g2b])
                wo = S.sb([128, 8, D], BF16, "wo")
                S.dma("pool", wo[:], I["rw_wo"].rearrange("(dc p) n -> p dc n", p=128), writes=[wo])
                mu = S.sb([128, 6, 8], F32, "mu"); S.dma("sp", mu[:], I["mu"], writes=[mu])
                pp = {}
                for n in ("w0", "a0", "k_k", "k_a", "r_k"):
                    pp[n] = S.sb([128, 8], F32, n); S.dma("sp", pp[n][:], I[n], writes=[pp[n]])
                lnxg = S.sb([128, D], F32, "lnxg"); lnxb = S.sb([128, D], F32, "lnxb")
                S.dma("sp", lnxg[:], I["lnx_g"].partition_broadcast(128), writes=[lnxg])
                S.dma("sp", lnxb[:], I["lnx_b"].partition_broadcast(128), writes=[lnxb])
                cst = {}
                for n, shp in (("c_msu", [128, 64]), ("c_msl", [128, 64]), ("c_miu", [128, 64]),
                               ("c_id64", [128, 64]), ("c_bones", [128, 128])):
                    cst[n] = S.sb(shp, F32, n); S.dma("sp", cst[n][:], I[n], writes=[cst[n]])
                cst["c_rmask"] = S.sb([128, 1024], BF16, "c_rmask"); S.dma("pool", cst["c_rmask"][:], I["c_rmask"], writes=[cst["c_rmask"]])
                hsel = S.sb([128, 2], BF16, "hsel"); S.dma("pool", hsel[:], I["c_hsel"], writes=[hsel])
                H32 = S.sb([128, 8, 64], F32, "H32"); Hb = S.sb([128, 8, 64], BF16, "Hb")
                S.op("dve", lambda e: e.memset(H32[:], 0.0), (), [H32])
                S.op("dve", lambda e: e.memset(Hb[:], 0.0), (), [Hb])
                xt = [S.sb([128, D], F32, "xt0")] * 2
                ss = S.sb([128, 4], F32, "ss")
                hb = S.sb([128, D], BF16, "hb")
                hT = [S.sb([128, 8, 129], BF16, "hT%d" % i) for i in range(2)]
                S.op("dve", lambda e: e.memset(hT[1][:], 0.0), (), [hT[1]])
                dx = S.sb([128, 8, 128], BF16, "dx")
                xs = [S.sb([128, 8, 128], BF16, "xs%d" % i) for i in range(6)]
                f = {n: S.sb([128, 8, 128], F32, "f_" + n) for n in
                     ("sigw", "lp", "P", "alr", "k", "r", "kk", "km", "bA", "tA", "tB")}
                for n in ("iP", "PC"):
                    f[n] = S.sb([128, 8, 128], BF16, "f_" + n)
                class _V:
                    def __init__(s_, b, pat, **kw):
                        s_.b, s_.pat, s_.kw = b, pat, kw
                    def __getitem__(s_, k):
                        return s_.b[:].rearrange(s_.pat, **s_.kw)[k]
                tmp_b = f["tA"]
                th = S.sb([64, 128], BF16, "th"); al = S.sb([64, 128], BF16, "al"); sg = S.sb([128, 2, 128], BF16, "sg")
                S.op("pool", lambda e: e.memset(sg[:], 0.0), (), [sg])
                v_sb = S.sb([128, D], BF16, "v_sb"); g_sb = S.sb([128, D], BF16, "g_sb")
                ARt = S.sb([128, 8, 2, 2, 64], BF16, "ARt"); BKt = S.sb([128, 8, 2, 2, 64], BF16, "BKt")
                BKh = S.sb([128, 8, 2, 128], BF16, "BKh")
                prod = S.sb([128, 8, 128], BF16, "prod"); rk_sb = S.sb([128, 16], F32, "rk_sb")
                def hview(b):
                    return View(b, lambda: b[:].rearrange("p a b -> p (a b)")[:, 0:512].rearrange("p (a b) -> p a b", b=64))
                A_sb = [hview(f["alr"]), hview(f["k"])]
                N_sb = [hview(f["kk"]), hview(f["km"])]
                T_sb = [hview(f["bA"]), hview(f["tB"])]
                Aak = S.sb([128, 8, 64], BF16, "Aak"); Qrb = S.sb([128, 8, 64], BF16, "Qrb"); Qrk = S.sb([128, 8, 64], BF16, "Qrk")
                BKhT = S.sb([128, 8, 2, 64], BF16, "BKhT")
                G_sb = hview(f["r"]); X0 = S.sb([128, 8, 64], F32, "X0"); U_sb = S.sb([128, 8, 64], BF16, "U_sb")
                st = S.sb([128, 4, 16], F32, "st")
                c0, c1, c2 = PSB[4], PSB[5], PSB[6]
                fm = [PSB[0], PSB[1]]
                tm = [PSB[2], PSB[3]]

                def bc8(t, n=128):
                    return t[:].unsqueeze(2).to_broadcast([128, 8, n])

                def m8(mk):
                    return cst[mk][:].unsqueeze(1).to_broadcast([128, 8, 64])

                def fl(b):
                    return b[:].rearrange("p a b -> p (a b)")

                def F1g(tt):
                    xcur = xt[tt % 2]
                    hTc, hTp = hT[tt % 2], hT[(tt + 1) % 2]
                    S.dma("sp", xcur[:], I["xv"][tt * 128:(tt + 1) * 128, :], writes=[xcur])
                    yield
                    for _ in norm_tile_g(xcur, gs0, sh0, hb, f["tA"], ss, vcol=valid_sb[:, tt:tt + 1]):
                        yield
                    CP("pool", hTc[:, :, 0:1], hTp[:, :, 128:129], [hTp], [hTc])
                    yield
                    transpose8(hb, hTc[:, :, 1:129], dstbuf=hTc)
                    yield
                    TTo("dve", dx[:], hTc[:, :, 0:128], hTc[:, :, 1:129], ALU.subtract, [hTc], [dx])
                    yield
                    for i in range(6):
                        eng = "dve" if i < 3 else "pool"
                        TTo(eng, xs[i][:], dx[:], mu[:, i, :].unsqueeze(2).to_broadcast([128, 8, 128]), ALU.mult, [dx, mu], [xs[i]])
                        yield
                        TTo(eng, xs[i][:], xs[i][:], hTc[:, :, 1:129], ALU.add, [xs[i], hTc], [xs[i]])
                        yield

                def F1(tt):
                    for _ in F1g(tt):
                        pass
                pend = [None]

                def drip(n=1):
                    g_ = pend[0]
                    if g_ is None:
                        return
                    for _ in range(n):
                        try:
                            next(g_)
                        except StopIteration:
                            pend[0] = None
                            return

                def F2a():
                    for qi, dst in ((0, f["r"]), (1, f["k"])):
                        for grp in range(2):
                            pb = fm[grp]
                            for pl in range(4):
                                pr = grp * 4 + pl
                                for dc in range(8):
                                    MM(pb[:, pl * 128:(pl + 1) * 128], wrkv[qi][:, dc, pr * 128:(pr + 1) * 128], xs[qi][:, dc, :],
                                       dc == 0, dc == 7, [wrkv[qi], xs[qi]], [pb])
                            CP("act", fl(dst)[:, grp * 512:(grp + 1) * 512], pb[:], [pb], [(dst, grp)])

                for tt in range(NT):
                  try:
                    if tt == 0:
                        F1(0)
                    ck(2)
                    if tt == 0:
                        F2a()
                    ck(3)
                    for hf in range(2):
                        for dc in range(8):
                            MM(tm[hf][:], xs[2][:, dc, :], wrkv[2][:, dc, hf * 512:(hf + 1) * 512], dc == 0, dc == 7, [xs[2], wrkv[2]], [tm[hf]])
                        CP("act", v_sb[:, hf * 512:(hf + 1) * 512], tm[hf][:], [tm[hf]], [v_sb])
                    ck(4)
                    for dc in range(8):
                        MM(c0[0:64, 0:128], w1[:, dc, :], xs[3][:, dc, :], dc == 0, dc == 7, [w1, xs[3]], [c0])
                    ACT(th[:], c0[0:64, 0:128], AF.Tanh, [c0], [th])
                    for dc in range(8):
                        MM(c1[0:64, 0:128], a1[:, dc, :], xs[4][:, dc, :], dc == 0, dc == 7, [a1, xs[4]], [c1])
                    CP("act", al[:], c1[0:64, 0:128], [c1], [al])
                    for (src, wgt, bias, dst) in ((th, w2, pp["w0"], f["sigw"]), (al, a2, pp["a0"], f["alr"])):
                        for grp in range(2):
                            pb = fm[grp]
                            for pl in range(4):
                                pr = grp * 4 + pl
                                MM(pb[:, pl * 128:(pl + 1) * 128], wgt[:, pr * 128:(pr + 1) * 128], src[:], True, True, [wgt, src], [pb])
                            for pl in range(4):
                                pr = grp * 4 + pl
                                ACT(dst[:, pr, :], pb[:, pl * 128:(pl + 1) * 128], AF.Sigmoid, [pb, bias], [(dst, pr)], bias=bias[:, pr:pr + 1])
                    ck(5)
                    for dc in range(8):
                        MM(c0[:, 0:128], g1[:, dc, 0:128], xs[5][:, dc, :], dc == 0, dc == 7, [g1, xs[5]], [c0])
                    for dc in range(8):
                        MM(c1[0:32, 0:128], g1[:, dc, 128:160], xs[5][:, dc, :], dc == 0, dc == 7, [g1, xs[5]], [c1])
                    ACT(sg[:, 0, :], c0[:, 0:128], AF.Sigmoid, [c0], [sg])
                    ACT(sg[0:32, 1, :], c1[0:32, 0:128], AF.Sigmoid, [c1], [sg])
                    for hf in range(2):
                        MM(tm[hf][:], sg[:, 0, :], g2a[:, hf * 512:(hf + 1) * 512], True, False, [sg, g2a], [tm[hf]])
                        MM(tm[hf][:], sg[:, 1, :], g2b[:, hf * 512:(hf + 1) * 512], False, True, [sg, g2b], [tm[hf]])
                        CP("act", g_sb[:, hf * 512:(hf + 1) * 512], tm[hf][:], [tm[hf]], [g_sb])
                    ck(6)
                    S.op("dve", lambda e: e.tensor_tensor_scan(out=fl(f["lp"]), data0=cst["c_rmask"][:], data1=fl(f["sigw"]),
                                                               initial=0.0, op0=ALU.mult, op1=ALU.add),
                         [cst["c_rmask"], f["sigw"]], [f["lp"]])
                    ACT(fl(f["P"]), fl(f["lp"]), AF.Exp, [f["lp"]], [f["P"]], scale=-CDEC)
                    ACT(fl(f["iP"]), fl(f["lp"]), AF.Exp, [f["lp"]], [f["iP"]], scale=CDEC)
                    lpv = fl(f["lp"]).rearrange("p (a t) -> p a t", t=64)
                    TTo("dve", fl(f["tB"]).rearrange("p (a t) -> p a t", t=64), lpv[:, :, 63:64].to_broadcast([128, 16, 64]), lpv,
                        ALU.subtract, [f["lp"]], [f["tB"]])
                    ACT(fl(f["PC"]), fl(f["tB"]), AF.Exp, [f["tB"]], [f["PC"]], scale=-CDEC)
                    ck(7)
                    TTo("dve", f["kk"][:], f["k"][:], bc8(pp["k_k"]), ALU.mult, [f["k"], pp["k_k"]], [f["kk"]])
                    ACT(fl(f["tA"]), fl(f["kk"]), AF.Square, [f["kk"]], [f["tA"]])
                    for grp in range(2):
                        MM(fm[grp][:], cst["c_bones"][:], fl(f["tA"])[:, grp * 512:(grp + 1) * 512], True, True, [cst["c_bones"], f["tA"]], [fm[grp]])
                        TS("dve", fl(f["tB"])[:, grp * 512:(grp + 1) * 512], fm[grp][:], 1e-24, None, ALU.max, None, [fm[grp]], [f["tB"]])
                    ACT(fl(f["tB"]), fl(f["tB"]), AF.Ln, [f["tB"]], [f["tB"]])
                    ACT(fl(f["tB"]), fl(f["tB"]), AF.Exp, [f["tB"]], [f["tB"]], scale=-0.5)
                    TTo("dve", f["kk"][:], f["kk"][:], f["tB"][:], ALU.mult, [f["kk"], f["tB"]], [f["kk"]])
                    ck(8)
                    STT(f["tA"][:], f["alr"][:], -1.0, bc8(pp["k_a"]), ALU.add, ALU.mult, [f["alr"], pp["k_a"]], [f["tA"]])
                    STT(f["km"][:], f["tA"][:], 1.0, f["k"][:], ALU.add, ALU.mult, [f["tA"], f["k"]], [f["km"]])
                    TTo("dve", f["bA"][:], f["kk"][:], f["alr"][:], ALU.mult, [f["kk"], f["alr"]], [f["bA"]])

                    def v4(b):
                        return b[:].rearrange("p a (c t) -> p a c t", t=64)
                    TTo("dve", ARt[:, :, :, 1, :], v4(f["r"]), v4(f["P"]), ALU.mult, [f["r"], f["P"]], [ARt])
                    A16 = ARt[:, :, :, 0, :].rearrange("p a c t -> p (a c) t")
                    kk16 = fl(f["kk"]).rearrange("p (a t) -> p a t", t=64); P16 = fl(f["P"]).rearrange("p (a t) -> p a t", t=64)
                    STT(A16[:, :, 1:64], kk16[:, :, 1:64], -1.0, P16[:, :, 0:63], ALU.mult, ALU.mult, [f["kk"], f["P"]], [ARt])
                    TS("dve", A16[:, :, 0:1], kk16[:, :, 0:1], -1.0, None, ALU.mult, None, [f["kk"]], [ARt])
                    TTo("pool", BKt[:, :, :, 1, :], v4(f["km"]), v4(f["iP"]), ALU.mult, [f["km"], f["iP"]], [BKt])
                    TTo("dve", BKt[:, :, :, 0, :], v4(f["bA"]), v4(f["iP"]), ALU.mult, [f["bA"], f["iP"]], [BKt])
                    TTo("dve", BKh[:, :, 1, :], f["km"][:], f["PC"][:], ALU.mult, [f["km"], f["PC"]], [BKh])
                    TTo("pool", BKh[:, :, 0, :], f["bA"][:], f["PC"][:], ALU.mult, [f["bA"], f["PC"]], [BKh])
                    ck(9)
                    TTo("pool", f["tA"][:], f["r"][:], f["km"][:], ALU.mult, [f["r"], f["km"]], [f["tA"]])
                    TTo("pool", prod[:], f["tA"][:], bc8(pp["r_k"]), ALU.mult, [f["tA"], pp["r_k"]], [prod])
                    for pr in range(8):
                        MM(c2[:, pr * 2:pr * 2 + 2], prod[:, pr, :], hsel[:], True, True, [prod, hsel], [c2])
                    CP("act", rk_sb[:], c2[:, 0:16], [c2], [rk_sb])
                    ck(10)
                    def bv(bk):
                        return bk[:].rearrange("p (a b) -> p a b", b=64)
                    B = PSB
                    Pv = f["P"][:].rearrange("p a (c t) -> p a c t", t=64)
                    for g in range(2):
                        hp = 64 * g
                        if g == 0 and tt + 1 < NT:
                            pend[0] = F1g(tt + 1)
                        def b4(bk):
                            return bk[:].rearrange("p (a w t) -> p a w t", w=2, t=64)
                        for hl in range(8):
                            bx = B[0 + hl // 4]; by = B[2 + hl // 4]
                            for ch in range(2):
                                tp = 64 * ch
                                bt = BKt[hp:hp + 64, hl, ch, 0, :]; kt = BKt[hp:hp + 64, hl, ch, 1, :]
                                at = ARt[hp:hp + 64, hl, ch, 0, :]; art = ARt[hp:hp + 64, hl, ch, :, :]
                                MM(b4(bx)[tp:tp + 64, hl % 4, :, :], bt, art, True, True, [BKt, ARt], [bx])
                                MM(b4(by)[tp:tp + 64, hl % 4, :, :], kt, art, True, True, [BKt, ARt], [by])
                                MM(bv(B[4])[tp:tp + 64, hl, :], at, bt, True, True, [BKt, ARt], [B[4]])
                        TTo("dve", A_sb[0][:], bv(B[4]), m8("c_msl"), ALU.mult, [B[4], cst["c_msl"]], [A_sb[0]])
                        m4u = cst["c_msu"][:].unsqueeze(1).to_broadcast([128, 4, 64]); m4i = cst["c_miu"][:].unsqueeze(1).to_broadcast([128, 4, 64])
                        for hb2 in range(2):
                            hs = slice(hb2 * 4, hb2 * 4 + 4)
                            TTo("dve", N_sb[0][:, hs, :], b4(B[0 + hb2])[:, :, 0, :], m4u, ALU.mult, [B[0 + hb2], cst["c_msu"]], [(N_sb[0].key, "h%d" % hb2)])
                            TTo("dve", Aak[:, hs, :], b4(B[2 + hb2])[:, :, 0, :], m4u, ALU.mult, [B[2 + hb2], cst["c_msu"]], [(Aak, hb2)])
                        drip(1)
                        for hb2 in range(2):
                            hs = slice(hb2 * 4, hb2 * 4 + 4)
                            TTo("dve", Qrb[:, hs, :], b4(B[0 + hb2])[:, :, 1, :], m4i, ALU.mult, [B[0 + hb2], cst["c_miu"]], [(Qrb, hb2)])
                            TTo("dve", Qrk[:, hs, :], b4(B[2 + hb2])[:, :, 1, :], m4i, ALU.mult, [B[2 + hb2], cst["c_miu"]], [(Qrk, hb2)])
                        drip(1)
                        ck(12)
                        drip(1)
                        psTv = psT[:].rearrange("p (a w j) -> p a w j", w=2, j=64)
                        for hl in range(8):
                            for w in range(2):
                                TR(psTv[:, hl, w, :], BKh[hp:hp + 64, hl, w, :], identb[hp:hp + 64, hp:hp + 64], [BKh, identb], [psT])
                        CP("act", BKhT[:], psTv, [psT], [BKhT])
                        for ch in range(2):
                            tp = 64 * ch
                            for hl in range(8):
                                h = 2 * hl + g
                                MM(bv(B[ch])[tp:tp + 64, hl, :], Aak[tp:tp + 64, hl, :], v_sb[tp:tp + 64, h * 64:(h + 1) * 64], True, True, [Aak, v_sb], [B[ch]])
                            CP("act", X0[tp:tp + 64, :, :], bv(B[ch])[tp:tp + 64, :, :], [B[ch]], [kc(X0, ch)])
                        ck(13)
                        drip(1)
                        TTo("dve", T_sb[0][:], N_sb[0][:], m8("c_id64"), ALU.add, [N_sb[0], cst["c_id64"]], [T_sb[0]])
                        cur = 0
                        for lvl in range(5):
                            nxt = 1 - cur
                            for ch in range(2):
                                tp = 64 * ch
                                for hl in range(8):
                                    MM(bv(B[ch])[tp:tp + 64, hl, :], N_sb[cur][tp:tp + 64, hl, :], A_sb[cur][tp:tp + 64, hl, :], True, True,
                                       [kc(N_sb[cur], ch), kc(A_sb[cur], ch)], [B[ch]])
                                    if lvl < 4:
                                        MM(bv(B[2 + ch])[tp:tp + 64, hl, :], A_sb[cur][tp:tp + 64, hl, :], N_sb[cur][tp:tp + 64, hl, :], True, True,
                                           [kc(N_sb[cur], ch), kc(A_sb[cur], ch)], [B[2 + ch]])
                            for ch in range(2):
                                tp = 64 * ch
                                CP("act", A_sb[nxt][tp:tp + 64, :, :], bv(B[ch])[tp:tp + 64, :, :], [B[ch]], [kc(A_sb[nxt], ch)])
                                if lvl < 4:
                                    CP("act", N_sb[nxt][tp:tp + 64, :, :], bv(B[2 + ch])[tp:tp + 64, :, :], [B[2 + ch]], [kc(N_sb[nxt], ch)])
                            drip(1)
                            for ch in range(2):
                                tp = 64 * ch
                                for hl in range(8):
                                    MM(bv(B[4 + ch])[tp:tp + 64, hl, :], A_sb[nxt][tp:tp + 64, hl, :], T_sb[cur][tp:tp + 64, hl, :], True, True,
                                       [kc(A_sb[nxt], ch), kc(T_sb[cur], ch)], [B[4 + ch]])
                            for ch in range(2):
                                tp = 64 * ch
                                TTo("dve", T_sb[nxt][tp:tp + 64, :, :], bv(B[4 + ch])[tp:tp + 64, :, :], T_sb[cur][tp:tp + 64, :, :], ALU.add,
                                    [B[4 + ch], kc(T_sb[cur], ch)], [kc(T_sb[nxt], ch)])
                            cur = nxt
                            drip(1)
                        TTf = T_sb[cur]
                        ck(14)
                        GX, UX, HX, Y1 = B[0], B[1], B[2], B[3]
                        for ch in range(2):
                            tp = 64 * ch
                            Y2 = B[4 + ch]
                            for hl in range(8):
                                MM(bv(GX)[tp:tp + 64, hl, :], ARt[hp:hp + 64, hl, ch, 0, :], Hb[hp:hp + 64, hl, :], True, True, [ARt, Hb], [GX])
                            TTo("dve", G_sb[tp:tp + 64, :, :], bv(GX)[tp:tp + 64, :, :], X0[tp:tp + 64, :, :], ALU.add, [GX, kc(X0, ch)], [kc(G_sb, ch)])
                            for hl in range(8):
                                MM(bv(UX)[tp:tp + 64, hl, :], TTf[tp:tp + 64, hl, :], G_sb[tp:tp + 64, hl, :], True, True, [kc(TTf, ch), kc(G_sb, ch)], [UX])
                            CP("act", U_sb[tp:tp + 64, :, :], bv(UX)[tp:tp + 64, :, :], [UX], [kc(U_sb, ch)])
                            drip(1)
                            for hl in range(8):
                                MM(bv(Y1)[tp:tp + 64, hl, :], ARt[hp:hp + 64, hl, ch, 1, :], Hb[hp:hp + 64, hl, :], True, True, [ARt, Hb], [Y1])
                            for hl in range(8):
                                h = 2 * hl + g
                                MM(bv(Y2)[tp:tp + 64, hl, :], Qrb[tp:tp + 64, hl, :], U_sb[tp:tp + 64, hl, :], True, False, [Qrb, kc(U_sb, ch)], [Y2])
                                MM(bv(Y2)[tp:tp + 64, hl, :], Qrk[tp:tp + 64, hl, :], v_sb[tp:tp + 64, h * 64:(h + 1) * 64], False, True, [Qrk, v_sb], [Y2])
                            for hl in range(8):
                                h = 2 * hl + g
                                MM(bv(HX)[hp:hp + 64, hl, :], BKhT[tp:tp + 64, hl, 0, :], U_sb[tp:tp + 64, hl, :], True, False, [BKhT, kc(U_sb, ch)], [HX])
                                MM(bv(HX)[hp:hp + 64, hl, :], BKhT[tp:tp + 64, hl, 1, :], v_sb[tp:tp + 64, h * 64:(h + 1) * 64], False, True, [BKhT, v_sb], [HX])
                            Hg = H32[hp:hp + 64, :, :]
                            TTo("dve", Hg, Hg, Pv[hp:hp + 64, :, ch, 63:64].to_broadcast([64, 8, 64]), ALU.mult, [H32, f["P"]], [H32])
                            TTo("dve", Hg, Hg, bv(HX)[hp:hp + 64, :, :], ALU.add, [H32, HX], [H32])
                            CP("act", Hb[hp:hp + 64, :, :], Hg, [H32], [Hb])
                        Yg = fl(f["sigw"]).rearrange("p (a g b) -> p a g b", g=2, b=64)[:, :, g, :]
                        CP("act", Yg, bv(Y1), [Y1], [f["sigw"]])
                        for ch in range(2):
                            tp = 64 * ch
                            TTo("dve", Yg[tp:tp + 64], Yg[tp:tp + 64], bv(B[4 + ch])[tp:tp + 64, :, :], ALU.add, [f["sigw"], B[4 + ch]], [f["sigw"]])
                    ck(15)
                    drip(1000)
                    if tt + 1 < NT:
                        F2a()
                    YK, TK, XK = f["sigw"], f["tA"], f["lp"]
                    Yf = fl(YK); Y3 = Yf.rearrange("p (a b) -> p a b", b=64)
                    tmpf = fl(TK); tmp3 = tmpf.rearrange("p (a b) -> p a b", b=64)
                    x1f = fl(XK)
                    S.op("dve", lambda e: e.tensor_reduce(out=st[:, 0, :], in_=Y3, axis=AX.X, op=ALU.add), [YK], [st])
                    ACT(tmpf, Yf, AF.Square, [YK], [TK])
                    S.op("dve", lambda e: e.tensor_reduce(out=st[:, 1, :], in_=tmp3, axis=AX.X, op=ALU.add), [TK], [st])
                    TS("dve", st[:, 0, :], st[:, 0, :], 1.0 / 64, None, ALU.mult, None, [st], [st])
                    TTo("dve", st[:, 2, :], st[:, 0, :], st[:, 0, :], ALU.mult, [st], [st])
                    STT(st[:, 1, :], st[:, 1, :], 1.0 / 64, st[:, 2, :], ALU.mult, ALU.subtract, [st], [st])
                    ACT(st[:, 1, :], st[:, 1, :], AF.Sqrt, [st], [st], bias=64e-5, scale=1.0)
                    S.op("dve", lambda e: e.reciprocal(out=st[:, 1, :], in_=st[:, 1, :]), [st], [st])
                    TTo("dve", Y3, Y3, st[:, 0, :].unsqueeze(2).to_broadcast([128, 16, 64]), ALU.subtract, [YK, st], [YK])
                    TTo("dve", Y3, Y3, st[:, 1, :].unsqueeze(2).to_broadcast([128, 16, 64]), ALU.mult, [YK, st], [YK])
                    TTo("pool", Yf, Yf, lnxg[:], ALU.mult, [YK, lnxg], [YK])
                    TTo("pool", Yf, Yf, lnxb[:], ALU.add, [YK, lnxb], [YK])
                    TTo("dve", tmp3, v_sb[:].rearrange("p (a b) -> p a b", b=64),
                        rk_sb[:].unsqueeze(2).to_broadcast([128, 16, 64]), ALU.mult, [v_sb, rk_sb], [TK])
                    TTo("dve", tmpf, tmpf, Yf, ALU.add, [TK, YK], [TK])
                    zb = View(prod, lambda: prod[:].rearrange("p a b -> p (a b)")); zT = View(BKh, lambda: BKh[:, :, 0, :])
                    TTo("dve", zb[:], tmpf, g_sb[:], ALU.mult, [TK, g_sb], [zb])
                    transpose8(zb, zT[:], dstbuf=zT)
                    for hf in range(2):
                        for dc in range(8):
                            MM(tm[hf][:], zT[:, dc, :], wo[:, dc, hf * 512:(hf + 1) * 512], dc == 0, dc == 7, [zT, wo], [tm[hf]])
                        TTo("dve", x1f[:, hf * 512:(hf + 1) * 512], tm[hf][:], gt0[:, hf * 512:(hf + 1) * 512], ALU.mult, [tm[hf], gt0], [XK])
                    S.dma("sp", fl(f["P"]), I["xv"][tt * 128:(tt + 1) * 128, :], writes=[f["P"]])
                    TTo("pool", x1f, x1f, fl(f["P"]), ALU.add, [XK, f["P"]], [XK])
                    S.dma("sp", XS[tt * 128:(tt + 1) * 128, :], x1f, reads=[XK], writes=[("XS", tt)])
                    if tt == NT - 1:
                        tap("x1_last", x1f, [XK])
                  except StopBuild:
                    break
                S.barrier()
            S.es = es0

        def ffn_stage(L, tile0, tile1, to_out):
            with contextlib.ExitStack() as es:
                S.es = es
                sh, gs, gt = ada_rows(es, I["w_ada"][2 * L + 1], I["b_ada"][2 * L + 1], 3, I["norm_g"][2 * L + 1])
                wg = S.sb([128, 8, FF], BF16, "wg"); wu = S.sb([128, 8, FF], BF16, "wu"); wd = S.sb([128, NFC, D], BF16, "wd")
                gv = I["ffn_g"][L].rearrange("(dc p) n -> p dc n", p=128); uv = I["ffn_u"][L].rearrange("(dc p) n -> p dc n", p=128)
                dv = I["ffn_d"][L].rearrange("(fc p) n -> p fc n", p=128)
                for dc in range(8):
                    S.dma("pool", wg[:, dc, :], gv[:, dc, :], writes=[wg]); S.dma("pool", wu[:, dc, :], uv[:, dc, :], writes=[wu])
                for fc in range(0, NFC, 2):
                    S.dma("pool", wd[:, fc:fc + 2, :], dv[:, fc:fc + 2, :], writes=[wd])
                NS = 4
                TW = NS * 128
                xt = S.sb([128, NS, D], F32, "fx"); tmpk = S.sb([128, D], F32, "ftmp"); ss = S.sb([128, 4], F32, "fss")
                hb = S.sb([128, D], BF16, "fhb"); hT = S.sb([128, 8, TW], BF16, "fhT")
                act = S.sb([128, NFC, TW], BF16, "fact"); sls = [S.sb([128, TW], BF16, "fsl%d" % i) for i in range(2)]
                x2 = S.sb([128, D], F32, "fx2"); xr = tmpk
                nt4 = (tile1 - tile0) // NS

                def front(t4):
                    for sub in range(NS):
                        tt = tile0 + t4 * NS + sub
                        xs_v = View((xt, sub), lambda sub=sub: xt[:, sub, :])
                        S.dma("sp", xt[:, sub, :], XS[tt * 128:(tt + 1) * 128, :], reads=[("XS", tt)], writes=[(xt, sub)])
                        norm_tile(xs_v, gs, sh, hb, tmpk, ss)
                        for dc in range(8):
                            TR(psT[:, dc * 128:(dc + 1) * 128], hb[:, dc * 128:(dc + 1) * 128], identb[:], [hb, identb], [psT])
                        CP("act", hT[:, :, sub * 128:(sub + 1) * 128], psT[:].rearrange("p (a b) -> p a b", b=128), [psT], [(hT, sub)])

                def back(t4):
                    for fc in range(NFC):
                        pg = PSB[fc % 2]; pu = PSB[2 + fc % 2]
                        for dc in range(8):
                            MM(pg[:], wg[:, dc, fc * 128:(fc + 1) * 128], hT[:, dc, :], dc == 0, dc == 7, [wg, hT], [pg])
                        for dc in range(8):
                            MM(pu[:], wu[:, dc, fc * 128:(fc + 1) * 128], hT[:, dc, :], dc == 0, dc == 7, [wu, hT], [pu])
                        sl = sls[fc % 2]
                        ACT(sl[:], pg[:], AF.Silu, [pg], [sl])
                        TTo("dve", act[:, fc, :], pu[:], sl[:], ALU.mult, [pu, sl], [(act, fc)])
                    if t4 + 1 < nt4:
                        front(t4 + 1)
                    for sub in range(NS):
                        tt = tile0 + t4 * NS + sub
                        S.dma("act", xr[:], XS[tt * 128:(tt + 1) * 128, :], reads=[("XS", tt)], writes=[xr])
                        for hf in range(2):
                            po = PSB[4 + hf]
                            for fc in range(NFC):
                                MM(po[:], act[:, fc, sub * 128:(sub + 1) * 128], wd[:, fc, hf * 512:(hf + 1) * 512], fc == 0, fc == NFC - 1, [(act, fc), wd], [po])
                            TTo("dve", x2[:, hf * 512:(hf + 1) * 512], po[:], gt[:, hf * 512:(hf + 1) * 512], ALU.mult, [po, gt], [x2])
                        TTo("pool", x2[:], x2[:], xr[:], ALU.add, [x2, xr], [x2])
                        if to_out:
                            S.dma("sp", out[(tt - tile0) * 128:(tt - tile0 + 1) * 128, :], x2[:], reads=[x2], writes=[("OUT", tt)])
                        else:
                            S.dma("sp", XS[tt * 128:(tt + 1) * 128, :], x2[:], reads=[x2], writes=[("XS", tt)])
                        if tt == tile1 - 1:
                            tap("ffn%d_last" % L, x2[:], [x2])
                front(0)
                for t4 in range(nt4):
                    back(t4)
                S.barrier()
            S.es = es0

        if "C" in stages:
            ffn_stage(0, 0, NT, False)

        VS2 = nc.dram_tensor("v2_scr", [8, T, 130], BF16, kind="Internal").ap()
        kmean = S.sb([128, 8, NB], BF16, "kmean")
        hnorm_tmp = {}

        def head_rmsnorm(src_ps_halves, grow, dst_f32_key, dst_ap, sq_key, sq_ap, st, extra_scale=1.0):
            for hf in range(2):
                ACT(sq_ap[:, hf * 512:(hf + 1) * 512], src_ps_halves[hf][:], AF.Square, [src_ps_halves[hf]], [sq_key])
            S.op("dve", lambda e: e.tensor_reduce(out=st[:, 0, :], in_=sq_ap.rearrange("p (a b) -> p a b", b=64), axis=AX.X, op=ALU.add), [sq_key], [st])
            ACT(st[:, 1, :], st[:, 0, :], AF.Sqrt, [st], [st], bias=1e-6, scale=1.0 / 64)
            S.op("dve", lambda e: e.reciprocal(out=st[:, 1, :], in_=st[:, 1, :]), [st], [st])
            if extra_scale != 1.0:
                TS("dve", st[:, 1, :], st[:, 1, :], extra_scale, None, ALU.mult, None, [st], [st])
            for hf in range(2):
                TTo("dve", dst_ap[:, hf * 512:(hf + 1) * 512].rearrange("p (a b) -> p a b", b=64),
                    src_ps_halves[hf][:].rearrange("p (a b) -> p a b", b=64),
                    st[:, 1, hf * 8:(hf + 1) * 8].unsqueeze(2).to_broadcast([128, 8, 64]), ALU.mult, [src_ps_halves[hf], st], [dst_f32_key])
            TTo("pool", dst_ap, dst_ap, grow[:], ALU.mult, [dst_f32_key, grow], [dst_f32_key])

        if "D" in stages:
            with contextlib.ExitStack() as es:
                S.es = es
                shk, gsk = ada_rows(es, I["kv_w_ada"], I["kv_b_ada"], 2, I["kv_norm_g"])
                wk = S.sb([128, 8, D], BF16, "wk"); wv = S.sb([128, 8, D], BF16, "wv")
                S.dma("pool", wk[:], I["kv_wk"].rearrange("(dc p) n -> p dc n", p=128), writes=[wk])
                S.dma("pool", wv[:], I["kv_wv"].rearrange("(dc p) n -> p dc n", p=128), writes=[wv])
                kng = S.sb([128, D], F32, "kng"); S.dma("sp", kng[:], I["k_norm_g"].partition_broadcast(128), writes=[kng])
                onesc = S.sb([128, 1], F32, "onesc"); S.op("dve", lambda e: e.memset(onesc[:], 1.0 / 256), (), [onesc])
                xts = [S.sb([128, D], F32, "dx_%d" % i) for i in range(2)]; tmpk = S.sb([128, D], F32, "dtmp"); ss = S.sb([128, 4], F32, "dss")
                hb = S.sb([128, D], BF16, "dhb"); hTs = [S.sb([128, 8, 128], BF16, "dhT%d" % i) for i in range(2)]
                kf = S.sb([128, D], F32, "kf"); kb = S.sb([128, D], BF16, "kb"); sq = S.sb([128, D], F32, "ksq")
                st = S.sb([128, 2, 16], F32, "kst"); kTs = [S.sb([128, 8, 128], BF16, "kT%d" % i) for i in range(2)]
                vexts = [S.sb([128, 16, 65], BF16, "vext%d" % i) for i in range(2)]
                for v_ in vexts:
                    S.op("dve", lambda e: e.memset(v_[:], 1.0), (), [v_])
                kmacc = S.sb([128, 8], F32, "kmacc")

                def dfront(tt):
                    xt = xts[tt % 2]; hT = hTs[tt % 2]
                    S.dma("sp", xt[:], XS[tt * 128:(tt + 1) * 128, :], reads=[("XS", tt)], writes=[xt])
                    norm_tile(xt, gsk, shk, hb, tmpk, ss)
                    transpose8(hb, hT[:], dstbuf=hT)

                kfs = [kf, S.sb([128, D], F32, "kf2")]; kbs = [kb, S.sb([128, D], BF16, "kb2")]

                def dmm(tt):
                    hT = hTs[tt % 2]; vext = vexts[tt % 2]
                    KB = [PSB[0], PSB[1]] if tt % 2 == 0 else [PSB[5], PSB[6]]
                    for hf in range(2):
                        for dc in range(8):
                            MM(KB[hf][:], hT[:, dc, :], wk[:, dc, hf * 512:(hf + 1) * 512], dc == 0, dc == 7, [hT, wk], [KB[hf]])
                    for hf in range(2):
                        for dc in range(8):
                            MM(PSB[2 + hf][:], hT[:, dc, :], wv[:, dc, hf * 512:(hf + 1) * 512], dc == 0, dc == 7, [hT, wv], [PSB[2 + hf]])
                        CP("act", vext[:, hf * 8:(hf + 1) * 8, 0:64], PSB[2 + hf][:].rearrange("p (a b) -> p a b", b=64), [PSB[2 + hf]], [vext])

                def dhead(tt):
                    KB = [PSB[0], PSB[1]] if tt % 2 == 0 else [PSB[5], PSB[6]]
                    kf_ = kfs[tt % 2]; kb_ = kbs[tt % 2]
                    head_rmsnorm(KB, kng, kf_, kf_[:], sq, sq[:], st)
                    CP("act", kb_[:], kf_[:], [kf_], [kb_])

                def dpost(tt):
                    kT = kTs[tt % 2]; vext = vexts[tt % 2]; kf_ = kfs[tt % 2]; kb_ = kbs[tt % 2]
                    for pr in range(8):
                        MM(PSB[4][:, pr:pr + 1], kf_[:, pr * 128:(pr + 1) * 128], onesc[:], True, True, [kf_, onesc], [PSB[4]])
                    if tt % 2 == 0:
                        CP("dve", kmacc[:], PSB[4][:, 0:8], [PSB[4]], [kmacc])
                    else:
                        TTo("dve", kmean[:, :, tt // 2], PSB[4][:, 0:8], kmacc[:], ALU.add, [PSB[4], kmacc], [kmean])
                    transpose8(kb_, kT[:], dstbuf=kT)
                    S.dma("sp", KT[:, :, tt * 128:(tt + 1) * 128].rearrange("a p t -> p a t"), kT[:], reads=[kT], writes=[("KT", tt)])
                    S.dma("sp", VS2[:, tt * 128:(tt + 1) * 128, :].rearrange("a p c -> p a c"),
                          vext[:].rearrange("p (a g) c -> p a (g c)", g=2), reads=[vext], writes=[("VS2", tt)])
                    if tt == NT - 1:
                        tap("kf_last", kf_[:], [kf_])
                dfront(0)
                for tt in range(NT):
                    dmm(tt)
                    if tt + 1 < NT:
                        dfront(tt + 1)
                    dhead(tt)
                    if tt >= 1:
                        dpost(tt - 1)
                dpost(NT - 1)
                S.barrier()
            S.es = es0

        if "E" in stages:
            NQT = NT // 2
            NQB = NB // 2
            gt1p = S.sb([128, D], F32, "gt1p")
            with contextlib.ExitStack() as es:
                S.es = es
                sh1, gs1, gt1 = ada_rows(es, I["w_ada"][2], I["b_ada"][2], 3, I["norm_g"][2])
                CP("pool", gt1p[:], gt1[:], [gt1], [gt1p])
                wq = S.sb([128, 8, D], BF16, "wq")
                S.dma("pool", wq[:], I["mb_wq"].rearrange("(dc p) n -> p dc n", p=128), writes=[wq])
                qng = S.sb([128, D], F32, "qng"); S.dma("sp", qng[:], I["q_norm_g"].partition_broadcast(128), writes=[qng])
                bb = S.sb([128, NB], F32, "bb"); S.dma("sp", bb[:], I["bbias"], writes=[bb])
                fut = S.sb([128, NB, NB], F32, "fut"); S.dma("sp", fut[:], I["c_fut"], writes=[fut])
                xts = [S.sb([128, D], F32, "ex%d" % i) for i in range(2)]; tmpk = S.sb([128, D], F32, "etmp"); ss = S.sb([128, 4], F32, "ess")
                hb = S.sb([128, D], BF16, "ehb"); hTs = [S.sb([128, 8, 128], BF16, "ehT%d" % i) for i in range(2)]
                qf = S.sb([128, D], F32, "qf"); qb_ = S.sb([128, D], BF16, "qb"); sq = S.sb([128, D], F32, "qsq")
                st = S.sb([128, 2, 16], F32, "qst"); qTs = [S.sb([128, 8, 128], BF16, "qT%d" % i) for i in range(2)]
                gsb = S.sb([128, 16, NB], F32, "gsb"); m8t = S.sb([128, 16, 8], F32, "m8t"); selts = [S.sb([128, 16, NB], F32, "selt%d" % i) for i in range(2)]
                sel2 = S.sb([128, 16, NB], F32, "sel2"); brow = S.sb([128, NB], F32, "browq")

                def efront(qt):
                    tt = NQT + qt
                    xt = xts[qt % 2]; hT = hTs[qt % 2]
                    S.dma("sp", xt[:], XS[tt * 128:(tt + 1) * 128, :], reads=[("XS", tt)], writes=[xt])
                    norm_tile(xt, gs1, sh1, hb, tmpk, ss)
                    transpose8(hb, hT[:], dstbuf=hT)

                qfs = [qf, S.sb([128, D], F32, "qf2")]; qbs = [qb_, S.sb([128, D], BF16, "qb2")]

                def emm(qt):
                    hT = hTs[qt % 2]
                    QB = [PSB[0], PSB[1]] if qt % 2 == 0 else [PSB[5], PSB[6]]
                    for hf in range(2):
                        for dc in range(8):
                            MM(QB[hf][:], hT[:, dc, :], wq[:, dc, hf * 512:(hf + 1) * 512], dc == 0, dc == 7, [hT, wq], [QB[hf]])

                def ehead(qt):
                    QB = [PSB[0], PSB[1]] if qt % 2 == 0 else [PSB[5], PSB[6]]
                    qf_ = qfs[qt % 2]; qb2_ = qbs[qt % 2]
                    head_rmsnorm(QB, qng, qf_, qf_[:], sq, sq[:], st, extra_scale=0.125)
                    CP("act", qb2_[:], qf_[:], [qf_], [qb2_])

                def epost(qt):
                    tt = NQT + qt
                    vb = tt // 2
                    qT = qTs[qt % 2]; selt = selts[qt % 2]; qf_ = qfs[qt % 2]; qb2_ = qbs[qt % 2]
                    transpose8(qb2_, qT[:], dstbuf=qT)
                    S.dma("sp", QT[:, :, qt * 128:(qt + 1) * 128].rearrange("a p t -> p a t"), qT[:], reads=[qT], writes=[("QT", qt)])
                    for g in range(2):
                        hp = 64 * g
                        for pr in range(8):
                            MM(PSB[2 + g][:, pr * NB:(pr + 1) * NB], qT[hp:hp + 64, pr, :], kmean[hp:hp + 64, pr, :], True, True, [qT, kmean], [PSB[2 + g]])
                    TTo("dve", brow[:], bb[:], fut[:, vb, :], ALU.add, [bb, fut], [brow])
                    gv4 = gsb[:].rearrange("p (a g) n -> p a g n", g=2)
                    for g in range(2):
                        TTo("dve", gv4[:, :, g, :], PSB[2 + g][:, 0:8 * NB].rearrange("p (a n) -> p a n", n=NB),
                            brow[:].unsqueeze(1).to_broadcast([128, 8, NB]), ALU.add, [PSB[2 + g], brow], [gsb])
                    for h in range(16):
                        S.op("dve", lambda e: e.max(out=m8t[:, h, :], in_=gsb[:, h, :]), [gsb], [(m8t, h)])
                    TTo("dve", selt[:], gsb[:], m8t[:, :, 2:3].to_broadcast([128, 16, NB]), ALU.is_ge, [gsb, m8t], [selt])
                    TS("pool", sel2[:], gsb[:], -1.0e29, None, ALU.is_gt, None, [gsb], [sel2])
                    TTo("dve", selt[:], selt[:], sel2[:], ALU.mult, [selt, sel2], [selt])
                    S.dma("sp", SEL[qt * 128:(qt + 1) * 128, :], selt[:].rearrange("p a n -> p (a n)"), reads=[selt], writes=[("SEL", qt)])
                    if qt == NQT - 1:
                        tap("sel_last", selt[:].rearrange("p a n -> p (a n)"), [selt])
                        tap("qf_last", qf_[:], [qf_])
                efront(0)
                for qt in range(NQT):
                    emm(qt)
                    if qt + 1 < NQT:
                        efront(qt + 1)
                    ehead(qt)
                    if qt >= 1:
                        epost(qt - 1)
                epost(NQT - 1)
                S.barrier()
            S.es = es0
            with contextlib.ExitStack() as es:
                S.es = es
                Kp = S.sb([128, T], BF16, "Kp"); Vp = S.sb([128, NT, 130], BF16, "Vp"); Qp = S.sb([128, TH], BF16, "Qp")
                caus = S.sb([128, 512], BF16, "caus"); S.dma("pool", caus[:], I["c_causal"].rearrange("p a b -> p (a b)"), writes=[caus])
                accs = [S.sb([128, 2, 2, 65], F32, "acc%d" % i) for i in range(2)]; pt = [S.sb([128, 512], BF16, "pt%d" % i) for i in range(6)]
                selps = [S.sb([128, 2, 2, NB], F32, "selp%d" % i) for i in range(2)]
                rc = S.sb([128, 4], F32, "rc"); ob = S.sb([128, 2, 128], BF16, "ob")
                SELv = SEL.rearrange("(q p) (h n) -> p q h n", p=128, n=NB)
                for pr in range(8):
                    S.dma("sp", Kp[:], KT[pr], reads=["KT"], writes=[Kp])
                    S.dma("sp", Vp[:], VS2[pr].rearrange("(n p) c -> p n c", p=128), reads=["VS2"], writes=[Vp])
                    S.dma("sp", Qp[:], QT[pr], reads=["QT"], writes=[Qp])
                    Vp4 = Vp[:].rearrange("p n (g c) -> p n g c", g=2)
                    for qb in range(NQB):
                        vb = NQB + qb
                        acc = accs[qb % 2]; selp = selps[qb % 2]
                        S.dma("act", selp[:], SELv[:, 2 * qb:2 * qb + 2, 2 * pr:2 * pr + 2, :], reads=["SEL"], writes=[selp])
                        S.op("pool", lambda e: e.memset(acc[:], 0.0), (), [acc])
                        SB6 = [PSB[0], PSB[1], PSB[2], PSB[3], PSB[4], PSB[5]]; OB2 = [PSB[6], psTf]

                        def emitS(n):
                            for kt in range(2):
                                for g in range(2):
                                    hp = 64 * g
                                    sb_ = SB6[(n % 3) * 2 + g]
                                    MM(sb_[:, kt * 256:(kt + 1) * 256], Kp[hp:hp + 64, n * 256 + kt * 128:n * 256 + (kt + 1) * 128],
                                       Qp[hp:hp + 64, qb * 256:(qb + 1) * 256], True, True, [Kp, Qp], [sb_])

                        def emitRest(n):
                            own = (n == vb)
                            for g in range(2):
                                sb_ = SB6[(n % 3) * 2 + g]; p_ = pt[(n % 3) * 2 + g]
                                ACT(p_[:], sb_[:], AF.Exp, [sb_], [p_])
                                if own:
                                    TTo("pool", p_[:], p_[:], caus[:], ALU.mult, [p_, caus], [p_])
                            for g in range(2):
                                p_ = pt[(n % 3) * 2 + g]; ob_ = OB2[g]
                                for qtl in range(2):
                                    for kt in range(2):
                                        MM(ob_[:, qtl * 65:(qtl + 1) * 65], p_[:, kt * 256 + qtl * 128:kt * 256 + (qtl + 1) * 128],
                                           Vp4[:, n * 2 + kt, g, :], kt == 0, kt == 1, [p_, Vp], [ob_])
                            for g in range(2):
                                ob_ = OB2[g]
                                for qtl in range(2):
                                    if own:
                                        TTo("dve", acc[:, qtl, g, :], ob_[:, qtl * 65:(qtl + 1) * 65], acc[:, qtl, g, :], ALU.add, [ob_, (acc, qtl * 2 + g)], [(acc, qtl * 2 + g)])
                                    else:
                                        STT(acc[:, qtl, g, :], ob_[:, qtl * 65:(qtl + 1) * 65], selp[:, qtl, g, n:n + 1], acc[:, qtl, g, :],
                                            ALU.mult, ALU.add, [ob_, selp, (acc, qtl * 2 + g)], [(acc, qtl * 2 + g)])
                        emitS(0)
                        if vb >= 1:
                            emitS(1)
                        for n in range(vb + 1):
                            if n + 2 <= vb:
                                emitS(n + 2)
                            emitRest(n)
                        S.op("dve", lambda e: e.reciprocal(out=rc[:].rearrange("p (a b) -> p a b", b=2), in_=acc[:, :, :, 64]), [acc], [rc])
                        TTo("dve", ob[:].rearrange("p q (g c) -> p q g c", g=2), acc[:, :, :, 0:64],
                            rc[:].rearrange("p (a b) -> p a b", b=2).unsqueeze(3).to_broadcast([128, 2, 2, 64]), ALU.mult, [acc, rc], [ob])
                        S.dma("pool", OS[qb * 256:(qb + 1) * 256, pr * 128:(pr + 1) * 128].rearrange("(q p) c -> p q c", p=128), ob[:],
                              reads=[ob], writes=[("OS", pr * 1000 + qb)])
                S.barrier()
            S.es = es0
            with contextlib.ExitStack() as es:
                S.es = es
                gt1 = gt1p
                wo1 = S.sb([128, 8, D], BF16, "wo1")
                S.dma("pool", wo1[:], I["mb_wo"].rearrange("(dc p) n -> p dc n", p=128), writes=[wo1])
                xts = [S.sb([128, D], F32, "e3x%d" % i) for i in range(2)]; ots = [S.sb([128, D], BF16, "e3o%d" % i) for i in range(2)]
                oTs = [S.sb([128, 8, 128], BF16, "e3oT%d" % i) for i in range(2)]
                x3s = [S.sb([128, D], F32, "e3x3%d" % i) for i in range(2)]

                def ffront(qt):
                    tt = NQT + qt
                    xt = xts[qt % 2]; ot = ots[qt % 2]; oT = oTs[qt % 2]
                    S.dma("sp", xt[:], XS[tt * 128:(tt + 1) * 128, :], reads=[("XS", tt)], writes=[xt])
                    S.dma("sp", ot[:], OS[qt * 128:(qt + 1) * 128, :], reads=["OS"], writes=[ot])
                    transpose8(ot, oT[:], dstbuf=oT)

                def fback(qt):
                    tt = NQT + qt
                    xt = xts[qt % 2]; oT = oTs[qt % 2]; x3 = x3s[qt % 2]
                    for hf in range(2):
                        for dc in range(8):
                            MM(PSB[hf][:], oT[:, dc, :], wo1[:, dc, hf * 512:(hf + 1) * 512], dc == 0, dc == 7, [oT, wo1], [PSB[hf]])
                        if hf == 0 and qt + 1 < NQT:
                            ffront(qt + 1)
                        TTo("dve", x3[:, hf * 512:(hf + 1) * 512], PSB[hf][:], gt1[:, hf * 512:(hf + 1) * 512], ALU.mult, [PSB[hf], gt1], [x3])
                    TTo("pool", x3[:], x3[:], xt[:], ALU.add, [x3, xt], [x3])
                    S.dma("sp", XS[tt * 128:(tt + 1) * 128, :], x3[:], reads=[x3], writes=[("XS", tt)])
                    if qt == NQT - 1:
                        tap("x3_last", x3[:], [x3])
                ffront(0)
                for qt in range(NQT):
                    fback(qt)
                S.barrier()
            S.es = es0

        if "F" in stages:
            ffn_stage(1, NT // 2, NT, True)

        S.finish()
        print("ninst", S.ninst, "nwait", S.nwait)
    return nc


def _consts():
    p = np.arange(128)
    c = {}
    c["c_ident"] = np.eye(128, dtype=np.float32)
    rm = np.ones((128, 1024), np.float32); rm[:, ::64] = 0.0
    c["c_rmask"] = rm
    t = np.arange(64)
    pm = (p % 64)[:, None]
    c["c_msu"] = (pm < t[None, :]).astype(np.float32)
    c["c_msl"] = (t[None, :] < pm).astype(np.float32)
    c["c_miu"] = (pm <= t[None, :]).astype(np.float32)
    c["c_id64"] = (pm == t[None, :]).astype(np.float32)
    c["c_bones"] = ((p[:, None] // 64) == (p[None, :] // 64)).astype(np.float32)
    c["c_hsel"] = ((p[:, None] // 64) == np.arange(2)[None, :]).astype(np.float32)
    q = np.arange(256)
    cz = np.zeros((128, 2, 256), np.float32)
    for kt in range(2):
        cz[:, kt, :] = ((kt * 128 + p)[:, None] <= q[None, :])
    c["c_causal"] = cz
    return c


def prep_shared(inp):
    f = lambda a: np.ascontiguousarray(np.asarray(a, dtype=np.float32))
    pp = lambda v: f(np.asarray(v).reshape(8, 128).T)
    sh = {}
    sh["w_ada"] = f(np.asarray(inp["w_ada"]).reshape(4, D, 3 * D))
    sh["b_ada"] = f(np.asarray(inp["b_ada"]).reshape(4, 3 * D))
    sh["norm_g"] = f(np.asarray(inp["norm_g"]).reshape(4, D))
    sh["mu"] = f(np.asarray(inp["rw_mu"])[0].reshape(6, 8, 128).transpose(2, 0, 1))
    sh["w_rkv"] = f(np.asarray(inp["rw_w_rkv"])[0])
    sh["w0"] = pp(inp["rw_w0"][0]); sh["a0"] = pp(inp["rw_a0"][0]); sh["k_k"] = pp(inp["rw_k_k"][0])
    sh["k_a"] = pp(inp["rw_k_a"][0]); sh["r_k"] = pp(np.asarray(inp["rw_r_k"])[0].reshape(-1))
    sh["w1"] = f(inp["rw_w1"][0]); sh["w2"] = f(inp["rw_w2"][0]); sh["a1"] = f(inp["rw_a1"][0]); sh["a2"] = f(inp["rw_a2"][0])
    sh["g1"] = f(inp["rw_g1"][0]); sh["g2"] = f(inp["rw_g2"][0])
    sh["lnx_g"] = f(inp["rw_lnx_g"][0]); sh["lnx_b"] = f(inp["rw_lnx_b"][0]); sh["rw_wo"] = f(inp["rw_w_o"][0])
    sh["ffn_g"] = f(inp["ffn_w_gate"]); sh["ffn_u"] = f(inp["ffn_w_up"]); sh["ffn_d"] = f(inp["ffn_w_down"])
    sh["kv_norm_g"] = f(inp["kv_norm_g"]); sh["kv_w_ada"] = f(inp["kv_w_ada"]); sh["kv_b_ada"] = f(inp["kv_b_ada"])
    sh["kv_wk"] = f(inp["kv_w_k"]); sh["kv_wv"] = f(inp["kv_w_v"])
    sh["k_norm_g"] = f(np.tile(np.asarray(inp["k_norm_g"]), NH)); sh["q_norm_g"] = f(np.tile(np.asarray(inp["mb_q_norm_g"])[0], NH))
    sh["mb_wq"] = f(inp["mb_w_q"][0]); sh["mb_wo"] = f(inp["mb_w_o"][0])
    sh.update(_consts())
    return sh


def prep_core(x_b, c_b, hf, T):
    TH = T // 2
    NT, NB = T // 128, T // 256
    m = {}
    if hf == 1:
        xv = np.asarray(x_b, dtype=np.float32)
        valid = np.ones(T, np.float32)
    else:
        xv = np.concatenate([np.zeros((TH, D), np.float32), np.asarray(x_b[:TH], dtype=np.float32)], 0)
        valid = np.concatenate([np.zeros(TH, np.float32), np.ones(TH, np.float32)])
    m["xv"] = np.ascontiguousarray(xv)
    m["valid"] = np.ascontiguousarray(valid.reshape(NT, 128).T)
    bb = np.where(valid.reshape(NB, 256)[:, 0] > 0, 0.0, NEG).astype(np.float32)
    m["bbias"] = np.ascontiguousarray(np.tile(bb[None, :], (128, 1)))
    m["cvec"] = np.ascontiguousarray(np.asarray(c_b, dtype=np.float32).reshape(8, 128).T)
    fu = np.where(np.arange(NB)[None, :] >= np.arange(NB)[:, None], NEG, 0.0).astype(np.float32)
    m["c_fut"] = np.ascontiguousarray(np.tile(fu[None], (128, 1, 1)))
    return m


_NC_CACHE = {}


def kernel(**inputs):
    x = np.asarray(inputs["x"], dtype=np.float32)
    c = np.asarray(inputs["c"], dtype=np.float32)
    Bn, T, _ = x.shape
    TH = T // 2
    key = (T,)
    if key not in _NC_CACHE:
        _NC_CACHE[key] = build(T)
    nc = _NC_CACHE[key]
    sh = prep_shared(inputs)
    in_maps = []
    for b in range(Bn):
        for hf in range(2):
            m = dict(sh)
            m.update(prep_core(x[b], c[b], hf, T))
            in_maps.append(m)
    res = run_bass_kernel_spmd(nc, in_maps, core_ids=list(range(2 * Bn)))
    outp = np.empty((Bn, T, D), np.float32)
    for b in range(Bn):
        for hf in range(2):
            outp[b, hf * TH:(hf + 1) * TH] = res.results[b * 2 + hf]["out"]
    return outp
```

```python
import contextlib
import numpy as np
import concourse.bass as bass
import concourse.mybir as mybir
from concourse.bass_utils import run_bass_kernel_spmd

F32 = mybir.dt.float32
BF16 = mybir.dt.bfloat16
AF = mybir.ActivationFunctionType
ALU = mybir.AluOpType
AX = mybir.AxisListType

D = 1024
NH = 16
HD = 64
FF = 2816
NFC = FF // 128
CDEC = float(np.exp(-0.5))
NEG = -1.0e30
NO_SELF_SYNC = ()


class Buf:
    __slots__ = ("t", "name")

    def __init__(self, t, name):
        self.t = t
        self.name = name

    def __getitem__(self, k):
        return self.t[k]


class View:
    def __init__(self, key, fn):
        self.key = key
        self.fn = fn

    def __getitem__(self, k):
        return self.fn()[k]


class TokSet:
    __slots__ = ("d",)

    def __init__(self):
        self.d = {}

    def add(self, tok):
        sem, val = tok
        k = id(sem)
        if k not in self.d or self.d[k][1] < val:
            self.d[k] = (sem, val)

    def items_(self):
        return list(self.d.values())


class Sched:
    def __init__(self, nc, es, ndma=10):
        self.nc = nc
        self.es = es
        self.eng = {"pe": nc.tensor, "act": nc.scalar, "dve": nc.vector, "pool": nc.gpsimd, "sp": nc.sync}
        self.esem = {k: es.enter_context(nc.semaphore("e_" + k)) for k in self.eng}
        self.ecnt = {k: 0 for k in self.eng}
        self.seen = {k: {} for k in self.eng}
        self.lastw = {}
        self.readers = {}
        self.dpool = {}
        self.dnext = {}
        self.dcnt = {}
        for q in ("sp", "pool", "act"):
            self.dpool[q] = [es.enter_context(nc.semaphore("d_%s%d" % (q, i))) for i in range(ndma)]
            self.dnext[q] = 0
        self.nbuf = 0
        self.parts = {}
        self.ninst = {k: 0 for k in self.eng}
        self.nwait = 0

    def sb(self, shape, dt=F32, name=None):
        self.nbuf += 1
        name = (name or "sb") + "_%d" % self.nbuf
        t = self.es.enter_context(self.nc.sbuf_tensor(name, list(shape), dt))
        return Buf(t, name)

    def ps(self, shape, dt=F32, name=None):
        self.nbuf += 1
        name = (name or "ps") + "_%d" % self.nbuf
        t = self.es.enter_context(self.nc.psum_tensor(name, list(shape), dt))
        return Buf(t, name)

    def _wait(self, engine, deps):
        e = self.eng[engine]
        seen = self.seen[engine]
        best = {}
        for (sem, val) in deps:
            k = id(sem)
            if k not in best or best[k][1] < val:
                best[k] = (sem, val)
        for k, (sem, val) in best.items():
            if engine == "pe" and sem is self.esem["pe"]:
                continue
            if engine in NO_SELF_SYNC and sem is self.esem.get(engine):
                continue
            if seen.get(k, 0) < val:
                e.wait_ge(sem, val)
                self.nwait += 1
                seen[k] = val

    def _rel(self, k):
        if isinstance(k, tuple):
            self.parts.setdefault(k[0], set()).add(k)
            return (k, k[0])
        ps = self.parts.get(k)
        return (k,) + tuple(ps) if ps else (k,)

    def _collect(self, reads, writes):
        deps = []
        reads = [getattr(b, "key", b) for b in reads]
        writes = [getattr(b, "key", b) for b in writes]
        for b0 in reads:
            for b in self._rel(b0):
                w = self.lastw.get(b)
                if w:
                    deps.extend(w.items_())
        for b0 in writes:
            for b in self._rel(b0):
                w = self.lastw.get(b)
                if w:
                    deps.extend(w.items_())
                r = self.readers.get(b)
                if r:
                    deps.extend(r.items_())
        return deps

    def _commit(self, tok, reads, writes):
        reads = [getattr(b, "key", b) for b in reads]
        writes = [getattr(b, "key", b) for b in writes]
        for b in reads:
            r = self.readers.get(b)
            if r is None:
                r = self.readers[b] = TokSet()
            r.add(tok)
        for b in writes:
            w = TokSet()
            w.add(tok)
            self.lastw[b] = w
            self.readers[b] = TokSet()

    def op(self, engine, fn, reads=(), writes=()):
        self._wait(engine, self._collect(reads, writes))
        inst = fn(self.eng[engine])
        self.ecnt[engine] += 1
        self.ninst[engine] += 1
        inst.then_inc(self.esem[engine], 1)
        tok = (self.esem[engine], self.ecnt[engine])
        self._commit(tok, reads, writes)
        return tok

    def dma(self, queue, out, in_, reads=(), writes=(), **kw):
        pool = self.dpool[queue]
        i = self.dnext[queue]
        self.dnext[queue] = (i + 1) % len(pool)
        sem = pool[i]
        prev = self.dcnt.get(id(sem), 0)
        deps = self._collect(reads, writes)
        if prev:
            deps.append((sem, prev))
        self._wait(queue, deps)
        inst = self.eng[queue].dma_start(out=out, in_=in_, **kw)
        inst.then_inc(sem, 16)
        self.ninst[queue] += 1
        self.dcnt[id(sem)] = prev + 16
        tok = (sem, prev + 16)
        self._commit(tok, reads, writes)
        return tok

    def all_tokens(self):
        deps = []
        for q, pool in self.dpool.items():
            for sem in pool:
                v = self.dcnt.get(id(sem), 0)
                if v:
                    deps.append((sem, v))
        for k in self.eng:
            if self.ecnt[k]:
                deps.append((self.esem[k], self.ecnt[k]))
        return deps

    def barrier(self):
        deps = self.all_tokens()
        for k in self.eng:
            self._wait(k, deps)

    def finish(self):
        self._wait("sp", self.all_tokens())


class StopBuild(Exception):
    pass


def build(T, stages="ABCDEF", taps=(), stop=None):
    NT = T // 128
    NB = T // 256
    TH = T // 2
    nc = bass.Bass("TRN2", target_bir_lowering=False)
    I = {}

    def inp(name, shape, dt=F32):
        I[name] = nc.dram_tensor(name, list(shape), dt, kind="ExternalInput").ap()
        return I[name]

    inp("xv", [T, D]); inp("valid", [128, NT]); inp("bbias", [128, NB]); inp("cvec", [128, 8])
    inp("w_ada", [4, D, 3 * D]); inp("b_ada", [4, 3 * D]); inp("norm_g", [4, D])
    inp("mu", [128, 6, 8]); inp("w_rkv", [3, D, D])
    for n in ("w0", "a0", "k_k", "k_a", "r_k"):
        inp(n, [128, 8])
    inp("w1", [D, 64]); inp("w2", [64, D]); inp("a1", [D, 64]); inp("a2", [64, D])
    inp("g1", [D, 160]); inp("g2", [160, D]); inp("lnx_g", [D]); inp("lnx_b", [D]); inp("rw_wo", [D, D])
    inp("ffn_g", [2, D, FF]); inp("ffn_u", [2, D, FF]); inp("ffn_d", [2, FF, D])
    inp("kv_norm_g", [D]); inp("kv_w_ada", [D, 2 * D]); inp("kv_b_ada", [2 * D])
    inp("kv_wk", [D, D]); inp("kv_wv", [D, D]); inp("k_norm_g", [D]); inp("q_norm_g", [D])
    inp("mb_wq", [D, D]); inp("mb_wo", [D, D])
    inp("c_ident", [128, 128]); inp("c_rmask", [128, 1024]); inp("c_msu", [128, 64]); inp("c_msl", [128, 64])
    inp("c_miu", [128, 64]); inp("c_id64", [128, 64]); inp("c_bones", [128, 128]); inp("c_hsel", [128, 2])
    inp("c_causal", [128, 2, 256]); inp("c_fut", [128, NB, NB])

    out = nc.dram_tensor("out", [TH, D], F32, kind="ExternalOutput").ap()
    XS = nc.dram_tensor("xs_scr", [T, D], F32, kind="Internal").ap()
    KT = nc.dram_tensor("kt_scr", [8, 128, T], BF16, kind="Internal").ap()
    VS = nc.dram_tensor("v_scr", [T, D], BF16, kind="Internal").ap()
    QT = nc.dram_tensor("qt_scr", [8, 128, TH], BF16, kind="Internal").ap()
    SEL = nc.dram_tensor("sel_scr", [TH, NH * NB], F32, kind="Internal").ap()
    OS = nc.dram_tensor("o_scr", [TH, D], BF16, kind="Internal").ap()
    TAP = {}
    for (name, shape) in taps:
        TAP[name] = nc.dram_tensor("tap_" + name, list(shape), F32, kind="ExternalOutput").ap()

    xs_key = "XS"
    with contextlib.ExitStack() as es0:
        S = Sched(nc, es0)

        def TTo(eng, out, a, b, op, reads, writes):
            return S.op(eng, lambda e: e.tensor_tensor(out=out, in0=a, in1=b, op=op), reads, writes)

        def STT(out, a, sc, b, op0, op1, reads, writes):
            return S.op("dve", lambda e: e.scalar_tensor_tensor(out=out, in0=a, scalar=sc, in1=b, op0=op0, op1=op1), reads, writes)

        def TS(eng, out, a, s1, s2, op0, op1, reads, writes):
            if op1 is None:
                return S.op(eng, lambda e: e.tensor_scalar(out=out, in0=a, scalar1=s1, scalar2=None, op0=op0), reads, writes)
            return S.op(eng, lambda e: e.tensor_scalar(out=out, in0=a, scalar1=s1, scalar2=s2, op0=op0, op1=op1), reads, writes)

        def ACT(out, in_, func, reads, writes, bias=None, scale=None, accum=None):
            kw = {}
            if bias is not None:
                kw["bias"] = bias
            if scale is not None:
                kw["scale"] = scale
            if accum is not None:
                kw["accum_out"] = accum
            return S.op("act", lambda e: e.activation(out=out, in_=in_, func=func, **kw), reads, writes)

        def CP(eng, out, in_, reads, writes):
            if eng == "act":
                return S.op("act", lambda e: e.copy(out=out, in_=in_), reads, writes)
            return S.op(eng, lambda e: e.tensor_copy(out=out, in_=in_), reads, writes)

        def MM(out, lhsT, rhs, start, stop, reads, writes):
            return S.op("pe", lambda e: e.matmul(out, lhsT, rhs, start=start, stop=stop), reads, writes)

        def TR(out, in_, ident, reads, writes):
            return S.op("pe", lambda e: e.transpose(out, in_, ident), reads, writes)

        def kc(v, ch):
            return (getattr(v, "key", v), ch)

        def ck(n):
            if stop is not None and n == stop:
                raise StopBuild()

        def tap(name, src_ap, reads):
            if name in TAP:
                S.dma("sp", TAP[name], src_ap, reads=reads)

        identf = S.sb([128, 128], F32, "identf"); identb = S.sb([128, 128], BF16, "identb")
        S.dma("sp", identf[:], I["c_ident"], writes=[identf])
        S.dma("pool", identb[:], I["c_ident"], writes=[identb])
        PSB = [S.ps([128, 512], F32, "bank%d" % i) for i in range(7)]
        psTf = S.ps([128, 512], F32, "psTf")
        psT = View(psTf, lambda: psTf[:].bitcast(BF16))
        cs = S.sb([128, 8], F32, "cs")
        S.dma("sp", cs[:], I["cvec"], writes=[cs])
        ACT(cs[:], cs[:], AF.Silu, [cs], [cs])

        def ada_rows(es, w_ap, b_ap, n, g_ap):
            rows = [S.sb([128, D], F32, "row%d" % j) for j in range(n)]
            with contextlib.ExitStack() as es1:
                S.es = es1
                brow = S.sb([128, n * D], F32, "brow")
                csrep = S.sb([128, 8, 128], F32, "csrep")
                S.op("dve", lambda e: e.memset(csrep[:], 1.0), (), [csrep])
                TTo("dve", csrep[:], csrep[:], cs[:].unsqueeze(2).to_broadcast([128, 8, 128]), ALU.mult, [cs, csrep], [csrep])
                grow = S.sb([128, D], F32, "grow")
                wsl = [S.sb([128, 8, 512], F32, "wsl%d" % i) for i in range(2)]
                S.dma("sp", brow[:], b_ap.partition_broadcast(128), writes=[brow])
                S.dma("sp", grow[:], g_ap.partition_broadcast(128), writes=[grow])
                wv = w_ap.rearrange("(dc p) n -> p dc n", p=128)
                for ns in range(2 * n):
                    w = wsl[ns % 2]
                    S.dma("sp" if ns % 2 == 0 else "act", w[:], wv[:, :, ns * 512:(ns + 1) * 512], writes=[w])
                    pb = PSB[ns % 2]
                    for dc in range(8):
                        MM(pb[:], csrep[:, dc, :], w[:, dc, :], dc == 0, dc == 7, [csrep, w], [pb])
                    j, hf = ns // 2, ns % 2
                    TTo("dve", rows[j][:, hf * 512:(hf + 1) * 512], pb[:], brow[:, ns * 512:(ns + 1) * 512], ALU.add,
                        [pb, brow], [rows[j]])
                STT(rows[1][:], rows[1][:], 1.0, grow[:], ALU.add, ALU.mult, [rows[1], grow], [rows[1]])
                S.barrier()
            S.es = es
            return rows

        def norm_tile_g(xt, gs, sh, hb, tmpk, ss, vcol=None):
            class _T:
                def __getitem__(s_, k):
                    a = tmpk[:]
                    if len(a.shape) == 3:
                        a = a.rearrange("p a b -> p (a b)")
                    return a[k]
            tmp = _T()
            ACT(tmp[:], xt[:], AF.Square, [xt], [tmpk, ss], accum=ss[:, 0:1])
            yield
            ACT(ss[:, 1:2], ss[:, 0:1], AF.Sqrt, [ss], [ss], bias=1e-6, scale=1.0 / D)
            yield
            S.op("dve", lambda e: e.reciprocal(out=ss[:, 2:3], in_=ss[:, 1:2]), [ss], [ss])
            yield
            if vcol is not None:
                TTo("dve", ss[:, 2:3], ss[:, 2:3], vcol, ALU.mult, [ss], [ss])
                yield
            STT(tmp[:], xt[:], ss[:, 2:3], gs[:], ALU.mult, ALU.mult, [xt, ss, gs], [tmpk])
            yield
            if vcol is not None:
                STT(hb[:], sh[:], vcol, tmp[:], ALU.mult, ALU.add, [sh, tmpk], [hb])
                yield
            else:
                TTo("dve", hb[:], tmp[:], sh[:], ALU.add, [tmpk, sh], [hb])
                yield

        def norm_tile(xt, gs, sh, hb, tmpk, ss, vcol=None):
            for _ in norm_tile_g(xt, gs, sh, hb, tmpk, ss, vcol):
                pass

        def transpose8(src, dst_view, reads_extra=(), dstbuf=None, eng="act"):
            for dc in range(8):
                TR(psT[:, dc * 128:(dc + 1) * 128], src[:, dc * 128:(dc + 1) * 128], identb[:], [src, identb], [psT])
            CP(eng, dst_view, psT[:].rearrange("p (a b) -> p a b", b=128), [psT], [dstbuf])

        def load_w_bf16(buf_view, w_ap, bufkey, q="pool"):
            S.dma(q, buf_view, w_ap, writes=[bufkey])

        valid_sb = S.sb([128, NT], F32, "valid")
        S.dma("sp", valid_sb[:], I["valid"], writes=[valid_sb])

        if "B" in stages:
            with contextlib.ExitStack() as es:
                S.es = es
                sh0, gs0, gt0 = ada_rows(es, I["w_ada"][0], I["b_ada"][0], 3, I["norm_g"][0])
                wrkv = [S.sb([128, 8, D], BF16, "wrkv%d" % i) for i in range(3)]
                for i in range(3):
                    S.dma("pool", wrkv[i][:], I["w_rkv"][i].rearrange("(dc p) n -> p dc n", p=128), writes=[wrkv[i]])
                w1 = S.sb([128, 8, 64], BF16, "w1"); a1 = S.sb([128, 8, 64], BF16, "a1"); g1 = S.sb([128, 8, 160], BF16, "g1")
                S.dma("pool", w1[:], I["w1"].rearrange("(dc p) n -> p dc n", p=128), writes=[w1])
                S.dma("pool", a1[:], I["a1"].rearrange("(dc p) n -> p dc n", p=128), writes=[a1])
                S.dma("pool", g1[:], I["g1"].rearrange("(dc p) n -> p dc n", p=128), writes=[g1])
                w2 = S.sb([64, D], BF16, "w2"); a2 = S.sb([64, D], BF16, "a2")
                g2a = S.sb([128, D], BF16, "g2a"); g2b = S.sb([128, D], BF16, "g2b")
                S.op("pool", lambda e: e.memset(g2b[:], 0.0), (), [g2b])
                S.dma("pool", w2[:], I["w2"], writes=[w2]); S.dma("pool", a2[:], I["a2"], writes=[a2])
                S.dma("pool", g2a[:], I["g2"][0:128, :], writes=[g2a]); S.dma("pool", g2b[0:32, :], I["g2"][128:160, :], writes=[g2b])
                wo = S.sb([128, 8, D], BF16, "wo")
                S.dma("pool", wo[:], I["rw_wo"].rearrange("(dc p) n -> p dc n", p=128), writes=[wo])
                mu = S.sb([128, 6, 8], F32, "mu"); S.dma("sp", mu[:], I["mu"], writes=[mu])
                pp = {}
                for n in ("w0", "a0", "k_k", "k_a", "r_k"):
                    pp[n] = S.sb([128, 8], F32, n); S.dma("sp", pp[n][:], I[n], writes=[pp[n]])
                lnxg = S.sb([128, D], F32, "lnxg"); lnxb = S.sb([128, D], F32, "lnxb")
                S.dma("sp", lnxg[:], I["lnx_g"].partition_broadcast(128), writes=[lnxg])
                S.dma("sp", lnxb[:], I["lnx_b"].partition_broadcast(128), writes=[lnxb])
                cst = {}
                for n, shp in (("c_msu", [128, 64]), ("c_msl", [128, 64]), ("c_miu", [128, 64]),
                               ("c_id64", [128, 64]), ("c_bones", [128, 128])):
                    cst[n] = S.sb(shp, F32, n); S.dma("sp", cst[n][:], I[n], writes=[cst[n]])
                cst["c_rmask"] = S.sb([128, 1024], BF16, "c_rmask"); S.dma("pool", cst["c_rmask"][:], I["c_rmask"], writes=[cst["c_rmask"]])
                hsel = S.sb([128, 2], BF16, "hsel"); S.dma("pool", hsel[:], I["c_hsel"], writes=[hsel])
                H32 = S.sb([128, 8, 64], F32, "H32"); Hb = S.sb([128, 8, 64], BF16, "Hb")
                S.op("dve", lambda e: e.memset(H32[:], 0.0), (), [H32])
                S.op("dve", lambda e: e.memset(Hb[:], 0.0), (), [Hb])
                xt = [S.sb([128, D], F32, "xt0")] * 2
                ss = S.sb([128, 4], F32, "ss")
                hb = S.sb([128, D], BF16, "hb")
                hT = [S.sb([128, 8, 129], BF16, "hT%d" % i) for i in range(2)]
                S.op("dve", lambda e: e.memset(hT[1][:], 0.0), (), [hT[1]])
                dx = S.sb([128, 8, 128], BF16, "dx")
                xs = [S.sb([128, 8, 128], BF16, "xs%d" % i) for i in range(6)]
                f = {n: S.sb([128, 8, 128], F32, "f_" + n) for n in
                     ("sigw", "lp", "P", "alr", "k", "r", "kk", "km", "bA", "tA", "tB")}
                for n in ("iP", "PC"):
                    f[n] = S.sb([128, 8, 128], BF16, "f_" + n)
                class _V:
                    def __init__(s_, b, pat, **kw):
                        s_.b, s_.pat, s_.kw = b, pat, kw
                    def __getitem__(s_, k):
                        return s_.b[:].rearrange(s_.pat, **s_.kw)[k]
                tmp_b = f["tA"]
                th = S.sb([64, 128], BF16, "th"); al = S.sb([64, 128], BF16, "al"); sg = S.sb([128, 2, 128], BF16, "sg")
                S.op("pool", lambda e: e.memset(sg[:], 0.0), (), [sg])
                v_sb = S.sb([128, D], BF16, "v_sb"); g_sb = S.sb([128, D], BF16, "g_sb")
                ARt = S.sb([128, 8, 2, 2, 64], BF16, "ARt"); BKt = S.sb([128, 8, 2, 2, 64], BF16, "BKt")
                BKh = S.sb([128, 8, 2, 128], BF16, "BKh")
                prod = S.sb([128, 8, 128], BF16, "prod"); rk_sb = S.sb([128, 16], F32, "rk_sb")
                def hview(b):
                    return View(b, lambda: b[:].rearrange("p a b -> p (a b)")[:, 0:512].rearrange("p (a b) -> p a b", b=64))
                A_sb = [hview(f["alr"]), hview(f["k"])]
                N_sb = [hview(f["kk"]), hview(f["km"])]
                T_sb = [hview(f["bA"]), hview(f["tB"])]
                Aak = S.sb([128, 8, 64], BF16, "Aak"); Qrb = S.sb([128, 8, 64], BF16, "Qrb"); Qrk = S.sb([128, 8, 64], BF16, "Qrk")
                BKhT = S.sb([128, 8, 2, 64], BF16, "BKhT")
                G_sb = hview(f["r"]); X0 = S.sb([128, 8, 64], F32, "X0"); U_sb = S.sb([128, 8, 64], BF16, "U_sb")
                st = S.sb([128, 4, 16], F32, "st")
                c0, c1, c2 = PSB[4], PSB[5], PSB[6]
                fm = [PSB[0], PSB[1]]
                tm = [PSB[2], PSB[3]]

                def bc8(t, n=128):
                    return t[:].unsqueeze(2).to_broadcast([128, 8, n])

                def m8(mk):
                    return cst[mk][:].unsqueeze(1).to_broadcast([128, 8, 64])

                def fl(b):
                    return b[:].rearrange("p a b -> p (a b)")

                def F1g(tt):
                    xcur = xt[tt % 2]
                    hTc, hTp = hT[tt % 2], hT[(tt + 1) % 2]
                    S.dma("sp", xcur[:], I["xv"][tt * 128:(tt + 1) * 128, :], writes=[xcur])
                    yield
                    for _ in norm_tile_g(xcur, gs0, sh0, hb, f["tA"], ss, vcol=valid_sb[:, tt:tt + 1]):
                        yield
                    CP("pool", hTc[:, :, 0:1], hTp[:, :, 128:129], [hTp], [hTc])
                    yield
                    transpose8(hb, hTc[:, :, 1:129], dstbuf=hTc)
                    yield
                    TTo("dve", dx[:], hTc[:, :, 0:128], hTc[:, :, 1:129], ALU.subtract, [hTc], [dx])
                    yield
                    for i in range(6):
                        eng = "dve" if i < 3 else "pool"
                        TTo(eng, xs[i][:], dx[:], mu[:, i, :].unsqueeze(2).to_broadcast([128, 8, 128]), ALU.mult, [dx, mu], [xs[i]])
                        yield
                        TTo(eng, xs[i][:], xs[i][:], hTc[:, :, 1:129], ALU.add, [xs[i], hTc], [xs[i]])
                        yield

                def F1(tt):
                    for _ in F1g(tt):
                        pass
                pend = [None]

                def drip(n=1):
                    g_ = pend[0]
                    if g_ is None:
                        return
                    for _ in range(n):
                        try:
                            next(g_)
                        except StopIteration:
                            pend[0] = None
                            return

                def F2a():
                    for qi, dst in ((0, f["r"]), (1, f["k"])):
                        for grp in range(2):
                            pb = fm[grp]
                            for pl in range(4):
                                pr = grp * 4 + pl
                                for dc in range(8):
                                    MM(pb[:, pl * 128:(pl + 1) * 128], wrkv[qi][:, dc, pr * 128:(pr + 1) * 128], xs[qi][:, dc, :],
                                       dc == 0, dc == 7, [wrkv[qi], xs[qi]], [pb])
                            CP("act", fl(dst)[:, grp * 512:(grp + 1) * 512], pb[:], [pb], [(dst, grp)])
                    TTo("dve", f["kk"][:], f["k"][:], bc8(pp["k_k"]), ALU.mult, [f["k"], pp["k_k"]], [f["kk"]])
                    ACT(fl(f["bA"]), fl(f["kk"]), AF.Square, [f["kk"]], [f["bA"]])
                    for grp in range(2):
                        MM(fm[grp][:], cst["c_bones"][:], fl(f["bA"])[:, grp * 512:(grp + 1) * 512], True, True, [cst["c_bones"], f["bA"]], [fm[grp]])
                        TS("dve", fl(f["tB"])[:, grp * 512:(grp + 1) * 512], fm[grp][:], 1e-24, None, ALU.max, None, [fm[grp]], [f["tB"]])
                    ACT(fl(f["tB"]), fl(f["tB"]), AF.Ln, [f["tB"]], [f["tB"]])
                    ACT(fl(f["tB"]), fl(f["tB"]), AF.Exp, [f["tB"]], [f["tB"]], scale=-0.5)
                    TTo("dve", f["kk"][:], f["kk"][:], f["tB"][:], ALU.mult, [f["kk"], f["tB"]], [f["kk"]])

                for tt in range(NT):
                  try:
                    if tt == 0:
                        F1(0)
                    ck(2)
                    if tt == 0:
                        F2a()
                    ck(3)
                    for hf in range(2):
                        for dc in range(8):
                            MM(tm[hf][:], xs[2][:, dc, :], wrkv[2][:, dc, hf * 512:(hf + 1) * 512], dc == 0, dc == 7, [xs[2], wrkv[2]], [tm[hf]])
                        CP("act", v_sb[:, hf * 512:(hf + 1) * 512], tm[hf][:], [tm[hf]], [v_sb])
                    ck(4)
                    for dc in range(8):
                        MM(c0[0:64, 0:128], w1[:, dc, :], xs[3][:, dc, :], dc == 0, dc == 7, [w1, xs[3]], [c0])
                    ACT(th[:], c0[0:64, 0:128], AF.Tanh, [c0], [th])
                    for dc in range(8):
                        MM(c1[0:64, 0:128], a1[:, dc, :], xs[4][:, dc, :], dc == 0, dc == 7, [a1, xs[4]], [c1])
                    CP("act", al[:], c1[0:64, 0:128], [c1], [al])
                    for (src, wgt, bias, dst) in ((th, w2, pp["w0"], f["sigw"]), (al, a2, pp["a0"], f["alr"])):
                        for grp in range(2):
                            pb = fm[grp]
                            for pl in range(4):
                                pr = grp * 4 + pl
                                MM(pb[:, pl * 128:(pl + 1) * 128], wgt[:, pr * 128:(pr + 1) * 128], src[:], True, True, [wgt, src], [pb])
                            for pl in range(4):
                                pr = grp * 4 + pl
                                ACT(dst[:, pr, :], pb[:, pl * 128:(pl + 1) * 128], AF.Sigmoid, [pb, bias], [(dst, pr)], bias=bias[:, pr:pr + 1])
                    ck(5)
                    for dc in range(8):
                        MM(c0[:, 0:128], g1[:, dc, 0:128], xs[5][:, dc, :], dc == 0, dc == 7, [g1, xs[5]], [c0])
                    for dc in range(8):
                        MM(c1[0:32, 0:128], g1[:, dc, 128:160], xs[5][:, dc, :], dc == 0, dc == 7, [g1, xs[5]], [c1])
                    ACT(sg[:, 0, :], c0[:, 0:128], AF.Sigmoid, [c0], [sg])
                    ACT(sg[0:32, 1, :], c1[0:32, 0:128], AF.Sigmoid, [c1], [sg])
                    for hf in range(2):
                        MM(tm[hf][:], sg[:, 0, :], g2a[:, hf * 512:(hf + 1) * 512], True, False, [sg, g2a], [tm[hf]])
                        MM(tm[hf][:], sg[:, 1, :], g2b[:, hf * 512:(hf + 1) * 512], False, True, [sg, g2b], [tm[hf]])
                        CP("act", g_sb[:, hf * 512:(hf + 1) * 512], tm[hf][:], [tm[hf]], [g_sb])
                    ck(6)
                    S.op("dve", lambda e: e.tensor_tensor_scan(out=fl(f["lp"]), data0=cst["c_rmask"][:], data1=fl(f["sigw"]),
                                                               initial=0.0, op0=ALU.mult, op1=ALU.add),
                         [cst["c_rmask"], f["sigw"]], [f["lp"]])
                    ACT(fl(f["P"]), fl(f["lp"]), AF.Exp, [f["lp"]], [f["P"]], scale=-CDEC)
                    ACT(fl(f["iP"]), fl(f["lp"]), AF.Exp, [f["lp"]], [f["iP"]], scale=CDEC)
                    lpv = fl(f["lp"]).rearrange("p (a t) -> p a t", t=64)
                    TTo("dve", fl(f["tB"]).rearrange("p (a t) -> p a t", t=64), lpv[:, :, 63:64].to_broadcast([128, 16, 64]), lpv,
                        ALU.subtract, [f["lp"]], [f["tB"]])
                    ACT(fl(f["PC"]), fl(f["tB"]), AF.Exp, [f["tB"]], [f["PC"]], scale=-CDEC)
                    ck(7)
                    ck(8)
                    STT(f["tA"][:], f["alr"][:], -1.0, bc8(pp["k_a"]), ALU.add, ALU.mult, [f["alr"], pp["k_a"]], [f["tA"]])
                    STT(f["km"][:], f["tA"][:], 1.0, f["k"][:], ALU.add, ALU.mult, [f["tA"], f["k"]], [f["km"]])
                    TTo("dve", f["bA"][:], f["kk"][:], f["alr"][:], ALU.mult, [f["kk"], f["alr"]], [f["bA"]])

                    def v4(b):
                        return b[:].rearrange("p a (c t) -> p a c t", t=64)
                    TTo("dve", ARt[:, :, :, 1, :], v4(f["r"]), v4(f["P"]), ALU.mult, [f["r"], f["P"]], [ARt])
                    A16 = ARt[:, :, :, 0, :].rearrange("p a c t -> p (a c) t")
                    kk16 = fl(f["kk"]).rearrange("p (a t) -> p a t", t=64); P16 = fl(f["P"]).rearrange("p (a t) -> p a t", t=64)
                    STT(A16[:, :, 1:64], kk16[:, :, 1:64], -1.0, P16[:, :, 0:63], ALU.mult, ALU.mult, [f["kk"], f["P"]], [ARt])
                    TS("dve", A16[:, :, 0:1], kk16[:, :, 0:1], -1.0, None, ALU.mult, None, [f["kk"]], [ARt])
                    TTo("pool", BKt[:, :, :, 1, :], v4(f["km"]), v4(f["iP"]), ALU.mult, [f["km"], f["iP"]], [BKt])
                    TTo("dve", BKt[:, :, :, 0, :], v4(f["bA"]), v4(f["iP"]), ALU.mult, [f["bA"], f["iP"]], [BKt])
                    TTo("dve", BKh[:, :, 1, :], f["km"][:], f["PC"][:], ALU.mult, [f["km"], f["PC"]], [BKh])
                    TTo("pool", BKh[:, :, 0, :], f["bA"][:], f["PC"][:], ALU.mult, [f["bA"], f["PC"]], [BKh])
                    ck(9)
                    TTo("pool", f["tA"][:], f["r"][:], f["km"][:], ALU.mult, [f["r"], f["km"]], [f["tA"]])
                    TTo("pool", prod[:], f["tA"][:], bc8(pp["r_k"]), ALU.mult, [f["tA"], pp["r_k"]], [prod])
                    for pr in range(8):
                        MM(c2[:, pr * 2:pr * 2 + 2], prod[:, pr, :], hsel[:], True, True, [prod, hsel], [c2])
                    CP("act", rk_sb[:], c2[:, 0:16], [c2], [rk_sb])
                    ck(10)
                    def bv(bk):
                        return bk[:].rearrange("p (a b) -> p a b", b=64)
                    B = PSB
                    Pv = f["P"][:].rearrange("p a (c t) -> p a c t", t=64)
                    for g in range(2):
                        hp = 64 * g
                        if g == 0 and tt + 1 < NT:
                            pend[0] = F1g(tt + 1)
                        def b4(bk):
                            return bk[:].rearrange("p (a w t) -> p a w t", w=2, t=64)
                        for hl in range(8):
                            bx = B[0 + hl // 4]; by = B[2 + hl // 4]
                            for ch in range(2):
                                tp = 64 * ch
                                bt = BKt[hp:hp + 64, hl, ch, 0, :]; kt = BKt[hp:hp + 64, hl, ch, 1, :]
                                at = ARt[hp:hp + 64, hl, ch, 0, :]; art = ARt[hp:hp + 64, hl, ch, :, :]
                                MM(b4(bx)[tp:tp + 64, hl % 4, :, :], bt, art, True, True, [BKt, ARt], [bx])
                                MM(b4(by)[tp:tp + 64, hl % 4, :, :], kt, art, True, True, [BKt, ARt], [by])
                                MM(bv(B[4])[tp:tp + 64, hl, :], at, bt, True, True, [BKt, ARt], [B[4]])
                        TTo("dve", A_sb[0][:], bv(B[4]), m8("c_msl"), ALU.mult, [B[4], cst["c_msl"]], [A_sb[0]])
                        m4u = cst["c_msu"][:].unsqueeze(1).to_broadcast([128, 4, 64]); m4i = cst["c_miu"][:].unsqueeze(1).to_broadcast([128, 4, 64])
                        for hb2 in range(2):
                            hs = slice(hb2 * 4, hb2 * 4 + 4)
                            TTo("dve", N_sb[0][:, hs, :], b4(B[0 + hb2])[:, :, 0, :], m4u, ALU.mult, [B[0 + hb2], cst["c_msu"]], [(N_sb[0].key, "h%d" % hb2)])
                            TTo("dve", Aak[:, hs, :], b4(B[2 + hb2])[:, :, 0, :], m4u, ALU.mult, [B[2 + hb2], cst["c_msu"]], [(Aak, hb2)])
                        drip(1)
                        for hb2 in range(2):
                            hs = slice(hb2 * 4, hb2 * 4 + 4)
                            TTo("dve", Qrb[:, hs, :], b4(B[0 + hb2])[:, :, 1, :], m4i, ALU.mult, [B[0 + hb2], cst["c_miu"]], [(Qrb, hb2)])
                            TTo("dve", Qrk[:, hs, :], b4(B[2 + hb2])[:, :, 1, :], m4i, ALU.mult, [B[2 + hb2], cst["c_miu"]], [(Qrk, hb2)])
                        drip(1)
                        ck(12)
                        drip(1)
                        psTv = psT[:].rearrange("p (a w j) -> p a w j", w=2, j=64)
                        for hl in range(8):
                            for w in range(2):
                                TR(psTv[:, hl, w, :], BKh[hp:hp + 64, hl, w, :], identb[hp:hp + 64, hp:hp + 64], [BKh, identb], [psT])
                        CP("act", BKhT[:], psTv, [psT], [BKhT])
                        for ch in range(2):
                            tp = 64 * ch
                            for hl in range(8):
                                h = 2 * hl + g
                                MM(bv(B[ch])[tp:tp + 64, hl, :], Aak[tp:tp + 64, hl, :], v_sb[tp:tp + 64, h * 64:(h + 1) * 64], True, True, [Aak, v_sb], [B[ch]])
                            CP("act", X0[tp:tp + 64, :, :], bv(B[ch])[tp:tp + 64, :, :], [B[ch]], [kc(X0, ch)])
                        ck(13)
                        drip(1)
                        TTo("dve", T_sb[0][:], N_sb[0][:], m8("c_id64"), ALU.add, [N_sb[0], cst["c_id64"]], [T_sb[0]])
                        cur = 0
                        for lvl in range(5):
                            nxt = 1 - cur
                            for ch in range(2):
                                tp = 64 * ch
                                for hl in range(8):
                                    MM(bv(B[ch])[tp:tp + 64, hl, :], N_sb[cur][tp:tp + 64, hl, :], A_sb[cur][tp:tp + 64, hl, :], True, True,
                                       [kc(N_sb[cur], ch), kc(A_sb[cur], ch)], [B[ch]])
                                    if lvl < 4:
                                        MM(bv(B[2 + ch])[tp:tp + 64, hl, :], A_sb[cur][tp:tp + 64, hl, :], N_sb[cur][tp:tp + 64, hl, :], True, True,
                                           [kc(N_sb[cur], ch), kc(A_sb[cur], ch)], [B[2 + ch]])
                            for ch in range(2):
                                tp = 64 * ch
                                CP("act", A_sb[nxt][tp:tp + 64, :, :], bv(B[ch])[tp:tp + 64, :, :], [B[ch]], [kc(A_sb[nxt], ch)])
                                if lvl < 4:
                                    CP("act", N_sb[nxt][tp:tp + 64, :, :], bv(B[2 + ch])[tp:tp + 64, :, :], [B[2 + ch]], [kc(N_sb[nxt], ch)])
                            drip(1)
                            for ch in range(2):
                                tp = 64 * ch
                                for hl in range(8):
                                    MM(bv(B[4 + ch])[tp:tp + 64, hl, :], A_sb[nxt][tp:tp + 64, hl, :], T_sb[cur][tp:tp + 64, hl, :], True, True,
                                       [kc(A_sb[nxt], ch), kc(T_sb[cur], ch)], [B[4 + ch]])
                            for ch in range(2):
                                tp = 64 * ch
                                TTo("dve", T_sb[nxt][tp:tp + 64, :, :], bv(B[4 + ch])[tp:tp + 64, :, :], T_sb[cur][tp:tp + 64, :, :], ALU.add,
                                    [B[4 + ch], kc(T_sb[cur], ch)], [kc(T_sb[nxt], ch)])
                            cur = nxt
                            drip(1)
                        TTf = T_sb[cur]
                        ck(14)
                        GX, UX, HX, Y1 = B[0], B[1], B[2], B[3]
                        for ch in range(2):
                            tp = 64 * ch
                            Y2 = B[4 + ch]
                            for hl in range(8):
                                MM(bv(GX)[tp:tp + 64, hl, :], ARt[hp:hp + 64, hl, ch, 0, :], Hb[hp:hp + 64, hl, :], True, True, [ARt, Hb], [GX])
                            TTo("dve", G_sb[tp:tp + 64, :, :], bv(GX)[tp:tp + 64, :, :], X0[tp:tp + 64, :, :], ALU.add, [GX, kc(X0, ch)], [kc(G_sb, ch)])
                            for hl in range(8):
                                MM(bv(UX)[tp:tp + 64, hl, :], TTf[tp:tp + 64, hl, :], G_sb[tp:tp + 64, hl, :], True, True, [kc(TTf, ch), kc(G_sb, ch)], [UX])
                            CP("act", U_sb[tp:tp + 64, :, :], bv(UX)[tp:tp + 64, :, :], [UX], [kc(U_sb, ch)])
                            drip(1)
                            for hl in range(8):
                                MM(bv(Y1)[tp:tp + 64, hl, :], ARt[hp:hp + 64, hl, ch, 1, :], Hb[hp:hp + 64, hl, :], True, True, [ARt, Hb], [Y1])
                            for hl in range(8):
                                h = 2 * hl + g
                                MM(bv(Y2)[tp:tp + 64, hl, :], Qrb[tp:tp + 64, hl, :], U_sb[tp:tp + 64, hl, :], True, False, [Qrb, kc(U_sb, ch)], [Y2])
                                MM(bv(Y2)[tp:tp + 64, hl, :], Qrk[tp:tp + 64, hl, :], v_sb[tp:tp + 64, h * 64:(h + 1) * 64], False, True, [Qrk, v_sb], [Y2])
                            for hl in range(8):
                                h = 2 * hl + g
                                MM(bv(HX)[hp:hp + 64, hl, :], BKhT[tp:tp + 64, hl, 0, :], U_sb[tp:tp + 64, hl, :], True, False, [BKhT, kc(U_sb, ch)], [HX])
                                MM(bv(HX)[hp:hp + 64, hl, :], BKhT[tp:tp + 64, hl, 1, :], v_sb[tp:tp + 64, h * 64:(h + 1) * 64], False, True, [BKhT, v_sb], [HX])
                            Hg = H32[hp:hp + 64, :, :]
                            TTo("dve", Hg, Hg, Pv[hp:hp + 64, :, ch, 63:64].to_broadcast([64, 8, 64]), ALU.mult, [H32, f["P"]], [H32])
                            TTo("dve", Hg, Hg, bv(HX)[hp:hp + 64, :, :], ALU.add, [H32, HX], [H32])
                            CP("act", Hb[hp:hp + 64, :, :], Hg, [H32], [Hb])
                        Yg = fl(f["sigw"]).rearrange("p (a g b) -> p a g b", g=2, b=64)[:, :, g, :]
                        CP("act", Yg, bv(Y1), [Y1], [f["sigw"]])
                        for ch in range(2):
                            tp = 64 * ch
                            TTo("dve", Yg[tp:tp + 64], Yg[tp:tp + 64], bv(B[4 + ch])[tp:tp + 64, :, :], ALU.add, [f["sigw"], B[4 + ch]], [f["sigw"]])
                    ck(15)
                    drip(1000)
                    if tt + 1 < NT:
                        F2a()
                    YK, TK, XK = f["sigw"], f["tA"], f["lp"]
                    Yf = fl(YK); Y3 = Yf.rearrange("p (a b) -> p a b", b=64)
                    tmpf = fl(TK); tmp3 = tmpf.rearrange("p (a b) -> p a b", b=64)
                    x1f = fl(XK)
                    S.op("dve", lambda e: e.tensor_reduce(out=st[:, 0, :], in_=Y3, axis=AX.X, op=ALU.add), [YK], [st])
                    ACT(tmpf, Yf, AF.Square, [YK], [TK])
                    S.op("dve", lambda e: e.tensor_reduce(out=st[:, 1, :], in_=tmp3, axis=AX.X, op=ALU.add), [TK], [st])
                    TS("dve", st[:, 0, :], st[:, 0, :], 1.0 / 64, None, ALU.mult, None, [st], [st])
                    TTo("dve", st[:, 2, :], st[:, 0, :], st[:, 0, :], ALU.mult, [st], [st])
                    STT(st[:, 1, :], st[:, 1, :], 1.0 / 64, st[:, 2, :], ALU.mult, ALU.subtract, [st], [st])
                    ACT(st[:, 1, :], st[:, 1, :], AF.Sqrt, [st], [st], bias=64e-5, scale=1.0)
                    S.op("dve", lambda e: e.reciprocal(out=st[:, 1, :], in_=st[:, 1, :]), [st], [st])
                    TTo("dve", Y3, Y3, st[:, 0, :].unsqueeze(2).to_broadcast([128, 16, 64]), ALU.subtract, [YK, st], [YK])
                    TTo("dve", Y3, Y3, st[:, 1, :].unsqueeze(2).to_broadcast([128, 16, 64]), ALU.mult, [YK, st], [YK])
                    TTo("pool", Yf, Yf, lnxg[:], ALU.mult, [YK, lnxg], [YK])
                    TTo("pool", Yf, Yf, lnxb[:], ALU.add, [YK, lnxb], [YK])
                    TTo("dve", tmp3, v_sb[:].rearrange("p (a b) -> p a b", b=64),
                        rk_sb[:].unsqueeze(2).to_broadcast([128, 16, 64]), ALU.mult, [v_sb, rk_sb], [TK])
                    TTo("dve", tmpf, tmpf, Yf, ALU.add, [TK, YK], [TK])
                    zb = View(prod, lambda: prod[:].rearrange("p a b -> p (a b)")); zT = View(BKh, lambda: BKh[:, :, 0, :])
                    TTo("dve", zb[:], tmpf, g_sb[:], ALU.mult, [TK, g_sb], [zb])
                    transpose8(zb, zT[:], dstbuf=zT)
                    for hf in range(2):
                        for dc in range(8):
                            MM(tm[hf][:], zT[:, dc, :], wo[:, dc, hf * 512:(hf + 1) * 512], dc == 0, dc == 7, [zT, wo], [tm[hf]])
                        TTo("dve", x1f[:, hf * 512:(hf + 1) * 512], tm[hf][:], gt0[:, hf * 512:(hf + 1) * 512], ALU.mult, [tm[hf], gt0], [XK])
                    S.dma("sp", fl(f["P"]), I["xv"][tt * 128:(tt + 1) * 128, :], writes=[f["P"]])
                    TTo("pool", x1f, x1f, fl(f["P"]), ALU.add, [XK, f["P"]], [XK])
                    S.dma("sp", XS[tt * 128:(tt + 1) * 128, :], x1f, reads=[XK], writes=[("XS", tt)])
                    if tt == NT - 1:
                        tap("x1_last", x1f, [XK])
                  except StopBuild:
                    break
                S.barrier()
            S.es = es0

        def ffn_stage(L, tile0, tile1, to_out):
            with contextlib.ExitStack() as es:
                S.es = es
                sh, gs, gt = ada_rows(es, I["w_ada"][2 * L + 1], I["b_ada"][2 * L + 1], 3, I["norm_g"][2 * L + 1])
                wg = S.sb([128, 8, FF], BF16, "wg"); wu = S.sb([128, 8, FF], BF16, "wu"); wd = S.sb([128, NFC, D], BF16, "wd")
                gv = I["ffn_g"][L].rearrange("(dc p) n -> p dc n", p=128); uv = I["ffn_u"][L].rearrange("(dc p) n -> p dc n", p=128)
                dv = I["ffn_d"][L].rearrange("(fc p) n -> p fc n", p=128)
                for dc in range(8):
                    S.dma("pool", wg[:, dc, :], gv[:, dc, :], writes=[wg]); S.dma("pool", wu[:, dc, :], uv[:, dc, :], writes=[wu])
                for fc in range(0, NFC, 2):
                    S.dma("pool", wd[:, fc:fc + 2, :], dv[:, fc:fc + 2, :], writes=[wd])
                NS = 4
                TW = NS * 128
                xt = S.sb([128, NS, D], F32, "fx"); tmpk = S.sb([128, D], F32, "ftmp"); ss = S.sb([128, 4], F32, "fss")
                hb = S.sb([128, D], BF16, "fhb"); hT = S.sb([128, 8, TW], BF16, "fhT")
                act = S.sb([128, NFC, TW], BF16, "fact"); sls = [S.sb([128, TW], BF16, "fsl%d" % i) for i in range(2)]
                x2 = S.sb([128, D], F32, "fx2"); xr = tmpk
                nt4 = (tile1 - tile0) // NS

                def front(t4):
                    for sub in range(NS):
                        tt = tile0 + t4 * NS + sub
                        xs_v = View((xt, sub), lambda sub=sub: xt[:, sub, :])
                        S.dma("sp", xt[:, sub, :], XS[tt * 128:(tt + 1) * 128, :], reads=[("XS", tt)], writes=[(xt, sub)])
                        norm_tile(xs_v, gs, sh, hb, tmpk, ss)
                        for dc in range(8):
                            TR(psT[:, dc * 128:(dc + 1) * 128], hb[:, dc * 128:(dc + 1) * 128], identb[:], [hb, identb], [psT])
                        CP("act", hT[:, :, sub * 128:(sub + 1) * 128], psT[:].rearrange("p (a b) -> p a b", b=128), [psT], [(hT, sub)])

                def back(t4):
                    for fc in range(NFC):
                        pg = PSB[fc % 2]; pu = PSB[2 + fc % 2]
                        for dc in range(8):
                            MM(pg[:], wg[:, dc, fc * 128:(fc + 1) * 128], hT[:, dc, :], dc == 0, dc == 7, [wg, hT], [pg])
                        for dc in range(8):
                            MM(pu[:], wu[:, dc, fc * 128:(fc + 1) * 128], hT[:, dc, :], dc == 0, dc == 7, [wu, hT], [pu])
                        sl = sls[fc % 2]
                        ACT(sl[:], pg[:], AF.Silu, [pg], [sl])
                        TTo("dve", act[:, fc, :], pu[:], sl[:], ALU.mult, [pu, sl], [(act, fc)])
                    if t4 + 1 < nt4:
                        front(t4 + 1)
                    for sub in range(NS):
                        tt = tile0 + t4 * NS + sub
                        S.dma("act", xr[:], XS[tt * 128:(tt + 1) * 128, :], reads=[("XS", tt)], writes=[xr])
                        for hf in range(2):
                            po = PSB[4 + hf]
                            for fc in range(NFC):
                                MM(po[:], act[:, fc, sub * 128:(sub + 1) * 128], wd[:, fc, hf * 512:(hf + 1) * 512], fc == 0, fc == NFC - 1, [(act, fc), wd], [po])
                            TTo("dve", x2[:, hf * 512:(hf + 1) * 512], po[:], gt[:, hf * 512:(hf + 1) * 512], ALU.mult, [po, gt], [x2])
                        TTo("pool", x2[:], x2[:], xr[:], ALU.add, [x2, xr], [x2])
                        if to_out:
                            S.dma("sp", out[(tt - tile0) * 128:(tt - tile0 + 1) * 128, :], x2[:], reads=[x2], writes=[("OUT", tt)])
                        else:
                            S.dma("sp", XS[tt * 128:(tt + 1) * 128, :], x2[:], reads=[x2], writes=[("XS", tt)])
                        if tt == tile1 - 1:
                            tap("ffn%d_last" % L, x2[:], [x2])
                front(0)
                for t4 in range(nt4):
                    back(t4)
                S.barrier()
            S.es = es0

        if "C" in stages:
            ffn_stage(0, 0, NT, False)

        VS2 = nc.dram_tensor("v2_scr", [8, T, 130], BF16, kind="Internal").ap()
        kmean = S.sb([128, 8, NB], BF16, "kmean")
        hnorm_tmp = {}

        def head_rmsnorm(src_ps_halves, grow, dst_f32_key, dst_ap, sq_key, sq_ap, st, extra_scale=1.0):
            for hf in range(2):
                ACT(sq_ap[:, hf * 512:(hf + 1) * 512], src_ps_halves[hf][:], AF.Square, [src_ps_halves[hf]], [sq_key])
            S.op("dve", lambda e: e.tensor_reduce(out=st[:, 0, :], in_=sq_ap.rearrange("p (a b) -> p a b", b=64), axis=AX.X, op=ALU.add), [sq_key], [st])
            ACT(st[:, 1, :], st[:, 0, :], AF.Sqrt, [st], [st], bias=1e-6, scale=1.0 / 64)
            S.op("dve", lambda e: e.reciprocal(out=st[:, 1, :], in_=st[:, 1, :]), [st], [st])
            if extra_scale != 1.0:
                TS("dve", st[:, 1, :], st[:, 1, :], extra_scale, None, ALU.mult, None, [st], [st])
            for hf in range(2):
                TTo("dve", dst_ap[:, hf * 512:(hf + 1) * 512].rearrange("p (a b) -> p a b", b=64),
                    src_ps_halves[hf][:].rearrange("p (a b) -> p a b", b=64),
                    st[:, 1, hf * 8:(hf + 1) * 8].unsqueeze(2).to_broadcast([128, 8, 64]), ALU.mult, [src_ps_halves[hf], st], [dst_f32_key])
            TTo("pool", dst_ap, dst_ap, grow[:], ALU.mult, [dst_f32_key, grow], [dst_f32_key])

        if "D" in stages:
            with contextlib.ExitStack() as es:
                S.es = es
                shk, gsk = ada_rows(es, I["kv_w_ada"], I["kv_b_ada"], 2, I["kv_norm_g"])
                wk = S.sb([128, 8, D], BF16, "wk"); wv = S.sb([128, 8, D], BF16, "wv")
                S.dma("pool", wk[:], I["kv_wk"].rearrange("(dc p) n -> p dc n", p=128), writes=[wk])
                S.dma("pool", wv[:], I["kv_wv"].rearrange("(dc p) n -> p dc n", p=128), writes=[wv])
                kng = S.sb([128, D], F32, "kng"); S.dma("sp", kng[:], I["k_norm_g"].partition_broadcast(128), writes=[kng])
                onesc = S.sb([128, 1], F32, "onesc"); S.op("dve", lambda e: e.memset(onesc[:], 1.0 / 256), (), [onesc])
                xts = [S.sb([128, D], F32, "dx_%d" % i) for i in range(2)]; tmpk = S.sb([128, D], F32, "dtmp"); ss = S.sb([128, 4], F32, "dss")
                hb = S.sb([128, D], BF16, "dhb"); hTs = [S.sb([128, 8, 128], BF16, "dhT%d" % i) for i in range(2)]
                kf = S.sb([128, D], F32, "kf"); kb = S.sb([128, D], BF16, "kb"); sq = S.sb([128, D], F32, "ksq")
                st = S.sb([128, 2, 16], F32, "kst"); kTs = [S.sb([128, 8, 128], BF16, "kT%d" % i) for i in range(2)]
                vexts = [S.sb([128, 16, 65], BF16, "vext%d" % i) for i in range(2)]
                for v_ in vexts:
                    S.op("dve", lambda e: e.memset(v_[:], 1.0), (), [v_])
                kmacc = S.sb([128, 8], F32, "kmacc")

                def dfront(tt):
                    xt = xts[tt % 2]; hT = hTs[tt % 2]
                    S.dma("sp", xt[:], XS[tt * 128:(tt + 1) * 128, :], reads=[("XS", tt)], writes=[xt])
                    norm_tile(xt, gsk, shk, hb, tmpk, ss)
                    transpose8(hb, hT[:], dstbuf=hT)

                kfs = [kf, S.sb([128, D], F32, "kf2")]; kbs = [kb, S.sb([128, D], BF16, "kb2")]

                def dmm(tt):
                    hT = hTs[tt % 2]; vext = vexts[tt % 2]
                    KB = [PSB[0], PSB[1]] if tt % 2 == 0 else [PSB[5], PSB[6]]
                    for hf in range(2):
                        for dc in range(8):
                            MM(KB[hf][:], hT[:, dc, :], wk[:, dc, hf * 512:(hf + 1) * 512], dc == 0, dc == 7, [hT, wk], [KB[hf]])
                    for hf in range(2):
                        for dc in range(8):
                            MM(PSB[2 + hf][:], hT[:, dc, :], wv[:, dc, hf * 512:(hf + 1) * 512], dc == 0, dc == 7, [hT, wv], [PSB[2 + hf]])
                        CP("act", vext[:, hf * 8:(hf + 1) * 8, 0:64], PSB[2 + hf][:].rearrange("p (a b) -> p a b", b=64), [PSB[2 + hf]], [vext])

                def dhead(tt):
                    KB = [PSB[0], PSB[1]] if tt % 2 == 0 else [PSB[5], PSB[6]]
                    kf_ = kfs[tt % 2]; kb_ = kbs[tt % 2]
                    head_rmsnorm(KB, kng, kf_, kf_[:], sq, sq[:], st)
                    CP("act", kb_[:], kf_[:], [kf_], [kb_])

                def dpost(tt):
                    kT = kTs[tt % 2]; vext = vexts[tt % 2]; kf_ = kfs[tt % 2]; kb_ = kbs[tt % 2]
                    for pr in range(8):
                        MM(PSB[4][:, pr:pr + 1], kf_[:, pr * 128:(pr + 1) * 128], onesc[:], True, True, [kf_, onesc], [PSB[4]])
                    if tt % 2 == 0:
                        CP("dve", kmacc[:], PSB[4][:, 0:8], [PSB[4]], [kmacc])
                    else:
                        TTo("dve", kmean[:, :, tt // 2], PSB[4][:, 0:8], kmacc[:], ALU.add, [PSB[4], kmacc], [kmean])
                    transpose8(kb_, kT[:], dstbuf=kT)
                    S.dma("sp", KT[:, :, tt * 128:(tt + 1) * 128].rearrange("a p t -> p a t"), kT[:], reads=[kT], writes=[("KT", tt)])
                    S.dma("sp", VS2[:, tt * 128:(tt + 1) * 128, :].rearrange("a p c -> p a c"),
                          vext[:].rearrange("p (a g) c -> p a (g c)", g=2), reads=[vext], writes=[("VS2", tt)])
                    if tt == NT - 1:
                        tap("kf_last", kf_[:], [kf_])
                dfront(0)
                for tt in range(NT):
                    dmm(tt)
                    if tt + 1 < NT:
                        dfront(tt + 1)
                    dhead(tt)
                    if tt >= 1:
                        dpost(tt - 1)
                dpost(NT - 1)
                S.barrier()
            S.es = es0

        if "E" in stages:
            NQT = NT // 2
            NQB = NB // 2
            gt1p = S.sb([128, D], F32, "gt1p")
            with contextlib.ExitStack() as es:
                S.es = es
                sh1, gs1, gt1 = ada_rows(es, I["w_ada"][2], I["b_ada"][2], 3, I["norm_g"][2])
                CP("pool", gt1p[:], gt1[:], [gt1], [gt1p])
                wq = S.sb([128, 8, D], BF16, "wq")
                S.dma("pool", wq[:], I["mb_wq"].rearrange("(dc p) n -> p dc n", p=128), writes=[wq])
                qng = S.sb([128, D], F32, "qng"); S.dma("sp", qng[:], I["q_norm_g"].partition_broadcast(128), writes=[qng])
                bb = S.sb([128, NB], F32, "bb"); S.dma("sp", bb[:], I["bbias"], writes=[bb])
                fut = S.sb([128, NB, NB], F32, "fut"); S.dma("sp", fut[:], I["c_fut"], writes=[fut])
                xts = [S.sb([128, D], F32, "ex%d" % i) for i in range(2)]; tmpk = S.sb([128, D], F32, "etmp"); ss = S.sb([128, 4], F32, "ess")
                hb = S.sb([128, D], BF16, "ehb"); hTs = [S.sb([128, 8, 128], BF16, "ehT%d" % i) for i in range(2)]
                qf = S.sb([128, D], F32, "qf"); qb_ = S.sb([128, D], BF16, "qb"); sq = S.sb([128, D], F32, "qsq")
                st = S.sb([128, 2, 16], F32, "qst"); qTs = [S.sb([128, 8, 128], BF16, "qT%d" % i) for i in range(2)]
                gsb = S.sb([128, 16, NB], F32, "gsb"); m8t = S.sb([128, 16, 8], F32, "m8t"); selts = [S.sb([128, 16, NB], F32, "selt%d" % i) for i in range(2)]
                sel2 = S.sb([128, 16, NB], F32, "sel2"); brow = S.sb([128, NB], F32, "browq")

                def efront(qt):
                    tt = NQT + qt
                    xt = xts[qt % 2]; hT = hTs[qt % 2]
                    S.dma("sp", xt[:], XS[tt * 128:(tt + 1) * 128, :], reads=[("XS", tt)], writes=[xt])
                    norm_tile(xt, gs1, sh1, hb, tmpk, ss)
                    transpose8(hb, hT[:], dstbuf=hT)

                qfs = [qf, S.sb([128, D], F32, "qf2")]; qbs = [qb_, S.sb([128, D], BF16, "qb2")]

                def emm(qt):
                    hT = hTs[qt % 2]
                    QB = [PSB[0], PSB[1]] if qt % 2 == 0 else [PSB[5], PSB[6]]
                    for hf in range(2):
                        for dc in range(8):
                            MM(QB[hf][:], hT[:, dc, :], wq[:, dc, hf * 512:(hf + 1) * 512], dc == 0, dc == 7, [hT, wq], [QB[hf]])

                def ehead(qt):
                    QB = [PSB[0], PSB[1]] if qt % 2 == 0 else [PSB[5], PSB[6]]
                    qf_ = qfs[qt % 2]; qb2_ = qbs[qt % 2]
                    head_rmsnorm(QB, qng, qf_, qf_[:], sq, sq[:], st, extra_scale=0.125)
                    CP("act", qb2_[:], qf_[:], [qf_], [qb2_])

                def epost(qt):
                    tt = NQT + qt
                    vb = tt // 2
                    qT = qTs[qt % 2]; selt = selts[qt % 2]; qf_ = qfs[qt % 2]; qb2_ = qbs[qt % 2]
                    transpose8(qb2_, qT[:], dstbuf=qT)
                    S.dma("sp", QT[:, :, qt * 128:(qt + 1) * 128].rearrange("a p t -> p a t"), qT[:], reads=[qT], writes=[("QT", qt)])
                    for g in range(2):
                        hp = 64 * g
                        for pr in range(8):
                            MM(PSB[2 + g][:, pr * NB:(pr + 1) * NB], qT[hp:hp + 64, pr, :], kmean[hp:hp + 64, pr, :], True, True, [qT, kmean], [PSB[2 + g]])
                    TTo("dve", brow[:], bb[:], fut[:, vb, :], ALU.add, [bb, fut], [brow])
                    gv4 = gsb[:].rearrange("p (a g) n -> p a g n", g=2)
                    for g in range(2):
                        TTo("dve", gv4[:, :, g, :], PSB[2 + g][:, 0:8 * NB].rearrange("p (a n) -> p a n", n=NB),
                            brow[:].unsqueeze(1).to_broadcast([128, 8, NB]), ALU.add, [PSB[2 + g], brow], [gsb])
                    for h in range(16):
                        S.op("dve", lambda e: e.max(out=m8t[:, h, :], in_=gsb[:, h, :]), [gsb], [(m8t, h)])
                    TTo("dve", selt[:], gsb[:], m8t[:, :, 2:3].to_broadcast([128, 16, NB]), ALU.is_ge, [gsb, m8t], [selt])
                    TS("pool", sel2[:], gsb[:], -1.0e29, None, ALU.is_gt, None, [gsb], [sel2])
                    TTo("dve", selt[:], selt[:], sel2[:], ALU.mult, [selt, sel2], [selt])
                    S.dma("sp", SEL[qt * 128:(qt + 1) * 128, :], selt[:].rearrange("p a n -> p (a n)"), reads=[selt], writes=[("SEL", qt)])
                    if qt == NQT - 1:
                        tap("sel_last", selt[:].rearrange("p a n -> p (a n)"), [selt])
                        tap("qf_last", qf_[:], [qf_])
                efront(0)
                for qt in range(NQT):
                    emm(qt)
                    if qt + 1 < NQT:
                        efront(qt + 1)
                    ehead(qt)
                    if qt >= 1:
                        epost(qt - 1)
                epost(NQT - 1)
                S.barrier()
            S.es = es0
            with contextlib.ExitStack() as es:
                S.es = es
                Kp = S.sb([128, T], BF16, "Kp"); Vp = S.sb([128, NT, 130], BF16, "Vp"); Qp = S.sb([128, TH], BF16, "Qp")
                caus = S.sb([128, 512], BF16, "caus"); S.dma("pool", caus[:], I["c_causal"].rearrange("p a b -> p (a b)"), writes=[caus])
                accs = [S.sb([128, 2, 2, 65], F32, "acc%d" % i) for i in range(2)]; pt = [S.sb([128, 512], BF16, "pt%d" % i) for i in range(6)]
                selps = [S.sb([128, 2, 2, NB], F32, "selp%d" % i) for i in range(2)]
                rc = S.sb([128, 4], F32, "rc"); ob = S.sb([128, 2, 128], BF16, "ob")
                SELv = SEL.rearrange("(q p) (h n) -> p q h n", p=128, n=NB)
                for pr in range(8):
                    S.dma("sp", Kp[:], KT[pr], reads=["KT"], writes=[Kp])
                    S.dma("sp", Vp[:], VS2[pr].rearrange("(n p) c -> p n c", p=128), reads=["VS2"], writes=[Vp])
                    S.dma("sp", Qp[:], QT[pr], reads=["QT"], writes=[Qp])
                    Vp4 = Vp[:].rearrange("p n (g c) -> p n g c", g=2)
                    for qb in range(NQB):
                        vb = NQB + qb
                        acc = accs[qb % 2]; selp = selps[qb % 2]
                        S.dma("act", selp[:], SELv[:, 2 * qb:2 * qb + 2, 2 * pr:2 * pr + 2, :], reads=["SEL"], writes=[selp])
                        S.op("pool", lambda e: e.memset(acc[:], 0.0), (), [acc])
                        SB6 = [PSB[0], PSB[1], PSB[2], PSB[3], PSB[4], PSB[5]]; OB2 = [PSB[6], psTf]

                        def emitS(n):
                            for kt in range(2):
                                for g in range(2):
                                    hp = 64 * g
                                    sb_ = SB6[(n % 3) * 2 + g]
                                    MM(sb_[:, kt * 256:(kt + 1) * 256], Kp[hp:hp + 64, n * 256 + kt * 128:n * 256 + (kt + 1) * 128],
                                       Qp[hp:hp + 64, qb * 256:(qb + 1) * 256], True, True, [Kp, Qp], [sb_])

                        def emitRest(n):
                            own = (n == vb)
                            for g in range(2):
                                sb_ = SB6[(n % 3) * 2 + g]; p_ = pt[(n % 3) * 2 + g]
                                ACT(p_[:], sb_[:], AF.Exp, [sb_], [p_])
                                if own:
                                    TTo("pool", p_[:], p_[:], caus[:], ALU.mult, [p_, caus], [p_])
                            for g in range(2):
                                p_ = pt[(n % 3) * 2 + g]; ob_ = OB2[g]
                                for qtl in range(2):
                                    for kt in range(2):
                                        MM(ob_[:, qtl * 65:(qtl + 1) * 65], p_[:, kt * 256 + qtl * 128:kt * 256 + (qtl + 1) * 128],
                                           Vp4[:, n * 2 + kt, g, :], kt == 0, kt == 1, [p_, Vp], [ob_])
                            for g in range(2):
                                ob_ = OB2[g]
                                for qtl in range(2):
                                    if own:
                                        TTo("dve", acc[:, qtl, g, :], ob_[:, qtl * 65:(qtl + 1) * 65], acc[:, qtl, g, :], ALU.add, [ob_, (acc, qtl * 2 + g)], [(acc, qtl * 2 + g)])
                                    else:
                                        STT(acc[:, qtl, g, :], ob_[:, qtl * 65:(qtl + 1) * 65], selp[:, qtl, g, n:n + 1], acc[:, qtl, g, :],
                                            ALU.mult, ALU.add, [ob_, selp, (acc, qtl * 2 + g)], [(acc, qtl * 2 + g)])
                        emitS(0)
                        if vb >= 1:
                            emitS(1)
                        for n in range(vb + 1):
                            if n + 2 <= vb:
                                emitS(n + 2)
                            emitRest(n)
                        S.op("dve", lambda e: e.reciprocal(out=rc[:].rearrange("p (a b) -> p a b", b=2), in_=acc[:, :, :, 64]), [acc], [rc])
                        TTo("dve", ob[:].rearrange("p q (g c) -> p q g c", g=2), acc[:, :, :, 0:64],
                            rc[:].rearrange("p (a b) -> p a b", b=2).unsqueeze(3).to_broadcast([128, 2, 2, 64]), ALU.mult, [acc, rc], [ob])
                        S.dma("pool", OS[qb * 256:(qb + 1) * 256, pr * 128:(pr + 1) * 128].rearrange("(q p) c -> p q c", p=128), ob[:],
                              reads=[ob], writes=[("OS", pr * 1000 + qb)])
                S.barrier()
            S.es = es0
            with contextlib.ExitStack() as es:
                S.es = es
                gt1 = gt1p
                wo1 = S.sb([128, 8, D], BF16, "wo1")
                S.dma("pool", wo1[:], I["mb_wo"].rearrange("(dc p) n -> p dc n", p=128), writes=[wo1])
                xts = [S.sb([128, D], F32, "e3x%d" % i) for i in range(2)]; ots = [S.sb([128, D], BF16, "e3o%d" % i) for i in range(2)]
                oTs = [S.sb([128, 8, 128], BF16, "e3oT%d" % i) for i in range(2)]
                x3s = [S.sb([128, D], F32, "e3x3%d" % i) for i in range(2)]

                def ffront(qt):
                    tt = NQT + qt
                    xt = xts[qt % 2]; ot = ots[qt % 2]; oT = oTs[qt % 2]
                    S.dma("sp", xt[:], XS[tt * 128:(tt + 1) * 128, :], reads=[("XS", tt)], writes=[xt])
                    S.dma("sp", ot[:], OS[qt * 128:(qt + 1) * 128, :], reads=["OS"], writes=[ot])
                    transpose8(ot, oT[:], dstbuf=oT)

                def fback(qt):
                    tt = NQT + qt
                    xt = xts[qt % 2]; oT = oTs[qt % 2]; x3 = x3s[qt % 2]
                    for hf in range(2):
                        for dc in range(8):
                            MM(PSB[hf][:], oT[:, dc, :], wo1[:, dc, hf * 512:(hf + 1) * 512], dc == 0, dc == 7, [oT, wo1], [PSB[hf]])
                        if hf == 0 and qt + 1 < NQT:
                            ffront(qt + 1)
                        TTo("dve", x3[:, hf * 512:(hf + 1) * 512], PSB[hf][:], gt1[:, hf * 512:(hf + 1) * 512], ALU.mult, [PSB[hf], gt1], [x3])
                    TTo("pool", x3[:], x3[:], xt[:], ALU.add, [x3, xt], [x3])
                    S.dma("sp", XS[tt * 128:(tt + 1) * 128, :], x3[:], reads=[x3], writes=[("XS", tt)])
                    if qt == NQT - 1:
                        tap("x3_last", x3[:], [x3])
                ffront(0)
                for qt in range(NQT):
                    fback(qt)
                S.barrier()
            S.es = es0

        if "F" in stages:
            ffn_stage(1, NT // 2, NT, True)

        S.finish()
        print("ninst", S.ninst, "nwait", S.nwait)
    return nc


def _consts():
    p = np.arange(128)
    c = {}
    c["c_ident"] = np.eye(128, dtype=np.float32)
    rm = np.ones((128, 1024), np.float32); rm[:, ::64] = 0.0
    c["c_rmask"] = rm
    t = np.arange(64)
    pm = (p % 64)[:, None]
    c["c_msu"] = (pm < t[None, :]).astype(np.float32)
    c["c_msl"] = (t[None, :] < pm).astype(np.float32)
    c["c_miu"] = (pm <= t[None, :]).astype(np.float32)
    c["c_id64"] = (pm == t[None, :]).astype(np.float32)
    c["c_bones"] = ((p[:, None] // 64) == (p[None, :] // 64)).astype(np.float32)
    c["c_hsel"] = ((p[:, None] // 64) == np.arange(2)[None, :]).astype(np.float32)
    q = np.arange(256)
    cz = np.zeros((128, 2, 256), np.float32)
    for kt in range(2):
        cz[:, kt, :] = ((kt * 128 + p)[:, None] <= q[None, :])
    c["c_causal"] = cz
    return c


def prep_shared(inp):
    f = lambda a: np.ascontiguousarray(np.asarray(a, dtype=np.float32))
    pp = lambda v: f(np.asarray(v).reshape(8, 128).T)
    sh = {}
    sh["w_ada"] = f(np.asarray(inp["w_ada"]).reshape(4, D, 3 * D))
    sh["b_ada"] = f(np.asarray(inp["b_ada"]).reshape(4, 3 * D))
    sh["norm_g"] = f(np.asarray(inp["norm_g"]).reshape(4, D))
    sh["mu"] = f(np.asarray(inp["rw_mu"])[0].reshape(6, 8, 128).transpose(2, 0, 1))
    sh["w_rkv"] = f(np.asarray(inp["rw_w_rkv"])[0])
    sh["w0"] = pp(inp["rw_w0"][0]); sh["a0"] = pp(inp["rw_a0"][0]); sh["k_k"] = pp(inp["rw_k_k"][0])
    sh["k_a"] = pp(inp["rw_k_a"][0]); sh["r_k"] = pp(np.asarray(inp["rw_r_k"])[0].reshape(-1))
    sh["w1"] = f(inp["rw_w1"][0]); sh["w2"] = f(inp["rw_w2"][0]); sh["a1"] = f(inp["rw_a1"][0]); sh["a2"] = f(inp["rw_a2"][0])
    sh["g1"] = f(inp["rw_g1"][0]); sh["g2"] = f(inp["rw_g2"][0])
    sh["lnx_g"] = f(inp["rw_lnx_g"][0]); sh["lnx_b"] = f(inp["rw_lnx_b"][0]); sh["rw_wo"] = f(inp["rw_w_o"][0])
    sh["ffn_g"] = f(inp["ffn_w_gate"]); sh["ffn_u"] = f(inp["ffn_w_up"]); sh["ffn_d"] = f(inp["ffn_w_down"])
    sh["kv_norm_g"] = f(inp["kv_norm_g"]); sh["kv_w_ada"] = f(inp["kv_w_ada"]); sh["kv_b_ada"] = f(inp["kv_b_ada"])
    sh["kv_wk"] = f(inp["kv_w_k"]); sh["kv_wv"] = f(inp["kv_w_v"])
    sh["k_norm_g"] = f(np.tile(np.asarray(inp["k_norm_g"]), NH)); sh["q_norm_g"] = f(np.tile(np.asarray(inp["mb_q_norm_g"])[0], NH))
    sh["mb_wq"] = f(inp["mb_w_q"][0]); sh["mb_wo"] = f(inp["mb_w_o"][0])
    sh.update(_consts())
    return sh


def prep_core(x_b, c_b, hf, T):
    TH = T // 2
    NT, NB = T // 128, T // 256
    m = {}
    if hf == 1:
        xv = np.asarray(x_b, dtype=np.float32)
        valid = np.ones(T, np.float32)
    else:
        xv = np.concatenate([np.zeros((TH, D), np.float32), np.asarray(x_b[:TH], dtype=np.float32)], 0)
        valid = np.concatenate([np.zeros(TH, np.float32), np.ones(TH, np.float32)])
    m["xv"] = np.ascontiguousarray(xv)
    m["valid"] = np.ascontiguousarray(valid.reshape(NT, 128).T)
    bb = np.where(valid.reshape(NB, 256)[:, 0] > 0, 0.0, NEG).astype(np.float32)
    m["bbias"] = np.ascontiguousarray(np.tile(bb[None, :], (128, 1)))
    m["cvec"] = np.ascontiguousarray(np.asarray(c_b, dtype=np.float32).reshape(8, 128).T)
    fu = np.where(np.arange(NB)[None, :] >= np.arange(NB)[:, None], NEG, 0.0).astype(np.float32)
    m["c_fut"] = np.ascontiguousarray(np.tile(fu[None], (128, 1, 1)))
    return m


_NC_CACHE = {}


def kernel(**inputs):
    x = np.asarray(inputs["x"], dtype=np.float32)
    c = np.asarray(inputs["c"], dtype=np.float32)
    Bn, T, _ = x.shape
    TH = T // 2
    key = (T,)
    if key not in _NC_CACHE:
        _NC_CACHE[key] = build(T)
    nc = _NC_CACHE[key]
    sh = prep_shared(inputs)
    in_maps = []
    for b in range(Bn):
        for hf in range(2):
            m = dict(sh)
            m.update(prep_core(x[b], c[b], hf, T))
            in_maps.append(m)
    res = run_bass_kernel_spmd(nc, in_maps, core_ids=list(range(2 * Bn)))
    outp = np.empty((Bn, T, D), np.float32)
    for b in range(Bn):
        for hf in range(2):
            outp[b, hf * TH:(hf + 1) * TH] = res.results[b * 2 + hf]["out"]
    return outp
```

```python
import contextlib
import numpy as np
import concourse.bass as bass
import concourse.mybir as mybir
from concourse.bass_utils import run_bass_kernel_spmd

F32 = mybir.dt.float32
BF16 = mybir.dt.bfloat16
AF = mybir.ActivationFunctionType
ALU = mybir.AluOpType
AX = mybir.AxisListType

D = 1024
NH = 16
HD = 64
FF = 2816
NFC = FF // 128
CDEC = float(np.exp(-0.5))
NEG = -1.0e30
NO_SELF_SYNC = ()


class Buf:
    __slots__ = ("t", "name")

    def __init__(self, t, name):
        self.t = t
        self.name = name

    def __getitem__(self, k):
        return self.t[k]


class View:
    def __init__(self, key, fn):
        self.key = key
        self.fn = fn

    def __getitem__(self, k):
        return self.fn()[k]


class TokSet:
    __slots__ = ("d",)

    def __init__(self):
        self.d = {}

    def add(self, tok):
        sem, val = tok
        k = id(sem)
        if k not in self.d or self.d[k][1] < val:
            self.d[k] = (sem, val)

    def items_(self):
        return list(self.d.values())


class Sched:
    def __init__(self, nc, es, ndma=10):
        self.nc = nc
        self.es = es
        self.eng = {"pe": nc.tensor, "act": nc.scalar, "dve": nc.vector, "pool": nc.gpsimd, "sp": nc.sync}
        self.esem = {k: es.enter_context(nc.semaphore("e_" + k)) for k in self.eng}
        self.ecnt = {k: 0 for k in self.eng}
        self.seen = {k: {} for k in self.eng}
        self.lastw = {}
        self.readers = {}
        self.dpool = {}
        self.dnext = {}
        self.dcnt = {}
        for q in ("sp", "pool", "act"):
            self.dpool[q] = [es.enter_context(nc.semaphore("d_%s%d" % (q, i))) for i in range(ndma)]
            self.dnext[q] = 0
        self.nbuf = 0
        self.parts = {}
        self.ninst = {k: 0 for k in self.eng}
        self.nwait = 0

    def sb(self, shape, dt=F32, name=None):
        self.nbuf += 1
        name = (name or "sb") + "_%d" % self.nbuf
        t = self.es.enter_context(self.nc.sbuf_tensor(name, list(shape), dt))
        return Buf(t, name)

    def ps(self, shape, dt=F32, name=None):
        self.nbuf += 1
        name = (name or "ps") + "_%d" % self.nbuf
        t = self.es.enter_context(self.nc.psum_tensor(name, list(shape), dt))
        return Buf(t, name)

    def _wait(self, engine, deps):
        e = self.eng[engine]
        seen = self.seen[engine]
        best = {}
        for (sem, val) in deps:
            k = id(sem)
            if k not in best or best[k][1] < val:
                best[k] = (sem, val)
        for k, (sem, val) in best.items():
            if engine == "pe" and sem is self.esem["pe"]:
                continue
            if engine in NO_SELF_SYNC and sem is self.esem.get(engine):
                continue
            if seen.get(k, 0) < val:
                e.wait_ge(sem, val)
                self.nwait += 1
                seen[k] = val

    def _rel(self, k):
        if isinstance(k, tuple):
            self.parts.setdefault(k[0], set()).add(k)
            return (k, k[0])
        ps = self.parts.get(k)
        return (k,) + tuple(ps) if ps else (k,)

    def _collect(self, reads, writes):
        deps = []
        reads = [getattr(b, "key", b) for b in reads]
        writes = [getattr(b, "key", b) for b in writes]
        for b0 in reads:
            for b in self._rel(b0):
                w = self.lastw.get(b)
                if w:
                    deps.extend(w.items_())
        for b0 in writes:
            for b in self._rel(b0):
                w = self.lastw.get(b)
                if w:
                    deps.extend(w.items_())
                r = self.readers.get(b)
                if r:
                    deps.extend(r.items_())
        return deps

    def _commit(self, tok, reads, writes):
        reads = [getattr(b, "key", b) for b in reads]
        writes = [getattr(b, "key", b) for b in writes]
        for b in reads:
            r = self.readers.get(b)
            if r is None:
                r = self.readers[b] = TokSet()
            r.add(tok)
        for b in writes:
            w = TokSet()
            w.add(tok)
            self.lastw[b] = w
            self.readers[b] = TokSet()

    def op(self, engine, fn, reads=(), writes=()):
        self._wait(engine, self._collect(reads, writes))
        inst = fn(self.eng[engine])
        self.ecnt[engine] += 1
        self.ninst[engine] += 1
        inst.then_inc(self.esem[engine], 1)
        tok = (self.esem[engine], self.ecnt[engine])
        self._commit(tok, reads, writes)
        return tok

    def dma(self, queue, out, in_, reads=(), writes=(), **kw):
        pool = self.dpool[queue]
        i = self.dnext[queue]
        self.dnext[queue] = (i + 1) % len(pool)
        sem = pool[i]
        prev = self.dcnt.get(id(sem), 0)
        deps = self._collect(reads, writes)
        if prev:
            deps.append((sem, prev))
        self._wait(queue, deps)
        inst = self.eng[queue].dma_start(out=out, in_=in_, **kw)
        inst.then_inc(sem, 16)
        self.ninst[queue] += 1
        self.dcnt[id(sem)] = prev + 16
        tok = (sem, prev + 16)
        self._commit(tok, reads, writes)
        return tok

    def all_tokens(self):
        deps = []
        for q, pool in self.dpool.items():
            for sem in pool:
                v = self.dcnt.get(id(sem), 0)
                if v:
                    deps.append((sem, v))
        for k in self.eng:
            if self.ecnt[k]:
                deps.append((self.esem[k], self.ecnt[k]))
        return deps

    def barrier(self):
        deps = self.all_tokens()
        for k in self.eng:
            self._wait(k, deps)

    def finish(self):
        self._wait("sp", self.all_tokens())


class StopBuild(Exception):
    pass


def build(T, stages="ABCDEF", taps=(), stop=None):
    NT = T // 128
    NB = T // 256
    TH = T // 2
    nc = bass.Bass("TRN2", target_bir_lowering=False)
    I = {}

    def inp(name, shape, dt=F32):
        I[name] = nc.dram_tensor(name, list(shape), dt, kind="ExternalInput").ap()
        return I[name]

    inp("xv", [T, D]); inp("valid", [128, NT]); inp("bbias", [128, NB]); inp("cvec", [128, 8])
    inp("w_ada", [4, D, 3 * D]); inp("b_ada", [4, 3 * D]); inp("norm_g", [4, D])
    inp("mu", [128, 6, 8]); inp("w_rkv", [3, D, D])
    for n in ("w0", "a0", "k_k", "k_a", "r_k"):
        inp(n, [128, 8])
    inp("w1", [D, 64]); inp("w2", [64, D]); inp("a1", [D, 64]); inp("a2", [64, D])
    inp("g1", [D, 160]); inp("g2", [160, D]); inp("lnx_g", [D]); inp("lnx_b", [D]); inp("rw_wo", [D, D])
    inp("ffn_g", [2, D, FF]); inp("ffn_u", [2, D, FF]); inp("ffn_d", [2, FF, D])
    inp("kv_norm_g", [D]); inp("kv_w_ada", [D, 2 * D]); inp("kv_b_ada", [2 * D])
    inp("kv_wk", [D, D]); inp("kv_wv", [D, D]); inp("k_norm_g", [D]); inp("q_norm_g", [D])
    inp("mb_wq", [D, D]); inp("mb_wo", [D, D])
    inp("c_ident", [128, 128]); inp("c_rmask", [128, 1024]); inp("c_msu", [128, 64]); inp("c_msl", [128, 64])
    inp("c_miu", [128, 64]); inp("c_id64", [128, 64]); inp("c_bones", [128, 128]); inp("c_hsel", [128, 2])
    inp("c_causal", [128, 2, 256]); inp("c_fut", [128, NB, NB])

    out = nc.dram_tensor("out", [TH, D], F32, kind="ExternalOutput").ap()
    XS = nc.dram_tensor("xs_scr", [T, D], F32, kind="Internal").ap()
    KT = nc.dram_tensor("kt_scr", [8, 128, T], BF16, kind="Internal").ap()
    VS = nc.dram_tensor("v_scr", [T, D], BF16, kind="Internal").ap()
    QT = nc.dram_tensor("qt_scr", [8, 128, TH], BF16, kind="Internal").ap()
    SEL = nc.dram_tensor("sel_scr", [TH, NH * NB], F32, kind="Internal").ap()
    OS = nc.dram_tensor("o_scr", [TH, D], BF16, kind="Internal").ap()
    TAP = {}
    for (name, shape) in taps:
        TAP[name] = nc.dram_tensor("tap_" + name, list(shape), F32, kind="ExternalOutput").ap()

    xs_key = "XS"
    with contextlib.ExitStack() as es0:
        S = Sched(nc, es0)

        def TTo(eng, out, a, b, op, reads, writes):
            return S.op(eng, lambda e: e.tensor_tensor(out=out, in0=a, in1=b, op=op), reads, writes)

        def STT(out, a, sc, b, op0, op1, reads, writes):
            return S.op("dve", lambda e: e.scalar_tensor_tensor(out=out, in0=a, scalar=sc, in1=b, op0=op0, op1=op1), reads, writes)

        def TS(eng, out, a, s1, s2, op0, op1, reads, writes):
            if op1 is None:
                return S.op(eng, lambda e: e.tensor_scalar(out=out, in0=a, scalar1=s1, scalar2=None, op0=op0), reads, writes)
            return S.op(eng, lambda e: e.tensor_scalar(out=out, in0=a, scalar1=s1, scalar2=s2, op0=op0, op1=op1), reads, writes)

        def ACT(out, in_, func, reads, writes, bias=None, scale=None, accum=None):
            kw = {}
            if bias is not None:
                kw["bias"] = bias
            if scale is not None:
                kw["scale"] = scale
            if accum is not None:
                kw["accum_out"] = accum
            return S.op("act", lambda e: e.activation(out=out, in_=in_, func=func, **kw), reads, writes)

        def CP(eng, out, in_, reads, writes):
            if eng == "act":
                return S.op("act", lambda e: e.copy(out=out, in_=in_), reads, writes)
            return S.op(eng, lambda e: e.tensor_copy(out=out, in_=in_), reads, writes)

        def MM(out, lhsT, rhs, start, stop, reads, writes):
            return S.op("pe", lambda e: e.matmul(out, lhsT, rhs, start=start, stop=stop), reads, writes)

        def TR(out, in_, ident, reads, writes):
            return S.op("pe", lambda e: e.transpose(out, in_, ident), reads, writes)

        def kc(v, ch):
            return (getattr(v, "key", v), ch)

        def ck(n):
            if stop is not None and n == stop:
                raise StopBuild()

        def tap(name, src_ap, reads):
            if name in TAP:
                S.dma("sp", TAP[name], src_ap, reads=reads)

        identf = S.sb([128, 128], F32, "identf"); identb = S.sb([128, 128], BF16, "identb")
        S.dma("sp", identf[:], I["c_ident"], writes=[identf])
        S.dma("pool", identb[:], I["c_ident"], writes=[identb])
        PSB = [S.ps([128, 512], F32, "bank%d" % i) for i in range(7)]
        psTf = S.ps([128, 512], F32, "psTf")
        psT = View(psTf, lambda: psTf[:].bitcast(BF16))
        cs = S.sb([128, 8], F32, "cs")
        S.dma("sp", cs[:], I["cvec"], writes=[cs])
        ACT(cs[:], cs[:], AF.Silu, [cs], [cs])

        def ada_rows(es, w_ap, b_ap, n, g_ap):
            rows = [S.sb([128, D], F32, "row%d" % j) for j in range(n)]
            with contextlib.ExitStack() as es1:
                S.es = es1
                brow = S.sb([128, n * D], F32, "brow")
                csrep = S.sb([128, 8, 128], F32, "csrep")
                S.op("dve", lambda e: e.memset(csrep[:], 1.0), (), [csrep])
                TTo("dve", csrep[:], csrep[:], cs[:].unsqueeze(2).to_broadcast([128, 8, 128]), ALU.mult, [cs, csrep], [csrep])
                grow = S.sb([128, D], F32, "grow")
                wsl = [S.sb([128, 8, 512], F32, "wsl%d" % i) for i in range(2)]
                S.dma("sp", brow[:], b_ap.partition_broadcast(128), writes=[brow])
                S.dma("sp", grow[:], g_ap.partition_broadcast(128), writes=[grow])
                wv = w_ap.rearrange("(dc p) n -> p dc n", p=128)
                for ns in range(2 * n):
                    w = wsl[ns % 2]
                    S.dma("sp" if ns % 2 == 0 else "act", w[:], wv[:, :, ns * 512:(ns + 1) * 512], writes=[w])
                    pb = PSB[ns % 2]
                    for dc in range(8):
                        MM(pb[:], csrep[:, dc, :], w[:, dc, :], dc == 0, dc == 7, [csrep, w], [pb])
                    j, hf = ns // 2, ns % 2
                    TTo("dve", rows[j][:, hf * 512:(hf + 1) * 512], pb[:], brow[:, ns * 512:(ns + 1) * 512], ALU.add,
                        [pb, brow], [rows[j]])
                STT(rows[1][:], rows[1][:], 1.0, grow[:], ALU.add, ALU.mult, [rows[1], grow], [rows[1]])
                S.barrier()
            S.es = es
            return rows

        def norm_tile_g(xt, gs, sh, hb, tmpk, ss, vcol=None):
            class _T:
                def __getitem__(s_, k):
                    a = tmpk[:]
                    if len(a.shape) == 3:
                        a = a.rearrange("p a b -> p (a b)")
                    return a[k]
            tmp = _T()
            ACT(tmp[:], xt[:], AF.Square, [xt], [tmpk, ss], accum=ss[:, 0:1])
            yield
            ACT(ss[:, 1:2], ss[:, 0:1], AF.Sqrt, [ss], [ss], bias=1e-6, scale=1.0 / D)
            yield
            S.op("dve", lambda e: e.reciprocal(out=ss[:, 2:3], in_=ss[:, 1:2]), [ss], [ss])
            yield
            if vcol is not None:
                TTo("dve", ss[:, 2:3], ss[:, 2:3], vcol, ALU.mult, [ss], [ss])
                yield
            STT(tmp[:], xt[:], ss[:, 2:3], gs[:], ALU.mult, ALU.mult, [xt, ss, gs], [tmpk])
            yield
            if vcol is not None:
                STT(hb[:], sh[:], vcol, tmp[:], ALU.mult, ALU.add, [sh, tmpk], [hb])
                yield
            else:
                TTo("dve", hb[:], tmp[:], sh[:], ALU.add, [tmpk, sh], [hb])
                yield

        def norm_tile(xt, gs, sh, hb, tmpk, ss, vcol=None):
            for _ in norm_tile_g(xt, gs, sh, hb, tmpk, ss, vcol):
                pass

        def transpose8(src, dst_view, reads_extra=(), dstbuf=None, eng="act"):
            for dc in range(8):
                TR(psT[:, dc * 128:(dc + 1) * 128], src[:, dc * 128:(dc + 1) * 128], identb[:], [src, identb], [psT])
            CP(eng, dst_view, psT[:].rearrange("p (a b) -> p a b", b=128), [psT], [dstbuf])

        def load_w_bf16(buf_view, w_ap, bufkey, q="pool"):
            S.dma(q, buf_view, w_ap, writes=[bufkey])

        valid_sb = S.sb([128, NT], F32, "valid")
        S.dma("sp", valid_sb[:], I["valid"], writes=[valid_sb])

        if "B" in stages:
            with contextlib.ExitStack() as es:
                S.es = es
                sh0, gs0, gt0 = ada_rows(es, I["w_ada"][0], I["b_ada"][0], 3, I["norm_g"][0])
                wrkv = [S.sb([128, 8, D], BF16, "wrkv%d" % i) for i in range(3)]
                for i in range(3):
                    S.dma("pool", wrkv[i][:], I["w_rkv"][i].rearrange("(dc p) n -> p dc n", p=128), writes=[wrkv[i]])
                w1 = S.sb([128, 8, 64], BF16, "w1"); a1 = S.sb([128, 8, 64], BF16, "a1"); g1 = S.sb([128, 8, 160], BF16, "g1")
                S.dma("pool", w1[:], I["w1"].rearrange("(dc p) n -> p dc n", p=128), writes=[w1])
                S.dma("pool", a1[:], I["a1"].rearrange("(dc p) n -> p dc n", p=128), writes=[a1])
                S.dma("pool", g1[:], I["g1"].rearrange("(dc p) n -> p dc n", p=128), writes=[g1])
                w2 = S.sb([64, D], BF16, "w2"); a2 = S.sb([64, D], BF16, "a2")
                g2a = S.sb([128, D], BF16, "g2a"); g2b = S.sb([128, D], BF16, "g2b")
                S.op("pool", lambda e: e.memset(g2b[:], 0.0), (), [g2b])
                S.dma("pool", w2[:], I["w2"], writes=[w2]); S.dma("pool", a2[:], I["a2"], writes=[a2])
                S.dma("pool", g2a[:], I["g2"][0:128, :], writes=[g2a]); S.dma("pool", g2b[0:32, :], I["g2"][128:160, :], writes=[g2b])
                wo = S.sb([128, 8, D], BF16, "wo")
                S.dma("pool", wo[:], I["rw_wo"].rearrange("(dc p) n -> p dc n", p=128), writes=[wo])
                mu = S.sb([128, 6, 8], F32, "mu"); S.dma("sp", mu[:], I["mu"], writes=[mu])
                pp = {}
                for n in ("w0", "a0", "k_k", "k_a", "r_k"):
                    pp[n] = S.sb([128, 8], F32, n); S.dma("sp", pp[n][:], I[n], writes=[pp[n]])
                lnxg = S.sb([128, D], F32, "lnxg"); lnxb = S.sb([128, D], F32, "lnxb")
                S.dma("sp", lnxg[:], I["lnx_g"].partition_broadcast(128), writes=[lnxg])
                S.dma("sp", lnxb[:], I["lnx_b"].partition_broadcast(128), writes=[lnxb])
                cst = {}
                for n, shp in (("c_msu", [128, 64]), ("c_msl", [128, 64]), ("c_miu", [128, 64]),
                               ("c_id64", [128, 64]), ("c_bones", [128, 128])):
                    cst[n] = S.sb(shp, F32, n); S.dma("sp", cst[n][:], I[n], writes=[cst[n]])
                cst["c_rmask"] = S.sb([128, 1024], BF16, "c_rmask"); S.dma("pool", cst["c_rmask"][:], I["c_rmask"], writes=[cst["c_rmask"]])
                hsel = S.sb([128, 2], BF16, "hsel"); S.dma("pool", hsel[:], I["c_hsel"], writes=[hsel])
                H32 = S.sb([128, 8, 64], F32, "H32"); Hb = S.sb([128, 8, 64], BF16, "Hb")
                S.op("dve", lambda e: e.memset(H32[:], 0.0), (), [H32])
                S.op("dve", lambda e: e.memset(Hb[:], 0.0), (), [Hb])
                xt = [S.sb([128, D], F32, "xt0")] * 2
                ss = S.sb([128, 4], F32, "ss")
                hb = S.sb([128, D], BF16, "hb")
                hT = [S.sb([128, 8, 129], BF16, "hT%d" % i) for i in range(2)]
                S.op("dve", lambda e: e.memset(hT[1][:], 0.0), (), [hT[1]])
                dx = S.sb([128, 8, 128], BF16, "dx")
                xs = [S.sb([128, 8, 128], BF16, "xs%d" % i) for i in range(6)]
                f = {n: S.sb([128, 8, 128], F32, "f_" + n) for n in
                     ("sigw", "lp", "P", "alr", "k", "r", "kk", "km", "bA", "tA", "tB")}
                for n in ("iP", "PC"):
                    f[n] = S.sb([128, 8, 128], BF16, "f_" + n)
                class _V:
                    def __init__(s_, b, pat, **kw):
                        s_.b, s_.pat, s_.kw = b, pat, kw
                    def __getitem__(s_, k):
                        return s_.b[:].rearrange(s_.pat, **s_.kw)[k]
                tmp_b = f["tA"]
                th = S.sb([64, 128], BF16, "th"); al = S.sb([64, 128], BF16, "al"); sg = S.sb([128, 2, 128], BF16, "sg")
                S.op("pool", lambda e: e.memset(sg[:], 0.0), (), [sg])
                v_sb = S.sb([128, D], BF16, "v_sb"); g_sb = S.sb([128, D], BF16, "g_sb")
                ARt = S.sb([128, 8, 2, 2, 64], BF16, "ARt"); BKt = S.sb([128, 8, 2, 2, 64], BF16, "BKt")
                BKh = S.sb([128, 8, 2, 128], BF16, "BKh")
                prod = S.sb([128, 8, 128], BF16, "prod"); rk_sb = S.sb([128, 16], F32, "rk_sb")
                def hview(b):
                    return View(b, lambda: b[:].rearrange("p a b -> p (a b)")[:, 0:512].rearrange("p (a b) -> p a b", b=64))
                A_sb = [hview(f["alr"]), hview(f["k"])]
                N_sb = [hview(f["kk"]), hview(f["km"])]
                T_sb = [hview(f["bA"]), hview(f["tB"])]
                Aak = S.sb([128, 8, 64], BF16, "Aak"); Qrb = S.sb([128, 8, 64], BF16, "Qrb"); Qrk = S.sb([128, 8, 64], BF16, "Qrk")
                BKhT = S.sb([128, 8, 2, 64], BF16, "BKhT")
                G_sb = hview(f["r"]); X0 = S.sb([128, 8, 64], F32, "X0"); U_sb = S.sb([128, 8, 64], BF16, "U_sb")
                st = S.sb([128, 4, 16], F32, "st")
                c0, c1, c2 = PSB[4], PSB[5], PSB[6]
                fm = [PSB[0], PSB[1]]
                tm = [PSB[2], PSB[3]]

                def bc8(t, n=128):
                    return t[:].unsqueeze(2).to_broadcast([128, 8, n])

                def m8(mk):
                    return cst[mk][:].unsqueeze(1).to_broadcast([128, 8, 64])

                def fl(b):
                    return b[:].rearrange("p a b -> p (a b)")

                def F1g(tt):
                    xcur = xt[tt % 2]
                    hTc, hTp = hT[tt % 2], hT[(tt + 1) % 2]
                    S.dma("sp", xcur[:], I["xv"][tt * 128:(tt + 1) * 128, :], writes=[xcur])
                    yield
                    for _ in norm_tile_g(xcur, gs0, sh0, hb, f["tA"], ss, vcol=valid_sb[:, tt:tt + 1]):
                        yield
                    CP("pool", hTc[:, :, 0:1], hTp[:, :, 128:129], [hTp], [hTc])
                    yield
                    transpose8(hb, hTc[:, :, 1:129], dstbuf=hTc)
                    yield
                    TTo("dve", dx[:], hTc[:, :, 0:128], hTc[:, :, 1:129], ALU.subtract, [hTc], [dx])
                    yield
                    for i in range(6):
                        eng = "dve" if i < 3 else "pool"
                        TTo(eng, xs[i][:], dx[:], mu[:, i, :].unsqueeze(2).to_broadcast([128, 8, 128]), ALU.mult, [dx, mu], [xs[i]])
                        yield
                        TTo(eng, xs[i][:], xs[i][:], hTc[:, :, 1:129], ALU.add, [xs[i], hTc], [xs[i]])
                        yield

                def F1(tt):
                    for _ in F1g(tt):
                        pass
                pend = []

                def drip(n=1):
                    for _ in range(n * max(1, len(pend))):
                        if not pend:
                            return
                        g_ = pend.pop(0)
                        try:
                            next(g_)
                            pend.append(g_)
                        except StopIteration:
                            pass

                def TLg(g):
                    def v4(ap):
                        return ap.rearrange("p (a g b) -> p a g b", g=2, b=64)[:, :, g, :]
                    YKg = (f["sigw"], "g%d" % g); SKg = (f["lp"], "g%d" % g); stk = (st, g); zk = (prod, "g%d" % g)
                    Yg = v4(fl(f["sigw"])); Sg = v4(fl(f["lp"]))
                    stv = st[:].rearrange("p k (a g) -> p k a g", g=2)
                    m_ = stv[:, 0, :, g]; r_ = stv[:, 1, :, g]; q_ = stv[:, 2, :, g]
                    S.op("dve", lambda e: e.tensor_reduce(out=m_, in_=Yg, axis=AX.X, op=ALU.add), [YKg], [stk]); yield
                    ACT(Sg, Yg, AF.Square, [YKg], [SKg]); yield
                    S.op("dve", lambda e: e.tensor_reduce(out=r_, in_=Sg, axis=AX.X, op=ALU.add), [SKg], [stk]); yield
                    TS("dve", m_, m_, 1.0 / 64, None, ALU.mult, None, [stk], [stk]); yield
                    TTo("dve", q_, m_, m_, ALU.mult, [stk], [stk]); yield
                    STT(r_, r_, 1.0 / 64, q_, ALU.mult, ALU.subtract, [stk], [stk]); yield
                    ACT(r_, r_, AF.Sqrt, [stk], [stk], bias=64e-5, scale=1.0); yield
                    S.op("dve", lambda e: e.reciprocal(out=r_, in_=r_), [stk], [stk]); yield
                    TTo("dve", Yg, Yg, m_.unsqueeze(2).to_broadcast([128, 8, 64]), ALU.subtract, [YKg, stk], [YKg]); yield
                    TTo("dve", Yg, Yg, r_.unsqueeze(2).to_broadcast([128, 8, 64]), ALU.mult, [YKg, stk], [YKg]); yield
                    TTo("pool", Yg, Yg, v4(lnxg[:]), ALU.mult, [YKg, lnxg], [YKg]); yield
                    TTo("pool", Yg, Yg, v4(lnxb[:]), ALU.add, [YKg, lnxb], [YKg]); yield
                    rkg = rk_sb[:].rearrange("p (a g) -> p a g", g=2)[:, :, g]
                    TTo("dve", Sg, v4(v_sb[:]), rkg.unsqueeze(2).to_broadcast([128, 8, 64]), ALU.mult, [v_sb, rk_sb], [SKg]); yield
                    TTo("dve", Sg, Sg, Yg, ALU.add, [SKg, YKg], [SKg]); yield
                    TTo("dve", v4(prod[:].rearrange("p a b -> p (a b)")), Sg, v4(g_sb[:]), ALU.mult, [SKg, g_sb], [zk]); yield

                def F2a():
                    for qi, dst in ((0, f["r"]), (1, f["k"])):
                        for grp in range(2):
                            pb = fm[grp]
                            for pl in range(4):
                                pr = grp * 4 + pl
                                for dc in range(8):
                                    MM(pb[:, pl * 128:(pl + 1) * 128], wrkv[qi][:, dc, pr * 128:(pr + 1) * 128], xs[qi][:, dc, :],
                                       dc == 0, dc == 7, [wrkv[qi], xs[qi]], [pb])
                            CP("act", fl(dst)[:, grp * 512:(grp + 1) * 512], pb[:], [pb], [(dst, grp)])
                    TTo("dve", f["kk"][:], f["k"][:], bc8(pp["k_k"]), ALU.mult, [f["k"], pp["k_k"]], [f["kk"]])
                    ACT(fl(f["bA"]), fl(f["kk"]), AF.Square, [f["kk"]], [f["bA"]])
                    for grp in range(2):
                        MM(fm[grp][:], cst["c_bones"][:], fl(f["bA"])[:, grp * 512:(grp + 1) * 512], True, True, [cst["c_bones"], f["bA"]], [fm[grp]])
                        TS("dve", fl(f["tB"])[:, grp * 512:(grp + 1) * 512], fm[grp][:], 1e-24, None, ALU.max, None, [fm[grp]], [f["tB"]])
                    ACT(fl(f["tB"]), fl(f["tB"]), AF.Ln, [f["tB"]], [f["tB"]])
                    ACT(fl(f["tB"]), fl(f["tB"]), AF.Exp, [f["tB"]], [f["tB"]], scale=-0.5)
                    TTo("dve", f["kk"][:], f["kk"][:], f["tB"][:], ALU.mult, [f["kk"], f["tB"]], [f["kk"]])

                for tt in range(NT):
                  try:
                    if tt == 0:
                        F1(0)
                    ck(2)
                    if tt == 0:
                        F2a()
                    ck(3)
                    for hf in range(2):
                        for dc in range(8):
                            MM(tm[hf][:], xs[2][:, dc, :], wrkv[2][:, dc, hf * 512:(hf + 1) * 512], dc == 0, dc == 7, [xs[2], wrkv[2]], [tm[hf]])
                        CP("act", v_sb[:, hf * 512:(hf + 1) * 512], tm[hf][:], [tm[hf]], [v_sb])
                    ck(4)
                    for dc in range(8):
                        MM(c0[0:64, 0:128], w1[:, dc, :], xs[3][:, dc, :], dc == 0, dc == 7, [w1, xs[3]], [c0])
                    ACT(th[:], c0[0:64, 0:128], AF.Tanh, [c0], [th])
                    for dc in range(8):
                        MM(c1[0:64, 0:128], a1[:, dc, :], xs[4][:, dc, :], dc == 0, dc == 7, [a1, xs[4]], [c1])
                    CP("act", al[:], c1[0:64, 0:128], [c1], [al])
                    for (src, wgt, bias, dst) in ((th, w2, pp["w0"], f["sigw"]), (al, a2, pp["a0"], f["alr"])):
                        for grp in range(2):
                            pb = fm[grp]
                            for pl in range(4):
                                pr = grp * 4 + pl
                                MM(pb[:, pl * 128:(pl + 1) * 128], wgt[:, pr * 128:(pr + 1) * 128], src[:], True, True, [wgt, src], [pb])
                            for pl in range(4):
                                pr = grp * 4 + pl
                                ACT(dst[:, pr, :], pb[:, pl * 128:(pl + 1) * 128], AF.Sigmoid, [pb, bias], [(dst, pr)], bias=bias[:, pr:pr + 1])
                    ck(5)
                    for dc in range(8):
                        MM(c0[:, 0:128], g1[:, dc, 0:128], xs[5][:, dc, :], dc == 0, dc == 7, [g1, xs[5]], [c0])
                    for dc in range(8):
                        MM(c1[0:32, 0:128], g1[:, dc, 128:160], xs[5][:, dc, :], dc == 0, dc == 7, [g1, xs[5]], [c1])
                    ACT(sg[:, 0, :], c0[:, 0:128], AF.Sigmoid, [c0], [sg])
                    ACT(sg[0:32, 1, :], c1[0:32, 0:128], AF.Sigmoid, [c1], [sg])
                    for hf in range(2):
                        MM(tm[hf][:], sg[:, 0, :], g2a[:, hf * 512:(hf + 1) * 512], True, False, [sg, g2a], [tm[hf]])
                        MM(tm[hf][:], sg[:, 1, :], g2b[:, hf * 512:(hf + 1) * 512], False, True, [sg, g2b], [tm[hf]])
                        CP("act", g_sb[:, hf * 512:(hf + 1) * 512], tm[hf][:], [tm[hf]], [g_sb])
                    ck(6)
                    S.op("dve", lambda e: e.tensor_tensor_scan(out=fl(f["lp"]), data0=cst["c_rmask"][:], data1=fl(f["sigw"]),
                                                               initial=0.0, op0=ALU.mult, op1=ALU.add),
                         [cst["c_rmask"], f["sigw"]], [f["lp"]])
                    ACT(fl(f["P"]), fl(f["lp"]), AF.Exp, [f["lp"]], [f["P"]], scale=-CDEC)
                    ACT(fl(f["iP"]), fl(f["lp"]), AF.Exp, [f["lp"]], [f["iP"]], scale=CDEC)
                    lpv = fl(f["lp"]).rearrange("p (a t) -> p a t", t=64)
                    TTo("dve", fl(f["tB"]).rearrange("p (a t) -> p a t", t=64), lpv[:, :, 63:64].to_broadcast([128, 16, 64]), lpv,
                        ALU.subtract, [f["lp"]], [f["tB"]])
                    ACT(fl(f["PC"]), fl(f["tB"]), AF.Exp, [f["tB"]], [f["PC"]], scale=-CDEC)
                    ck(7)
                    ck(8)
                    STT(f["tA"][:], f["alr"][:], -1.0, bc8(pp["k_a"]), ALU.add, ALU.mult, [f["alr"], pp["k_a"]], [f["tA"]])
                    STT(f["km"][:], f["tA"][:], 1.0, f["k"][:], ALU.add, ALU.mult, [f["tA"], f["k"]], [f["km"]])
                    TTo("dve", f["bA"][:], f["kk"][:], f["alr"][:], ALU.mult, [f["kk"], f["alr"]], [f["bA"]])

                    def v4(b):
                        return b[:].rearrange("p a (c t) -> p a c t", t=64)
                    TTo("dve", ARt[:, :, :, 1, :], v4(f["r"]), v4(f["P"]), ALU.mult, [f["r"], f["P"]], [ARt])
                    A16 = ARt[:, :, :, 0, :].rearrange("p a c t -> p (a c) t")
                    kk16 = fl(f["kk"]).rearrange("p (a t) -> p a t", t=64); P16 = fl(f["P"]).rearrange("p (a t) -> p a t", t=64)
                    STT(A16[:, :, 1:64], kk16[:, :, 1:64], -1.0, P16[:, :, 0:63], ALU.mult, ALU.mult, [f["kk"], f["P"]], [ARt])
                    TS("dve", A16[:, :, 0:1], kk16[:, :, 0:1], -1.0, None, ALU.mult, None, [f["kk"]], [ARt])
                    TTo("pool", BKt[:, :, :, 1, :], v4(f["km"]), v4(f["iP"]), ALU.mult, [f["km"], f["iP"]], [BKt])
                    TTo("dve", BKt[:, :, :, 0, :], v4(f["bA"]), v4(f["iP"]), ALU.mult, [f["bA"], f["iP"]], [BKt])
                    TTo("dve", BKh[:, :, 1, :], f["km"][:], f["PC"][:], ALU.mult, [f["km"], f["PC"]], [BKh])
                    TTo("pool", BKh[:, :, 0, :], f["bA"][:], f["PC"][:], ALU.mult, [f["bA"], f["PC"]], [BKh])
                    ck(9)
                    TTo("pool", f["tA"][:], f["r"][:], f["km"][:], ALU.mult, [f["r"], f["km"]], [f["tA"]])
                    TTo("pool", prod[:], f["tA"][:], bc8(pp["r_k"]), ALU.mult, [f["tA"], pp["r_k"]], [prod])
                    for pr in range(8):
                        MM(c2[:, pr * 2:pr * 2 + 2], prod[:, pr, :], hsel[:], True, True, [prod, hsel], [c2])
                    CP("act", rk_sb[:], c2[:, 0:16], [c2], [rk_sb])
                    ck(10)
                    def bv(bk):
                        return bk[:].rearrange("p (a b) -> p a b", b=64)
                    B = PSB
                    Pv = f["P"][:].rearrange("p a (c t) -> p a c t", t=64)
                    for g in range(2):
                        hp = 64 * g
                        if g == 0 and tt + 1 < NT:
                            pend.append(F1g(tt + 1))
                        if g == 1:
                            pend.append(TLg(0))
                        def b4(bk):
                            return bk[:].rearrange("p (a w t) -> p a w t", w=2, t=64)
                        for hl in range(8):
                            bx = B[0 + hl // 4]; by = B[2 + hl // 4]
                            for ch in range(2):
                                tp = 64 * ch
                                bt = BKt[hp:hp + 64, hl, ch, 0, :]; kt = BKt[hp:hp + 64, hl, ch, 1, :]
                                at = ARt[hp:hp + 64, hl, ch, 0, :]; art = ARt[hp:hp + 64, hl, ch, :, :]
                                MM(b4(bx)[tp:tp + 64, hl % 4, :, :], bt, art, True, True, [BKt, ARt], [bx])
                                MM(b4(by)[tp:tp + 64, hl % 4, :, :], kt, art, True, True, [BKt, ARt], [by])
                                MM(bv(B[4])[tp:tp + 64, hl, :], at, bt, True, True, [BKt, ARt], [B[4]])
                        TTo("dve", A_sb[0][:], bv(B[4]), m8("c_msl"), ALU.mult, [B[4], cst["c_msl"]], [A_sb[0]])
                        m4u = cst["c_msu"][:].unsqueeze(1).to_broadcast([128, 4, 64]); m4i = cst["c_miu"][:].unsqueeze(1).to_broadcast([128, 4, 64])
                        for hb2 in range(2):
                            hs = slice(hb2 * 4, hb2 * 4 + 4)
                            TTo("dve", N_sb[0][:, hs, :], b4(B[0 + hb2])[:, :, 0, :], m4u, ALU.mult, [B[0 + hb2], cst["c_msu"]], [(N_sb[0].key, "h%d" % hb2)])
                            TTo("dve", Aak[:, hs, :], b4(B[2 + hb2])[:, :, 0, :], m4u, ALU.mult, [B[2 + hb2], cst["c_msu"]], [(Aak, hb2)])
                        drip(1)
                        for hb2 in range(2):
                            hs = slice(hb2 * 4, hb2 * 4 + 4)
                            TTo("dve", Qrb[:, hs, :], b4(B[0 + hb2])[:, :, 1, :], m4i, ALU.mult, [B[0 + hb2], cst["c_miu"]], [(Qrb, hb2)])
                            TTo("dve", Qrk[:, hs, :], b4(B[2 + hb2])[:, :, 1, :], m4i, ALU.mult, [B[2 + hb2], cst["c_miu"]], [(Qrk, hb2)])
                        drip(1)
                        ck(12)
                        drip(1)
                        psTv = psT[:].rearrange("p (a w j) -> p a w j", w=2, j=64)
                        for hl in range(8):
                            for w in range(2):
                                TR(psTv[:, hl, w, :], BKh[hp:hp + 64, hl, w, :], identb[hp:hp + 64, hp:hp + 64], [BKh, identb], [psT])
                        CP("act", BKhT[:], psTv, [psT], [BKhT])
                        for ch in range(2):
                            tp = 64 * ch
                            for hl in range(8):
                                h = 2 * hl + g
                                MM(bv(B[ch])[tp:tp + 64, hl, :], Aak[tp:tp + 64, hl, :], v_sb[tp:tp + 64, h * 64:(h + 1) * 64], True, True, [Aak, v_sb], [B[ch]])
                            CP("act", X0[tp:tp + 64, :, :], bv(B[ch])[tp:tp + 64, :, :], [B[ch]], [kc(X0, ch)])
                        ck(13)
                        drip(1)
                        TTo("dve", T_sb[0][:], N_sb[0][:], m8("c_id64"), ALU.add, [N_sb[0], cst["c_id64"]], [T_sb[0]])
                        cur = 0
                        for lvl in range(5):
                            nxt = 1 - cur
                            for ch in range(2):
                                tp = 64 * ch
                                for hl in range(8):
                                    MM(bv(B[ch])[tp:tp + 64, hl, :], N_sb[cur][tp:tp + 64, hl, :], A_sb[cur][tp:tp + 64, hl, :], True, True,
                                       [kc(N_sb[cur], ch), kc(A_sb[cur], ch)], [B[ch]])
                                    if lvl < 4:
                                        MM(bv(B[2 + ch])[tp:tp + 64, hl, :], A_sb[cur][tp:tp + 64, hl, :], N_sb[cur][tp:tp + 64, hl, :], True, True,
                                           [kc(N_sb[cur], ch), kc(A_sb[cur], ch)], [B[2 + ch]])
                            for ch in range(2):
                                tp = 64 * ch
                                CP("act", A_sb[nxt][tp:tp + 64, :, :], bv(B[ch])[tp:tp + 64, :, :], [B[ch]], [kc(A_sb[nxt], ch)])
                                if lvl < 4:
                                    CP("act", N_sb[nxt][tp:tp + 64, :, :], bv(B[2 + ch])[tp:tp + 64, :, :], [B[2 + ch]], [kc(N_sb[nxt], ch)])
                            drip(1)
                            for ch in range(2):
                                tp = 64 * ch
                                for hl in range(8):
                                    MM(bv(B[4 + ch])[tp:tp + 64, hl, :], A_sb[nxt][tp:tp + 64, hl, :], T_sb[cur][tp:tp + 64, hl, :], True, True,
                                       [kc(A_sb[nxt], ch), kc(T_sb[cur], ch)], [B[4 + ch]])
                            for ch in range(2):
                                tp = 64 * ch
                                TTo("dve", T_sb[nxt][tp:tp + 64, :, :], bv(B[4 + ch])[tp:tp + 64, :, :], T_sb[cur][tp:tp + 64, :, :], ALU.add,
                                    [B[4 + ch], kc(T_sb[cur], ch)], [kc(T_sb[nxt], ch)])
                            cur = nxt
                            drip(1)
                        TTf = T_sb[cur]
                        ck(14)
                        GX, UX, HX, Y1 = B[0], B[1], B[2], B[3]
                        for ch in range(2):
                            tp = 64 * ch
                            Y2 = B[4 + ch]
                            for hl in range(8):
                                MM(bv(GX)[tp:tp + 64, hl, :], ARt[hp:hp + 64, hl, ch, 0, :], Hb[hp:hp + 64, hl, :], True, True, [ARt, Hb], [GX])
                            TTo("dve", G_sb[tp:tp + 64, :, :], bv(GX)[tp:tp + 64, :, :], X0[tp:tp + 64, :, :], ALU.add, [GX, kc(X0, ch)], [kc(G_sb, ch)])
                            for hl in range(8):
                                MM(bv(UX)[tp:tp + 64, hl, :], TTf[tp:tp + 64, hl, :], G_sb[tp:tp + 64, hl, :], True, True, [kc(TTf, ch), kc(G_sb, ch)], [UX])
                            CP("act", U_sb[tp:tp + 64, :, :], bv(UX)[tp:tp + 64, :, :], [UX], [kc(U_sb, ch)])
                            drip(1)
                            for hl in range(8):
                                MM(bv(Y1)[tp:tp + 64, hl, :], ARt[hp:hp + 64, hl, ch, 1, :], Hb[hp:hp + 64, hl, :], True, True, [ARt, Hb], [Y1])
                            for hl in range(8):
                                h = 2 * hl + g
                                MM(bv(Y2)[tp:tp + 64, hl, :], Qrb[tp:tp + 64, hl, :], U_sb[tp:tp + 64, hl, :], True, False, [Qrb, kc(U_sb, ch)], [Y2])
                                MM(bv(Y2)[tp:tp + 64, hl, :], Qrk[tp:tp + 64, hl, :], v_sb[tp:tp + 64, h * 64:(h + 1) * 64], False, True, [Qrk, v_sb], [Y2])
                            for hl in range(8):
                                h = 2 * hl + g
                                MM(bv(HX)[hp:hp + 64, hl, :], BKhT[tp:tp + 64, hl, 0, :], U_sb[tp:tp + 64, hl, :], True, False, [BKhT, kc(U_sb, ch)], [HX])
                                MM(bv(HX)[hp:hp + 64, hl, :], BKhT[tp:tp + 64, hl, 1, :], v_sb[tp:tp + 64, h * 64:(h + 1) * 64], False, True, [BKhT, v_sb], [HX])
                            Hg = H32[hp:hp + 64, :, :]
                            TTo("dve", Hg, Hg, Pv[hp:hp + 64, :, ch, 63:64].to_broadcast([64, 8, 64]), ALU.mult, [H32, f["P"]], [H32])
                            TTo("dve", Hg, Hg, bv(HX)[hp:hp + 64, :, :], ALU.add, [H32, HX], [H32])
                            CP("act", Hb[hp:hp + 64, :, :], Hg, [H32], [Hb])
                        Yg = fl(f["sigw"]).rearrange("p (a g b) -> p a g b", g=2, b=64)[:, :, g, :]
                        CP("act", Yg, bv(Y1), [Y1], [f["sigw"]])
                        for ch in range(2):
                            tp = 64 * ch
                            TTo("dve", Yg[tp:tp + 64], Yg[tp:tp + 64], bv(B[4 + ch])[tp:tp + 64, :, :], ALU.add, [f["sigw"], B[4 + ch]], [f["sigw"]])
                    ck(15)
                    drip(1000)
                    if tt + 1 < NT:
                        F2a()
                    YK, TK, XK = f["sigw"], f["tA"], f["lp"]
                    Yf = fl(YK); Y3 = Yf.rearrange("p (a b) -> p a b", b=64)
                    tmpf = fl(TK); tmp3 = tmpf.rearrange("p (a b) -> p a b", b=64)
                    x1f = fl(XK)
                    for _ in TLg(1):
                        pass
                    zb = View(prod, lambda: prod[:].rearrange("p a b -> p (a b)")); zT = View(BKh, lambda: BKh[:, :, 0, :])
                    transpose8(zb, zT[:], dstbuf=zT)
                    for hf in range(2):
                        for dc in range(8):
                            MM(tm[hf][:], zT[:, dc, :], wo[:, dc, hf * 512:(hf + 1) * 512], dc == 0, dc == 7, [zT, wo], [tm[hf]])
                        TTo("dve", x1f[:, hf * 512:(hf + 1) * 512], tm[hf][:], gt0[:, hf * 512:(hf + 1) * 512], ALU.mult, [tm[hf], gt0], [XK])
                    S.dma("sp", fl(f["P"]), I["xv"][tt * 128:(tt + 1) * 128, :], writes=[f["P"]])
                    TTo("pool", x1f, x1f, fl(f["P"]), ALU.add, [XK, f["P"]], [XK])
                    S.dma("sp", XS[tt * 128:(tt + 1) * 128, :], x1f, reads=[XK], writes=[("XS", tt)])
                    if tt == NT - 1:
                        tap("x1_last", x1f, [XK])
                  except StopBuild:
                    break
                S.barrier()
            S.es = es0

        def ffn_stage(L, tile0, tile1, to_out):
            with contextlib.ExitStack() as es:
                S.es = es
                sh, gs, gt = ada_rows(es, I["w_ada"][2 * L + 1], I["b_ada"][2 * L + 1], 3, I["norm_g"][2 * L + 1])
                wg = S.sb([128, 8, FF], BF16, "wg"); wu = S.sb([128, 8, FF], BF16, "wu"); wd = S.sb([128, NFC, D], BF16, "wd")
                gv = I["ffn_g"][L].rearrange("(dc p) n -> p dc n", p=128); uv = I["ffn_u"][L].rearrange("(dc p) n -> p dc n", p=128)
                dv = I["ffn_d"][L].rearrange("(fc p) n -> p fc n", p=128)
                for dc in range(8):
                    S.dma("pool", wg[:, dc, :], gv[:, dc, :], writes=[wg]); S.dma("pool", wu[:, dc, :], uv[:, dc, :], writes=[wu])
                for fc in range(0, NFC, 2):
                    S.dma("pool", wd[:, fc:fc + 2, :], dv[:, fc:fc + 2, :], writes=[wd])
                NS = 4
                TW = NS * 128
                xt = S.sb([128, NS, D], F32, "fx"); tmpk = S.sb([128, D], F32, "ftmp"); ss = S.sb([128, 4], F32, "fss")
                hb = S.sb([128, D], BF16, "fhb"); hT = S.sb([128, 8, TW], BF16, "fhT")
                act = S.sb([128, NFC, TW], BF16, "fact"); sls = [S.sb([128, TW], BF16, "fsl%d" % i) for i in range(2)]
                x2 = S.sb([128, D], F32, "fx2"); xr = tmpk
                nt4 = (tile1 - tile0) // NS

                def front(t4):
                    for sub in range(NS):
                        tt = tile0 + t4 * NS + sub
                        xs_v = View((xt, sub), lambda sub=sub: xt[:, sub, :])
                        S.dma("sp", xt[:, sub, :], XS[tt * 128:(tt + 1) * 128, :], reads=[("XS", tt)], writes=[(xt, sub)])
                        norm_tile(xs_v, gs, sh, hb, tmpk, ss)
                        for dc in range(8):
                            TR(psT[:, dc * 128:(dc + 1) * 128], hb[:, dc * 128:(dc + 1) * 128], identb[:], [hb, identb], [psT])
                        CP("act", hT[:, :, sub * 128:(sub + 1) * 128], psT[:].rearrange("p (a b) -> p a b", b=128), [psT], [(hT, sub)])

                def back(t4):
                    for fc in range(NFC):
                        pg = PSB[fc % 2]; pu = PSB[2 + fc % 2]
                        for dc in range(8):
                            MM(pg[:], wg[:, dc, fc * 128:(fc + 1) * 128], hT[:, dc, :], dc == 0, dc == 7, [wg, hT], [pg])
                        for dc in range(8):
                            MM(pu[:], wu[:, dc, fc * 128:(fc + 1) * 128], hT[:, dc, :], dc == 0, dc == 7, [wu, hT], [pu])
                        sl = sls[fc % 2]
                        ACT(sl[:], pg[:], AF.Silu, [pg], [sl])
                        TTo("dve", act[:, fc, :], pu[:], sl[:], ALU.mult, [pu, sl], [(act, fc)])
                    if t4 + 1 < nt4:
                        front(t4 + 1)
                    for sub in range(NS):
                        tt = tile0 + t4 * NS + sub
                        S.dma("act", xr[:], XS[tt * 128:(tt + 1) * 128, :], reads=[("XS", tt)], writes=[xr])
                        for hf in range(2):
                            po = PSB[4 + hf]
                            for fc in range(NFC):
                                MM(po[:], act[:, fc, sub * 128:(sub + 1) * 128], wd[:, fc, hf * 512:(hf + 1) * 512], fc == 0, fc == NFC - 1, [(act, fc), wd], [po])
                            TTo("dve", x2[:, hf * 512:(hf + 1) * 512], po[:], gt[:, hf * 512:(hf + 1) * 512], ALU.mult, [po, gt], [x2])
                        TTo("pool", x2[:], x2[:], xr[:], ALU.add, [x2, xr], [x2])
                        if to_out:
                            S.dma("sp", out[(tt - tile0) * 128:(tt - tile0 + 1) * 128, :], x2[:], reads=[x2], writes=[("OUT", tt)])
                        else:
                            S.dma("sp", XS[tt * 128:(tt + 1) * 128, :], x2[:], reads=[x2], writes=[("XS", tt)])
                        if tt == tile1 - 1:
                            tap("ffn%d_last" % L, x2[:], [x2])
                front(0)
                for t4 in range(nt4):
                    back(t4)
                S.barrier()
            S.es = es0

        if "C" in stages:
            ffn_stage(0, 0, NT, False)

        VS2 = nc.dram_tensor("v2_scr", [8, T, 130], BF16, kind="Internal").ap()
        kmean = S.sb([128, 8, NB], BF16, "kmean")
        hnorm_tmp = {}

        def head_rmsnorm(src_ps_halves, grow, dst_f32_key, dst_ap, sq_key, sq_ap, st, extra_scale=1.0):
            for hf in range(2):
                ACT(sq_ap[:, hf * 512:(hf + 1) * 512], src_ps_halves[hf][:], AF.Square, [src_ps_halves[hf]], [sq_key])
            S.op("dve", lambda e: e.tensor_reduce(out=st[:, 0, :], in_=sq_ap.rearrange("p (a b) -> p a b", b=64), axis=AX.X, op=ALU.add), [sq_key], [st])
            ACT(st[:, 1, :], st[:, 0, :], AF.Sqrt, [st], [st], bias=1e-6, scale=1.0 / 64)
            S.op("dve", lambda e: e.reciprocal(out=st[:, 1, :], in_=st[:, 1, :]), [st], [st])
            if extra_scale != 1.0:
                TS("dve", st[:, 1, :], st[:, 1, :], extra_scale, None, ALU.mult, None, [st], [st])
            for hf in range(2):
                TTo("dve", dst_ap[:, hf * 512:(hf + 1) * 512].rearrange("p (a b) -> p a b", b=64),
                    src_ps_halves[hf][:].rearrange("p (a b) -> p a b", b=64),
                    st[:, 1, hf * 8:(hf + 1) * 8].unsqueeze(2).to_broadcast([128, 8, 64]), ALU.mult, [src_ps_halves[hf], st], [dst_f32_key])
            TTo("pool", dst_ap, dst_ap, grow[:], ALU.mult, [dst_f32_key, grow], [dst_f32_key])

        if "D" in stages:
            with contextlib.ExitStack() as es:
                S.es = es
                shk, gsk = ada_rows(es, I["kv_w_ada"], I["kv_b_ada"], 2, I["kv_norm_g"])
                wk = S.sb([128, 8, D], BF16, "wk"); wv = S.sb([128, 8, D], BF16, "wv")
                S.dma("pool", wk[:], I["kv_wk"].rearrange("(dc p) n -> p dc n", p=128), writes=[wk])
                S.dma("pool", wv[:], I["kv_wv"].rearrange("(dc p) n -> p dc n", p=128), writes=[wv])
                kng = S.sb([128, D], F32, "kng"); S.dma("sp", kng[:], I["k_norm_g"].partition_broadcast(128), writes=[kng])
                onesc = S.sb([128, 1], F32, "onesc"); S.op("dve", lambda e: e.memset(onesc[:], 1.0 / 256), (), [onesc])
                xts = [S.sb([128, D], F32, "dx_%d" % i) for i in range(2)]; tmpk = S.sb([128, D], F32, "dtmp"); ss = S.sb([128, 4], F32, "dss")
                hb = S.sb([128, D], BF16, "dhb"); hTs = [S.sb([128, 8, 128], BF16, "dhT%d" % i) for i in range(2)]
                kf = S.sb([128, D], F32, "kf"); kb = S.sb([128, D], BF16, "kb"); sq = S.sb([128, D], F32, "ksq")
                st = S.sb([128, 2, 16], F32, "kst"); kTs = [S.sb([128, 8, 128], BF16, "kT%d" % i) for i in range(2)]
                vexts = [S.sb([128, 16, 65], BF16, "vext%d" % i) for i in range(2)]
                for v_ in vexts:
                    S.op("dve", lambda e: e.memset(v_[:], 1.0), (), [v_])
                kmacc = S.sb([128, 8], F32, "kmacc")

                def dfront(tt):
                    xt = xts[tt % 2]; hT = hTs[tt % 2]
                    S.dma("sp", xt[:], XS[tt * 128:(tt + 1) * 128, :], reads=[("XS", tt)], writes=[xt])
                    norm_tile(xt, gsk, shk, hb, tmpk, ss)
                    transpose8(hb, hT[:], dstbuf=hT)

                kfs = [kf, S.sb([128, D], F32, "kf2")]; kbs = [kb, S.sb([128, D], BF16, "kb2")]

                def dmm(tt):
                    hT = hTs[tt % 2]; vext = vexts[tt % 2]
                    KB = [PSB[0], PSB[1]] if tt % 2 == 0 else [PSB[5], PSB[6]]
                    for hf in range(2):
                        for dc in range(8):
                            MM(KB[hf][:], hT[:, dc, :], wk[:, dc, hf * 512:(hf + 1) * 512], dc == 0, dc == 7, [hT, wk], [KB[hf]])
                    for hf in range(2):
                        for dc in range(8):
                            MM(PSB[2 + hf][:], hT[:, dc, :], wv[:, dc, hf * 512:(hf + 1) * 512], dc == 0, dc == 7, [hT, wv], [PSB[2 + hf]])
                        CP("act", vext[:, hf * 8:(hf + 1) * 8, 0:64], PSB[2 + hf][:].rearrange("p (a b) -> p a b", b=64), [PSB[2 + hf]], [vext])

                def dhead(tt):
                    KB = [PSB[0], PSB[1]] if tt % 2 == 0 else [PSB[5], PSB[6]]
                    kf_ = kfs[tt % 2]; kb_ = kbs[tt % 2]
                    head_rmsnorm(KB, kng, kf_, kf_[:], sq, sq[:], st)
                    CP("act", kb_[:], kf_[:], [kf_], [kb_])

                def dpost(tt):
                    kT = kTs[tt % 2]; vext = vexts[tt % 2]; kf_ = kfs[tt % 2]; kb_ = kbs[tt % 2]
                    for pr in range(8):
                        MM(PSB[4][:, pr:pr + 1], kf_[:, pr * 128:(pr + 1) * 128], onesc[:], True, True, [kf_, onesc], [PSB[4]])
                    if tt % 2 == 0:
                        CP("dve", kmacc[:], PSB[4][:, 0:8], [PSB[4]], [kmacc])
                    else:
                        TTo("dve", kmean[:, :, tt // 2], PSB[4][:, 0:8], kmacc[:], ALU.add, [PSB[4], kmacc], [kmean])
                    transpose8(kb_, kT[:], dstbuf=kT)
                    S.dma("sp", KT[:, :, tt * 128:(tt + 1) * 128].rearrange("a p t -> p a t"), kT[:], reads=[kT], writes=[("KT", tt)])
                    S.dma("sp", VS2[:, tt * 128:(tt + 1) * 128, :].rearrange("a p c -> p a c"),
                          vext[:].rearrange("p (a g) c -> p a (g c)", g=2), reads=[vext], writes=[("VS2", tt)])
                    if tt == NT - 1:
                        tap("kf_last", kf_[:], [kf_])
                dfront(0)
                for tt in range(NT):
                    dmm(tt)
                    if tt + 1 < NT:
                        dfront(tt + 1)
                    dhead(tt)
                    if tt >= 1:
                        dpost(tt - 1)
                dpost(NT - 1)
                S.barrier()
            S.es = es0

        if "E" in stages:
            NQT = NT // 2
            NQB = NB // 2
            gt1p = S.sb([128, D], F32, "gt1p")
            with contextlib.ExitStack() as es:
                S.es = es
                sh1, gs1, gt1 = ada_rows(es, I["w_ada"][2], I["b_ada"][2], 3, I["norm_g"][2])
                CP("pool", gt1p[:], gt1[:], [gt1], [gt1p])
                wq = S.sb([128, 8, D], BF16, "wq")
                S.dma("pool", wq[:], I["mb_wq"].rearrange("(dc p) n -> p dc n", p=128), writes=[wq])
                qng = S.sb([128, D], F32, "qng"); S.dma("sp", qng[:], I["q_norm_g"].partition_broadcast(128), writes=[qng])
                bb = S.sb([128, NB], F32, "bb"); S.dma("sp", bb[:], I["bbias"], writes=[bb])
                fut = S.sb([128, NB, NB], F32, "fut"); S.dma("sp", fut[:], I["c_fut"], writes=[fut])
                xts = [S.sb([128, D], F32, "ex%d" % i) for i in range(2)]; tmpk = S.sb([128, D], F32, "etmp"); ss = S.sb([128, 4], F32, "ess")
                hb = S.sb([128, D], BF16, "ehb"); hTs = [S.sb([128, 8, 128], BF16, "ehT%d" % i) for i in range(2)]
                qf = S.sb([128, D], F32, "qf"); qb_ = S.sb([128, D], BF16, "qb"); sq = S.sb([128, D], F32, "qsq")
                st = S.sb([128, 2, 16], F32, "qst"); qTs = [S.sb([128, 8, 128], BF16, "qT%d" % i) for i in range(2)]
                gsb = S.sb([128, 16, NB], F32, "gsb"); m8t = S.sb([128, 16, 8], F32, "m8t"); selts = [S.sb([128, 16, NB], F32, "selt%d" % i) for i in range(2)]
                sel2 = S.sb([128, 16, NB], F32, "sel2"); brow = S.sb([128, NB], F32, "browq")

                def efront(qt):
                    tt = NQT + qt
                    xt = xts[qt % 2]; hT = hTs[qt % 2]
                    S.dma("sp", xt[:], XS[tt * 128:(tt + 1) * 128, :], reads=[("XS", tt)], writes=[xt])
                    norm_tile(xt, gs1, sh1, hb, tmpk, ss)
                    transpose8(hb, hT[:], dstbuf=hT)

                qfs = [qf, S.sb([128, D], F32, "qf2")]; qbs = [qb_, S.sb([128, D], BF16, "qb2")]

                def emm(qt):
                    hT = hTs[qt % 2]
                    QB = [PSB[0], PSB[1]] if qt % 2 == 0 else [PSB[5], PSB[6]]
                    for hf in range(2):
                        for dc in range(8):
                            MM(QB[hf][:], hT[:, dc, :], wq[:, dc, hf * 512:(hf + 1) * 512], dc == 0, dc == 7, [hT, wq], [QB[hf]])

                def ehead(qt):
                    QB = [PSB[0], PSB[1]] if qt % 2 == 0 else [PSB[5], PSB[6]]
                    qf_ = qfs[qt % 2]; qb2_ = qbs[qt % 2]
                    head_rmsnorm(QB, qng, qf_, qf_[:], sq, sq[:], st, extra_scale=0.125)
                    CP("act", qb2_[:], qf_[:], [qf_], [qb2_])

                def epost(qt):
                    tt = NQT + qt
                    vb = tt // 2
                    qT = qTs[qt % 2]; selt = selts[qt % 2]; qf_ = qfs[qt % 2]; qb2_ = qbs[qt % 2]
                    transpose8(qb2_, qT[:], dstbuf=qT)
                    S.dma("sp", QT[:, :, qt * 128:(qt + 1) * 128].rearrange("a p t -> p a t"), qT[:], reads=[qT], writes=[("QT", qt)])
                    for g in range(2):
                        hp = 64 * g
                        for pr in range(8):
                            MM(PSB[2 + g][:, pr * NB:(pr + 1) * NB], qT[hp:hp + 64, pr, :], kmean[hp:hp + 64, pr, :], True, True, [qT, kmean], [PSB[2 + g]])
                    TTo("dve", brow[:], bb[:], fut[:, vb, :], ALU.add, [bb, fut], [brow])
                    gv4 = gsb[:].rearrange("p (a g) n -> p a g n", g=2)
                    for g in range(2):
                        TTo("dve", gv4[:, :, g, :], PSB[2 + g][:, 0:8 * NB].rearrange("p (a n) -> p a n", n=NB),
                            brow[:].unsqueeze(1).to_broadcast([128, 8, NB]), ALU.add, [PSB[2 + g], brow], [gsb])
                    for h in range(16):
                        S.op("dve", lambda e: e.max(out=m8t[:, h, :], in_=gsb[:, h, :]), [gsb], [(m8t, h)])
                    TTo("dve", selt[:], gsb[:], m8t[:, :, 2:3].to_broadcast([128, 16, NB]), ALU.is_ge, [gsb, m8t], [selt])
                    TS("pool", sel2[:], gsb[:], -1.0e29, None, ALU.is_gt, None, [gsb], [sel2])
                    TTo("dve", selt[:], selt[:], sel2[:], ALU.mult, [selt, sel2], [selt])
                    S.dma("sp", SEL[qt * 128:(qt + 1) * 128, :], selt[:].rearrange("p a n -> p (a n)"), reads=[selt], writes=[("SEL", qt)])
                    if qt == NQT - 1:
                        tap("sel_last", selt[:].rearrange("p a n -> p (a n)"), [selt])
                        tap("qf_last", qf_[:], [qf_])
                efront(0)
                for qt in range(NQT):
                    emm(qt)
                    if qt + 1 < NQT:
                        efront(qt + 1)
                    ehead(qt)
                    if qt >= 1:
                        epost(qt - 1)
                epost(NQT - 1)
                S.barrier()
            S.es = es0
            with contextlib.ExitStack() as es:
                S.es = es
                Kp = S.sb([128, T], BF16, "Kp"); Vp = S.sb([128, NT, 130], BF16, "Vp"); Qp = S.sb([128, TH], BF16, "Qp")
                caus = S.sb([128, 512], BF16, "caus"); S.dma("pool", caus[:], I["c_causal"].rearrange("p a b -> p (a b)"), writes=[caus])
                accs = [S.sb([128, 2, 2, 65], F32, "acc%d" % i) for i in range(2)]; pt = [S.sb([128, 512], BF16, "pt%d" % i) for i in range(6)]
                selps = [S.sb([128, 2, 2, NB], F32, "selp%d" % i) for i in range(2)]
                rc = S.sb([128, 4], F32, "rc"); ob = S.sb([128, 2, 128], BF16, "ob")
                SELv = SEL.rearrange("(q p) (h n) -> p q h n", p=128, n=NB)
                for pr in range(8):
                    S.dma("sp", Kp[:], KT[pr], reads=["KT"], writes=[Kp])
                    S.dma("sp", Vp[:], VS2[pr].rearrange("(n p) c -> p n c", p=128), reads=["VS2"], writes=[Vp])
                    S.dma("sp", Qp[:], QT[pr], reads=["QT"], writes=[Qp])
                    Vp4 = Vp[:].rearrange("p n (g c) -> p n g c", g=2)
                    for qb in range(NQB):
                        vb = NQB + qb
                        acc = accs[qb % 2]; selp = selps[qb % 2]
                        S.dma("act", selp[:], SELv[:, 2 * qb:2 * qb + 2, 2 * pr:2 * pr + 2, :], reads=["SEL"], writes=[selp])
                        S.op("pool", lambda e: e.memset(acc[:], 0.0), (), [acc])
                        SB6 = [PSB[0], PSB[1], PSB[2], PSB[3], PSB[4], PSB[5]]; OB2 = [PSB[6], psTf]

                        def emitS(n):
                            for kt in range(2):
                                for g in range(2):
                                    hp = 64 * g
                                    sb_ = SB6[(n % 3) * 2 + g]
                                    MM(sb_[:, kt * 256:(kt + 1) * 256], Kp[hp:hp + 64, n * 256 + kt * 128:n * 256 + (kt + 1) * 128],
                                       Qp[hp:hp + 64, qb * 256:(qb + 1) * 256], True, True, [Kp, Qp], [sb_])

                        def emitRest(n):
                            own = (n == vb)
                            for g in range(2):
                                sb_ = SB6[(n % 3) * 2 + g]; p_ = pt[(n % 3) * 2 + g]
                                ACT(p_[:], sb_[:], AF.Exp, [sb_], [p_])
                                if own:
                                    TTo("pool", p_[:], p_[:], caus[:], ALU.mult, [p_, caus], [p_])
                            for g in range(2):
                                p_ = pt[(n % 3) * 2 + g]; ob_ = OB2[g]
                                for qtl in range(2):
                                    for kt in range(2):
                                        MM(ob_[:, qtl * 65:(qtl + 1) * 65], p_[:, kt * 256 + qtl * 128:kt * 256 + (qtl + 1) * 128],
                                           Vp4[:, n * 2 + kt, g, :], kt == 0, kt == 1, [p_, Vp], [ob_])
                            for g in range(2):
                                ob_ = OB2[g]
                                for qtl in range(2):
                                    if own:
                                        TTo("dve", acc[:, qtl, g, :], ob_[:, qtl * 65:(qtl + 1) * 65], acc[:, qtl, g, :], ALU.add, [ob_, (acc, qtl * 2 + g)], [(acc, qtl * 2 + g)])
                                    else:
                                        STT(acc[:, qtl, g, :], ob_[:, qtl * 65:(qtl + 1) * 65], selp[:, qtl, g, n:n + 1], acc[:, qtl, g, :],
                                            ALU.mult, ALU.add, [ob_, selp, (acc, qtl * 2 + g)], [(acc, qtl * 2 + g)])
                        emitS(0)
                        if vb >= 1:
                            emitS(1)
                        for n in range(vb + 1):
                            if n + 2 <= vb:
                                emitS(n + 2)
                            emitRest(n)
                        S.op("dve", lambda e: e.reciprocal(out=rc[:].rearrange("p (a b) -> p a b", b=2), in_=acc[:, :, :, 64]), [acc], [rc])
                        TTo("dve", ob[:].rearrange("p q (g c) -> p q g c", g=2), acc[:, :, :, 0:64],
                            rc[:].rearrange("p (a b) -> p a b", b=2).unsqueeze(3).to_broadcast([128, 2, 2, 64]), ALU.mult, [acc, rc], [ob])
                        S.dma("pool", OS[qb * 256:(qb + 1) * 256, pr * 128:(pr + 1) * 128].rearrange("(q p) c -> p q c", p=128), ob[:],
                              reads=[ob], writes=[("OS", pr * 1000 + qb)])
                S.barrier()
            S.es = es0
            with contextlib.ExitStack() as es:
                S.es = es
                gt1 = gt1p
                wo1 = S.sb([128, 8, D], BF16, "wo1")
                S.dma("pool", wo1[:], I["mb_wo"].rearrange("(dc p) n -> p dc n", p=128), writes=[wo1])
                xts = [S.sb([128, D], F32, "e3x%d" % i) for i in range(2)]; ots = [S.sb([128, D], BF16, "e3o%d" % i) for i in range(2)]
                oTs = [S.sb([128, 8, 128], BF16, "e3oT%d" % i) for i in range(2)]
                x3s = [S.sb([128, D], F32, "e3x3%d" % i) for i in range(2)]

                def ffront(qt):
                    tt = NQT + qt
                    xt = xts[qt % 2]; ot = ots[qt % 2]; oT = oTs[qt % 2]
                    S.dma("sp", xt[:], XS[tt * 128:(tt + 1) * 128, :], reads=[("XS", tt)], writes=[xt])
                    S.dma("sp", ot[:], OS[qt * 128:(qt + 1) * 128, :], reads=["OS"], writes=[ot])
                    transpose8(ot, oT[:], dstbuf=oT)

                def fback(qt):
                    tt = NQT + qt
                    xt = xts[qt % 2]; oT = oTs[qt % 2]; x3 = x3s[qt % 2]
                    for hf in range(2):
                        for dc in range(8):
                            MM(PSB[hf][:], oT[:, dc, :], wo1[:, dc, hf * 512:(hf + 1) * 512], dc == 0, dc == 7, [oT, wo1], [PSB[hf]])
                        if hf == 0 and qt + 1 < NQT:
                            ffront(qt + 1)
                        TTo("dve", x3[:, hf * 512:(hf + 1) * 512], PSB[hf][:], gt1[:, hf * 512:(hf + 1) * 512], ALU.mult, [PSB[hf], gt1], [x3])
                    TTo("pool", x3[:], x3[:], xt[:], ALU.add, [x3, xt], [x3])
                    S.dma("sp", XS[tt * 128:(tt + 1) * 128, :], x3[:], reads=[x3], writes=[("XS", tt)])
                    if qt == NQT - 1:
                        tap("x3_last", x3[:], [x3])
                ffront(0)
                for qt in range(NQT):
                    fback(qt)
                S.barrier()
            S.es = es0

        if "F" in stages:
            ffn_stage(1, NT // 2, NT, True)

        S.finish()
        print("ninst", S.ninst, "nwait", S.nwait)
    return nc


def _consts():
    p = np.arange(128)
    c = {}
    c["c_ident"] = np.eye(128, dtype=np.float32)
    rm = np.ones((128, 1024), np.float32); rm[:, ::64] = 0.0
    c["c_rmask"] = rm
    t = np.arange(64)
    pm = (p % 64)[:, None]
    c["c_msu"] = (pm < t[None, :]).astype(np.float32)
    c["c_msl"] = (t[None, :] < pm).astype(np.float32)
    c["c_miu"] = (pm <= t[None, :]).astype(np.float32)
    c["c_id64"] = (pm == t[None, :]).astype(np.float32)
    c["c_bones"] = ((p[:, None] // 64) == (p[None, :] // 64)).astype(np.float32)
    c["c_hsel"] = ((p[:, None] // 64) == np.arange(2)[None, :]).astype(np.float32)
    q = np.arange(256)
    cz = np.zeros((128, 2, 256), np.float32)
    for kt in range(2):
        cz[:, kt, :] = ((kt * 128 + p)[:, None] <= q[None, :])
    c["c_causal"] = cz
    return c


def prep_shared(inp):
    f = lambda a: np.ascontiguousarray(np.asarray(a, dtype=np.float32))
    pp = lambda v: f(np.asarray(v).reshape(8, 128).T)
    sh = {}
    sh["w_ada"] = f(np.asarray(inp["w_ada"]).reshape(4, D, 3 * D))
    sh["b_ada"] = f(np.asarray(inp["b_ada"]).reshape(4, 3 * D))
    sh["norm_g"] = f(np.asarray(inp["norm_g"]).reshape(4, D))
    sh["mu"] = f(np.asarray(inp["rw_mu"])[0].reshape(6, 8, 128).transpose(2, 0, 1))
    sh["w_rkv"] = f(np.asarray(inp["rw_w_rkv"])[0])
    sh["w0"] = pp(inp["rw_w0"][0]); sh["a0"] = pp(inp["rw_a0"][0]); sh["k_k"] = pp(inp["rw_k_k"][0])
    sh["k_a"] = pp(inp["rw_k_a"][0]); sh["r_k"] = pp(np.asarray(inp["rw_r_k"])[0].reshape(-1))
    sh["w1"] = f(inp["rw_w1"][0]); sh["w2"] = f(inp["rw_w2"][0]); sh["a1"] = f(inp["rw_a1"][0]); sh["a2"] = f(inp["rw_a2"][0])
    sh["g1"] = f(inp["rw_g1"][0]); sh["g2"] = f(inp["rw_g2"][0])
    sh["lnx_g"] = f(inp["rw_lnx_g"][0]); sh["lnx_b"] = f(inp["rw_lnx_b"][0]); sh["rw_wo"] = f(inp["rw_w_o"][0])
    sh["ffn_g"] = f(inp["ffn_w_gate"]); sh["ffn_u"] = f(inp["ffn_w_up"]); sh["ffn_d"] = f(inp["ffn_w_down"])
    sh["kv_norm_g"] = f(inp["kv_norm_g"]); sh["kv_w_ada"] = f(inp["kv_w_ada"]); sh["kv_b_ada"] = f(inp["kv_b_ada"])
    sh["kv_wk"] = f(inp["kv_w_k"]); sh["kv_wv"] = f(inp["kv_w_v"])
    sh["k_norm_g"] = f(np.tile(np.asarray(inp["k_norm_g"]), NH)); sh["q_norm_g"] = f(np.tile(np.asarray(inp["mb_q_norm_g"])[0], NH))
    sh["mb_wq"] = f(inp["mb_w_q"][0]); sh["mb_wo"] = f(inp["mb_w_o"][0])
    sh.update(_consts())
    return sh


def prep_core(x_b, c_b, hf, T):
    TH = T // 2
    NT, NB = T // 128, T // 256
    m = {}
    if hf == 1:
        xv = np.asarray(x_b, dtype=np.float32)
        valid = np.ones(T, np.float32)
    else:
        xv = np.concatenate([np.zeros((TH, D), np.float32), np.asarray(x_b[:TH], dtype=np.float32)], 0)
        valid = np.concatenate([np.zeros(TH, np.float32), np.ones(TH, np.float32)])
    m["xv"] = np.ascontiguousarray(xv)
    m["valid"] = np.ascontiguousarray(valid.reshape(NT, 128).T)
    bb = np.where(valid.reshape(NB, 256)[:, 0] > 0, 0.0, NEG).astype(np.float32)
    m["bbias"] = np.ascontiguousarray(np.tile(bb[None, :], (128, 1)))
    m["cvec"] = np.ascontiguousarray(np.asarray(c_b, dtype=np.float32).reshape(8, 128).T)
    fu = np.where(np.arange(NB)[None, :] >= np.arange(NB)[:, None], NEG, 0.0).astype(np.float32)
    m["c_fut"] = np.ascontiguousarray(np.tile(fu[None], (128, 1, 1)))
    return m


_NC_CACHE = {}


def kernel(**inputs):
    x = np.asarray(inputs["x"], dtype=np.float32)
    c = np.asarray(inputs["c"], dtype=np.float32)
    Bn, T, _ = x.shape
    TH = T // 2
    key = (T,)
    if key not in _NC_CACHE:
        _NC_CACHE[key] = build(T)
    nc = _NC_CACHE[key]
    sh = prep_shared(inputs)
    in_maps = []
    for b in range(Bn):
        for hf in range(2):
            m = dict(sh)
            m.update(prep_core(x[b], c[b], hf, T))
            in_maps.append(m)
    res = run_bass_kernel_spmd(nc, in_maps, core_ids=list(range(2 * Bn)))
    outp = np.empty((Bn, T, D), np.float32)
    for b in range(Bn):
        for hf in range(2):
            outp[b, hf * TH:(hf + 1) * TH] = res.results[b * 2 + hf]["out"]
    return outp
```
